# Optimizing a Trainium2 kernel written in Bass

```python
import jax, jax.numpy as jnp
from jax import lax
import numpy as np

D_MODEL = 1024
BATCH = 32
SEQ = 256
DEPTH = 2
DEC_BATCH = 4
DEC_SEQ = 4096
PAST_LEN = 512

GRID_W = 64
POOL_WIDTH = 256
POOL_GROUPS = 4
POOL_WINDOWS = (2, 4, 8, 16)
POOL_GROUP_DIM = POOL_WIDTH // POOL_GROUPS
CONV_WIDTH = 256
CONV_TAPS = 31
N_HEADS = 8
N_KV_HEADS = 2
HEAD_DIM = 64
Q_GROUP = N_HEADS // N_KV_HEADS
ATTN_WIDTH = N_HEADS * HEAD_DIM
KV_WIDTH = N_KV_HEADS * HEAD_DIM
MIX_WIDTH = POOL_WIDTH + CONV_WIDTH + ATTN_WIDTH
ATTN_OFFSET = POOL_WIDTH + 2 * CONV_WIDTH
IN_WIDTH = ATTN_OFFSET + ATTN_WIDTH + 2 * KV_WIDTH
WINDOW = 128
BLOCK = 128
D_FF = 2816
N_MOD = 9
ROPE_THETA = 10000.0
EPS = 1e-6
NEG_INF = -1e30

kernel_name = 'hymba_pool_conv_swa_macaron_dit_step'


def _rms(x, g):
    xf = x.astype(jnp.float32)
    y = xf * lax.rsqrt(jnp.mean(xf * xf, axis=-1, keepdims=True) + EPS)
    return (y * g.astype(jnp.float32)).astype(x.dtype)


def _swiglu(h, wi, wo):
    gate, up = jnp.split(h @ wi, 2, axis=-1)
    return (jax.nn.silu(gate) * up) @ wo


def _modulations(cond, w, b):
    m = jax.nn.silu(cond) @ w + b
    return jnp.split(m[:, None, :], N_MOD, axis=-1)


def _pool_mixer(u, pool_w, pool_scale):
    n = u.shape[1]
    uf = u.astype(jnp.float32)
    cs = jnp.pad(jnp.cumsum(uf, axis=1), ((0, 0), (1, 0), (0, 0)))
    t = jnp.arange(n)
    outs = []
    for j, w in enumerate(POOL_WINDOWS):
        lo = jnp.clip(t - w // 2, 0, n)
        hi = jnp.clip(t + w // 2, 0, n)
        sl = slice(j * POOL_GROUP_DIM, (j + 1) * POOL_GROUP_DIM)
        csj = cs[..., sl]
        mean = (csj[:, hi] - csj[:, lo]) / (hi - lo).astype(jnp.float32)[None, :, None]
        pooled = (mean - uf[..., sl]).astype(u.dtype)
        outs.append(pooled @ pool_w[j])
    return jnp.concatenate(outs, axis=-1) * pool_scale


def _conv_mixer(g, dw, db, norm_g, pw):
    glu = g[..., :CONV_WIDTH] * jax.nn.sigmoid(g[..., CONV_WIDTH:])
    y = lax.conv_general_dilated(
        glu, dw[:, None, :].astype(glu.dtype), window_strides=(1,),
        padding=[(CONV_TAPS // 2, CONV_TAPS // 2)],
        dimension_numbers=('NWC', 'WIO', 'NWC'), feature_group_count=CONV_WIDTH) + db
    return jax.nn.silu(_rms(y, norm_g)) @ pw


def _qkv(u, q_g, k_g):
    b, n, _ = u.shape
    q = u[..., :ATTN_WIDTH].reshape(b, n, N_HEADS, HEAD_DIM)
    k = u[..., ATTN_WIDTH:ATTN_WIDTH + KV_WIDTH].reshape(b, n, N_KV_HEADS, HEAD_DIM)
    v = u[..., ATTN_WIDTH + KV_WIDTH:].reshape(b, n, N_KV_HEADS, HEAD_DIM)
    return _rms(q, q_g), _rms(k, k_g), v


def _rope_2d(x, n):
    rows = n // GRID_W
    row = jnp.repeat(jnp.arange(rows), GRID_W).astype(jnp.float32)
    col = jnp.tile(jnp.arange(GRID_W), rows).astype(jnp.float32)
    half = HEAD_DIM // 2
    inv = ROPE_THETA ** (-jnp.arange(0, half, 2, dtype=jnp.float32) / half)

    def rot(xa, pos):
        ang = pos[:, None] * inv[None, :]
        cos = jnp.cos(ang)[None, :, None, :]
        sin = jnp.sin(ang)[None, :, None, :]
        x1, x2 = xa[..., :half // 2], xa[..., half // 2:]
        return jnp.concatenate([x1 * cos - x2 * sin, x2 * cos + x1 * sin], axis=-1)

    xf = x.astype(jnp.float32)
    return jnp.concatenate([rot(xf[..., :half], row), rot(xf[..., half:], col)], axis=-1).astype(x.dtype)


def _softmax_with_sink(s, sink):
    b, _, _, nq, _ = s.shape
    sk = jnp.broadcast_to(sink.astype(jnp.float32).reshape(N_KV_HEADS, Q_GROUP)[None, :, :, None, None],
                          (b, N_KV_HEADS, Q_GROUP, nq, 1))
    p = jax.nn.softmax(jnp.concatenate([s, sk], axis=-1), axis=-1)
    return p[..., :-1]


def _context_attention(q, k, v, sink):
    b, n = q.shape[:2]
    nb = n // BLOCK
    qb = jnp.moveaxis(q.reshape(b, nb, BLOCK, N_KV_HEADS, Q_GROUP, HEAD_DIM), 1, 0)
    scale = HEAD_DIM ** -0.5

    def one(qi):
        s = jnp.einsum('bqkgd,bskd->bkgqs', qi, k).astype(jnp.float32) * scale
        p = _softmax_with_sink(s, sink).astype(v.dtype)
        return jnp.einsum('bkgqs,bskd->bqkgd', p, v)

    o = lax.map(one, qb)
    return jnp.moveaxis(o, 0, 1).reshape(b, n, ATTN_WIDTH)


def _latent_attention(q, k, v, kc, vc, sink):
    b, n = q.shape[:2]
    nb = n // BLOCK
    span = BLOCK + 2 * WINDOW
    qb = jnp.moveaxis(q.reshape(b, nb, BLOCK, N_KV_HEADS, Q_GROUP, HEAD_DIM), 1, 0)
    kp = jnp.pad(k, ((0, 0), (WINDOW, WINDOW), (0, 0), (0, 0)))
    vp = jnp.pad(v, ((0, 0), (WINDOW, WINDOW), (0, 0), (0, 0)))
    scale = HEAD_DIM ** -0.5
    offs_q = jnp.arange(BLOCK)
    offs_k = jnp.arange(span)

    def one(args):
        i, qi = args
        start = i * BLOCK
        kw = lax.dynamic_slice_in_dim(kp, start, span, axis=1)
        vw = lax.dynamic_slice_in_dim(vp, start, span, axis=1)
        qpos = start + offs_q
        kpos = start - WINDOW + offs_k
        valid = ((jnp.abs(qpos[:, None] - kpos[None, :]) <= WINDOW)
                 & (kpos >= 0)[None, :] & (kpos < n)[None, :])
        s_w = jnp.einsum('bqkgd,bskd->bkgqs', qi, kw).astype(jnp.float32) * scale
        s_w = jnp.where(valid, s_w, NEG_INF)
        s_c = jnp.einsum('bqkgd,bskd->bkgqs', qi, kc).astype(jnp.float32) * scale
        p = _softmax_with_sink(jnp.concatenate([s_w, s_c], axis=-1), sink).astype(v.dtype)
        return (jnp.einsum('bkgqs,bskd->bqkgd', p[..., :span], vw)
                + jnp.einsum('bkgqs,bskd->bqkgd', p[..., span:], vc))

    o = lax.map(one, (jnp.arange(nb), qb))
    return jnp.moveaxis(o, 0, 1).reshape(b, n, ATTN_WIDTH)


def _layer(x, cond, p, cache_k=None, cache_v=None):
    sh1, sc1, g1, sh2, sc2, g2, sh3, sc3, g3 = _modulations(cond, p['mod_w'], p['mod_b'])
    h = _rms(x, p['norm_g'][0]) * (1.0 + sc1) + sh1
    x = x + 0.5 * g1 * _swiglu(h, p['ffn1_wi'], p['ffn1_wo'])
    h = _rms(x, p['norm_g'][1]) * (1.0 + sc2) + sh2
    u = h @ p['w_in']
    o_pool = _pool_mixer(u[..., :POOL_WIDTH], p['pool_w'], p['pool_scale'])
    o_conv = _conv_mixer(u[..., POOL_WIDTH:ATTN_OFFSET], p['conv_dw'], p['conv_b'],
                         p['conv_norm_g'], p['conv_pw'])
    q, k, v = _qkv(u[..., ATTN_OFFSET:], p['q_norm_g'], p['k_norm_g'])
    if cache_k is None:
        o_attn = _context_attention(q, k, v, p['sink'])
        new_kv = (k, v)
    else:
        n = x.shape[1]
        o_attn = _latent_attention(_rope_2d(q, n), _rope_2d(k, n), v, cache_k, cache_v, p['sink'])
        new_kv = None
    x = x + g2 * (jnp.concatenate([o_pool, o_conv, o_attn], axis=-1) @ p['w_out'])
    h = _rms(x, p['norm_g'][2]) * (1.0 + sc3) + sh3
    x = x + 0.5 * g3 * _swiglu(h, p['ffn2_wi'], p['ffn2_wo'])
    return x, new_kv


def setup_inputs(seed: int = 0) -> dict:
    key = jax.random.key(seed)
    ks = jax.random.split(key, 26)
    D = D_MODEL

    def nrm(k, shape, s):
        return jax.random.normal(k, shape, jnp.float32) * s

    cache_shape = (DEC_BATCH, DEPTH, PAST_LEN, N_KV_HEADS, HEAD_DIM)
    return {
        'x_prompt': nrm(ks[0], (BATCH, SEQ, D), 1.0),
        'x_sample': nrm(ks[1], (DEC_BATCH, DEC_SEQ, D), 1.0),
        'cache_k': nrm(ks[2], cache_shape, 1.0),
        'cache_v': nrm(ks[3], cache_shape, 1.0),
        'c': nrm(ks[4], (DEC_BATCH, D), 1.0),
        'c_ctx': nrm(ks[5], (D,), 1.0),
        'mod_w': nrm(ks[6], (DEPTH, D, N_MOD * D), 0.5 * D ** -0.5),
        'mod_b': nrm(ks[7], (DEPTH, N_MOD * D), 0.02),
        'norm_g': 1.0 + nrm(ks[8], (DEPTH, 3, D), 0.02),
        'ffn1_wi': nrm(ks[9], (DEPTH, D, 2 * D_FF), D ** -0.5),
        'ffn1_wo': nrm(ks[10], (DEPTH, D_FF, D), D_FF ** -0.5),
        'ffn2_wi': nrm(ks[11], (DEPTH, D, 2 * D_FF), D ** -0.5),
        'ffn2_wo': nrm(ks[12], (DEPTH, D_FF, D), D_FF ** -0.5),
        'w_in': nrm(ks[13], (DEPTH, D, IN_WIDTH), D ** -0.5),
        'w_out': nrm(ks[14], (DEPTH, MIX_WIDTH, D), MIX_WIDTH ** -0.5),
        'pool_w': nrm(ks[15], (DEPTH, POOL_GROUPS, POOL_GROUP_DIM, POOL_GROUP_DIM), POOL_GROUP_DIM ** -0.5),
        'pool_scale': 1.0 + nrm(ks[16], (DEPTH, POOL_WIDTH), 0.1),
        'conv_dw': nrm(ks[17], (DEPTH, CONV_TAPS, CONV_WIDTH), CONV_TAPS ** -0.5),
        'conv_b': nrm(ks[18], (DEPTH, CONV_WIDTH), 0.02),
        'conv_norm_g': 1.0 + nrm(ks[19], (DEPTH, CONV_WIDTH), 0.02),
        'conv_pw': nrm(ks[20], (DEPTH, CONV_WIDTH, CONV_WIDTH), CONV_WIDTH ** -0.5),
        'q_norm_g': 1.0 + nrm(ks[21], (DEPTH, HEAD_DIM), 0.02),
        'k_norm_g': 1.0 + nrm(ks[22], (DEPTH, HEAD_DIM), 0.02),
        'sink': nrm(ks[23], (DEPTH, N_HEADS), 0.5),
    }


def reference(x_prompt, x_sample, cache_k, cache_v, c, c_ctx, mod_w, mod_b, norm_g,
              ffn1_wi, ffn1_wo, ffn2_wi, ffn2_wo, w_in, w_out, pool_w, pool_scale,
              conv_dw, conv_b, conv_norm_g, conv_pw, q_norm_g, k_norm_g, sink):
    y_prompt = x_prompt
    y_sample = x_sample
    ks, vs = [], []
    for l in range(DEPTH):
        p = {
            'mod_w': mod_w[l], 'mod_b': mod_b[l], 'norm_g': norm_g[l],
            'ffn1_wi': ffn1_wi[l], 'ffn1_wo': ffn1_wo[l],
            'ffn2_wi': ffn2_wi[l], 'ffn2_wo': ffn2_wo[l],
            'w_in': w_in[l], 'w_out': w_out[l],
            'pool_w': pool_w[l], 'pool_scale': pool_scale[l],
            'conv_dw': conv_dw[l], 'conv_b': conv_b[l], 'conv_norm_g': conv_norm_g[l],
            'conv_pw': conv_pw[l], 'q_norm_g': q_norm_g[l], 'k_norm_g': k_norm_g[l],
            'sink': sink[l],
        }
        y_prompt, (k_l, v_l) = _layer(y_prompt, c_ctx[None, :], p)
        ks.append(k_l)
        vs.append(v_l)
        y_sample, _ = _layer(y_sample, c, p, cache_k[:, l], cache_v[:, l])
    new_cache_k = jnp.stack(ks, axis=1)
    new_cache_v = jnp.stack(vs, axis=1)
    return (y_prompt, y_sample, new_cache_k, new_cache_v)
```

```python
import numpy as np
from contextlib import ExitStack
import concourse.bass as bass
import concourse.mybir as mybir
from concourse.bass_utils import run_bass_kernel_spmd

F32 = mybir.dt.float32
BF16 = mybir.dt.bfloat16
AF = mybir.ActivationFunctionType
ALU = mybir.AluOpType

D = 1024
NCH = 8
DFF = 2816
NFC = 22
TP = 1024
TS = 2304
PAST = 512
PADW = 16
EPS = 1e-6
SLOT = 9216
FBLOCKS = [3, 3, 3, 3, 3, 3, 3, 1]
NTMP = 8


class Prog:
    def __init__(self):
        self.ops = []
        self.lastw = {}
        self.readers = {}
        self.ambient = True

    def add(self, eng, fn, reads=(), writes=(), dma=None):
        i = len(self.ops)
        def _exp(lst):
            o = []
            for t in lst:
                if isinstance(t, tuple) and len(t) == 2 and t[0] == "slotpair":
                    o += [("slot", t[1], "a"), ("slot", t[1], "b")]
                else:
                    o.append(t)
            return o
        reads = _exp(reads)
        writes = _exp(writes)
        if self.ambient and eng in ("pe", "act", "dve") and dma is None and "BIG" not in writes:
            reads.append("BIG")
        deps = set()
        for t in reads:
            w = self.lastw.get(t)
            if w is not None:
                deps.add(w)
        for t in writes:
            w = self.lastw.get(t)
            if w is not None:
                deps.add(w)
            deps.update(self.readers.get(t, ()))
        red = {}
        for d in deps:
            p = self.ops[d]
            k = ("dma", p["dma"]) if p["dma"] is not None else ("eng", p["eng"])
            if k not in red or d > red[k]:
                red[k] = d
        deps = set(red.values())
        self.ops.append(dict(eng=eng, fn=fn, deps=deps, dma=dma, val=None))
        for t in reads:
            self.readers.setdefault(t, []).append(i)
        for t in writes:
            self.lastw[t] = i
            self.readers[t] = []
        return i

    def emit(self, nc, es, final_keys):
        ops = self.ops
        needed = set()
        for op in ops:
            for d in op["deps"]:
                p = ops[d]
                if p["dma"] is None:
                    if p["eng"] == "pe" and op["eng"] == "pe" and op["dma"] is None:
                        continue
                    needed.add(d)
        cnt = {}
        dcnt = {}
        for i, op in enumerate(ops):
            if op["dma"] is not None:
                dcnt[op["dma"]] = dcnt.get(op["dma"], 0) + 16
                op["val"] = dcnt[op["dma"]]
            elif i in needed:
                cnt[op["eng"]] = cnt.get(op["eng"], 0) + 1
                op["val"] = cnt[op["eng"]]
        self.stats = dict(cnt=dict(cnt), dmax=max(dcnt.values()), nops=len(ops), ndma=len(dcnt))
        sems = {}
        for e in ["pe", "act", "dve", "pool", "sp"]:
            sems[("eng", e)] = es.enter_context(nc.semaphore("s_" + e))
        for k in dcnt:
            sems[("dma", k)] = es.enter_context(nc.semaphore("d_" + str(len(sems))))
        block = es.enter_context(nc.Block())
        per = {e: [] for e in ["pe", "act", "dve", "pool", "sp"]}
        for i, op in enumerate(ops):
            per[op["eng"]].append(i)

        def run(ename, eng):
            waited = {}
            for i in per[ename]:
                op = ops[i]
                need = {}
                for d in op["deps"]:
                    p = ops[d]
                    if p["dma"] is not None:
                        key = ("dma", p["dma"])
                    else:
                        if p["eng"] == "pe" and ename == "pe" and op["dma"] is None:
                            continue
                        key = ("eng", p["eng"])
                    v = p["val"]
                    if v > need.get(key, 0):
                        need[key] = v
                todo = []
                for key in sorted(need, key=str):
                    v = need[key]
                    if waited.get(key, 0) >= v:
                        continue
                    todo.append((key, v))
                    waited[key] = v
                for key, v in todo[:-1]:
                    eng.wait_ge(sems[key], v)
                ins = op["fn"](eng)
                if todo:
                    key, v = todo[-1]
                    ins.wait_op(sems[key], v, "sem-ge")
                if op["dma"] is not None:
                    ins.then_inc(sems[("dma", op["dma"])], 16)
                elif op["val"] is not None:
                    ins.then_inc(sems[("eng", ename)], 1)
            if ename == "sp":
                for k in final_keys:
                    if k in dcnt:
                        eng.wait_ge(sems[("dma", k)], dcnt[k])

        @block.tensor
        def _(e):
            run("pe", e)

        @block.scalar
        def _(e):
            run("act", e)

        @block.vector
        def _(e):
            run("dve", e)

        @block.gpsimd
        def _(e):
            run("pool", e)

        @block.sync
        def _(e):
            run("sp", e)


def build_program(debug_stop=None):
    nc = bass.Bass("TRN2", target_bir_lowering=False)
    P = Prog()
    es = ExitStack()

    def din(name, shape, dt=F32):
        return nc.dram_tensor(name, list(shape), dt, kind="ExternalInput").ap()

    def dout(name, shape):
        return nc.dram_tensor(name, list(shape), F32, kind="ExternalOutput").ap()

    xp_d = din("xp", [NCH, 128, TP])
    xs_d = din("xs", [NCH, 128, TS])
    cond_d = din("cond", [128, NCH, 2])
    modw_d = din("mod_w", [2, D, 9 * D])
    modb_d = din("mod_b", [2, 128, 72])
    ng_d = din("norm_g", [2, 128, 3, NCH])
    wi_d = [din("ffn1_wi", [2, D, 2 * DFF]), din("ffn2_wi", [2, D, 2 * DFF])]
    wo_d = [din("ffn1_wo", [2, DFF, D]), din("ffn2_wo", [2, DFF, D])]
    win_d = din("w_in", [2, D, 1536])
    wout_d = din("w_out", [2, D, D])
    poolw_d = din("pool_w", [2, 4, 64, 64])
    small_d = din("small", [2, 128, 16])
    dw_d = din("conv_dw", [2, 128, 2, 31])
    pw_d = din("conv_pw", [2, 256, 256])
    sink_d = din("sink_b", [2, 128, 8])
    ckT_d = din("cache_kT", [2, 128, PAST])
    cv_d = din("cache_v", [2, PAST, 128])
    cos_d = din("rope_cos", [128, TS])
    sin_d = din("rope_sin", [128, TS])
    cmat_d = din("cmat", [128, 5, 128])
    mask_d = din("masks", [128, 2, 128])
    edge_d = din("pool_edge", [128, 2, 2, 8])

    yp_d = dout("yp", [NCH, 128, TP])
    ys_d = dout("ys", [NCH, 128, TS])
    nk_d = dout("nk", [2, 128, TP])
    nv_d = dout("nv", [2, TP, 128])

    def sb(name, shape, dt):
        return es.enter_context(nc.sbuf_tensor(name, list(shape), dt))

    X = sb("X", [128, NCH, TS], F32)
    BIG = sb("BIG", [128, 29824], BF16)
    slots = [sb("slot0", [128, SLOT], BF16), sb("slot1", [128, SLOT], BF16)]
    tmps = [sb("tmp%d" % i, [128, 528], F32) for i in range(NTMP)]
    hgrp = sb("hgrp", [128, NCH, 512], BF16)
    cmat = sb("cmatb", [128, 5, 128], BF16)
    masks = sb("masksb", [128, 2, 128], BF16)
    esb = sb("esb", [128, 8], F32)
    sinkraw = sb("sinkraw", [128, 8], F32)
    condf = sb("condf", [128, NCH, 2], F32)
    condb = sb("condb", [128, NCH, 2], BF16)
    modv = [sb("modv%d" % l, [128, 72, 2], F32) for l in range(2)]
    modb = sb("modb", [128, 2, 72], F32)
    ngt = sb("ngt", [128, 2, 3, NCH], F32)
    AB = sb("ABt", [128, 2, 2, 3, 3, NCH], F32)
    smallt = sb("smallt", [128, 16], F32)
    dwt = sb("dwt", [128, 2, 31], F32)
    edget = sb("edget", [128, 2, 2, 8], F32)
    opc = sb("omix", [128, 4, 512], BF16)
    oattn = opc
    scr = sb("scr", [128, 2], F32)
    rv = sb("rv", [128, 1024], F32)
    ropec = rv[:, 0:512]
    ropes = rv[:, 512:1024]
    vout = rv[:, :].rearrange("p (t f) -> p t f", t=8)
    ps = es.enter_context(nc.psum_tensor("ps", [128, 8, 512], F32))

    Hv = BIG[:, 0:NCH * TS].rearrange("p (c t) -> p c t", c=NCH)
    actb = [BIG[:, NCH * TS + i * 1536:NCH * TS + (i + 1) * 1536].rearrange("p (c t) -> p c t", c=3) for i in range(2)]
    o = 0
    qst = BIG[:, o:o + 4 * TS].rearrange("p (c t) -> p c t", c=4); o += 4 * TS
    KW = TS + PAST
    kz = []
    for _ in range(2):
        kz.append(BIG[:, o:o + KW]); o += KW
    NVT = 22
    vaug = BIG[:, o:o + NVT * 256].rearrange("p (t g f) -> p t g f", t=NVT, g=2); o += NVT * 256
    TPAD = TS + 2 * PADW
    upad = BIG[:, o:o + 2 * TPAD].rearrange("p (c t) -> p c t", c=2); o += 2 * TPAD
    gpad = BIG[:, o:o + 2 * TPAD].rearrange("p (c t) -> p c t", c=2); o += 2 * TPAD
    assert o <= 29824, o

    tctr = [0]

    def tmp():
        i = tctr[0] % NTMP
        tctr[0] += 1
        return tmps[i], ("tmp", i)

    def tmpb(t):
        return t

    def mm(out, lhsT, rhs, start, stop, reads, writes):
        P.add("pe", lambda e: e.matmul(out, lhsT, rhs, start=start, stop=stop), reads, writes)

    def act(out, in_, func, reads, writes, scale=1.0, bias=0.0):
        P.add("act", lambda e: e.activation(out=out, in_=in_, func=func, scale=scale, bias=bias), reads, writes)

    def tt(out, in0, in1, op, reads, writes):
        P.add("dve", lambda e: e.tensor_tensor(out=out, in0=in0, in1=in1, op=op), reads, writes)

    def stt(out, in0, scalar, in1, op0, op1, reads, writes):
        P.add("dve", lambda e: e.scalar_tensor_tensor(out=out, in0=in0, scalar=scalar, in1=in1, op0=op0, op1=op1),
              reads, writes)

    def ts(out, in0, s1, op0, reads, writes, s2=None, op1=None):
        if op1 is None:
            P.add("dve", lambda e: e.tensor_scalar(out=out, in0=in0, scalar1=s1, scalar2=None, op0=op0), reads, writes)
        else:
            P.add("dve", lambda e: e.tensor_scalar(out=out, in0=in0, scalar1=s1, scalar2=s2, op0=op0, op1=op1),
                  reads, writes)

    def vcopy(out, in_, reads, writes):
        P.add("dve", lambda e: e.tensor_copy(out=out, in_=in_), reads, writes)

    def vmemset(ap, val, writes):
        P.add("dve", lambda e: e.memset(ap, val), (), writes)

    def dma(q, out, in_, key, reads, writes):
        P.add(q, lambda e: e.dma_start(out=out, in_=in_), reads, writes, dma=key)

    dma("pool", cmat[:], cmat_d[:, :, :], "c_cmat", (), ["cmat"])
    dma("pool", masks[:], mask_d[:, :, :], "c_mask", (), ["masks"])
    dma("sp", condf[:], cond_d[:, :, :], "c_cond", (), ["condf"])
    dma("sp", modb[:, 0, :], modb_d[0], "c_modb0", (), ["modb"])
    dma("sp", modb[:, 1, :], modb_d[1], "c_modb1", (), ["modb1"])
    dma("sp", ngt[:, 0], ng_d[0], "c_ng0", (), ["ngt"])
    dma("sp", ngt[:, 1], ng_d[1], "c_ng1", (), ["ngt1"])
    dma("sp", edget[:], edge_d[:, :, :, :], "c_edge", (), ["edget"])
    act(condb[:], condf[:], AF.Silu, ["condf"], ["condb"])
    ONES_MEAN = cmat[:, 0, :]
    BLK64 = cmat[:, 1, :]
    ONES256 = cmat[:, 2, :]
    IDENT = cmat[:, 3, :]
    PERM = cmat[:, 4, :]

    units = []

    def wi_src(l, which, f0, nf):
        v = wi_d[which][l].rearrange("(kc p) f -> p kc f", p=128)
        return v[:, :, f0 * 128:(f0 + nf) * 128], v[:, :, DFF + f0 * 128:DFF + (f0 + nf) * 128]

    for_units = []

    def plan_units():
        seq = []
        for l in range(1):
            pass
        return seq

    ucount = [0]
    ucursor = [0]
    ureleased = [0]

    def unit_views(kind, s, nf=3):
        sl = slots[s]
        if kind == "F":
            wi = sl[:, 0:8 * 2 * nf * 128].rearrange("p (k f) -> p k f", k=8)
            wo = sl[:, 6144:6144 + nf * 1024].rearrange("p (c f) -> p c f", c=nf)
            return wi, wo
        if kind == "M":
            return sl[:, 0:9216].rearrange("p (k f) -> p k f", k=8)
        if kind == "WIN":
            return sl[:, 0:6144].rearrange("p (k f) -> p k f", k=8)
        if kind == "WOUT":
            wout = sl[:, 0:8192].rearrange("p (k f) -> p k f", k=8)
            pw = sl[:, 8192:8704].rearrange("p (k f) -> p k f", k=2)
            pb = sl[:, 8704:8960].rearrange("p (k f) -> p k f", k=2)
            return wout, pw, pb
        if kind == "DIAG":
            return sl[:, 0:7936].rearrange("p (c j f) -> p c j f", c=2, j=31)
        raise ValueError(kind)

    def stoks(s):
        return [("slot", s, "a"), ("slot", s, "b")]

    def load_unit(u):
        idx, kind, l, arg = u
        s = idx % 2
        both = stoks(s)
        if kind == "F":
            which, f0, nf = arg
            wi, wo = unit_views("F", s, nf)
            g_src, u_src = wi_src(l, which, f0, nf)
            dma("pool", wi[:, :, 0:nf * 128], g_src, "u%d_a" % s, (), [("slot", s, "a")])
            dma("pool", wi[:, :, nf * 128:2 * nf * 128], u_src, "u%d_b" % s, (), [("slot", s, "a")])
            wsrc = wo_d[which][l][f0 * 128:(f0 + nf) * 128, :].rearrange("(c p) f -> p c f", p=128)
            dma("pool", wo, wsrc, "u%d_c" % s, (), [("slot", s, "b")])
        elif kind == "M":
            j = arg
            mv = unit_views("M", s)
            src = modw_d[l].rearrange("(kc p) f -> p kc f", p=128)[:, :, j * 1152:(j + 1) * 1152]
            dma("pool", mv, src, "u%d_a" % s, (), both)
        elif kind == "WIN":
            j = arg
            wv = unit_views("WIN", s)
            src = win_d[l].rearrange("(kc p) f -> p kc f", p=128)[:, :, j * 768:(j + 1) * 768]
            dma("pool", wv, src, "u%d_a" % s, (), both)
        elif kind == "WOUT":
            wout, pw, pb = unit_views("WOUT", s)
            dma("pool", wout, wout_d[l].rearrange("(kc p) f -> p kc f", p=128), "u%d_a" % s, (), both)
            dma("pool", pw, pw_d[l].rearrange("(kc p) f -> p kc f", p=128), "u%d_b" % s, (), both)
            P.add("pool", lambda e, pb=pb: e.memset(pb, 0.0), (), both)
            for gi in range(4):
                ch, half = gi // 2, gi % 2
                dma("pool", pb[64 * half:64 * half + 64, ch, 64 * half:64 * half + 64], poolw_d[l, gi],
                    "u%d_p%d" % (s, gi), (), both)
        elif kind == "DIAG":
            pass
        else:
            raise ValueError(kind)

    def _load_ready():
        while ucount[0] < len(units) and ucount[0] < ureleased[0] + 2:
            load_unit(units[ucount[0]])
            ucount[0] += 1

    def next_unit(expect_kind):
        u = units[ucursor[0]]
        assert u[1] == expect_kind, (u, expect_kind)
        _load_ready()
        assert ucount[0] > ucursor[0], ("unit not loadable yet", u, ureleased[0])
        ucursor[0] += 1
        return u[0] % 2, ("slotpair", u[0] % 2), u

    def release_unit():
        ureleased[0] += 1
        _load_ready()

    def add_units(kind_list):
        for (kind, l, arg) in kind_list:
            units.append((len(units), kind, l, arg))

    def layer_units(l):
        r = []
        f0 = 0
        for nf in FBLOCKS:
            r.append(("F", l, (0, f0, nf)))
            f0 += nf
        r += [("WIN", l, 0), ("WIN", l, 1), ("WOUT", l, None), ("DIAG", l, None)]
        f0 = 0
        for nf in FBLOCKS:
            r.append(("F", l, (1, f0, nf)))
            f0 += nf
        return r

    k0, k1 = (6, 6) if debug_stop is None else debug_stop
    plan = [("mod", 0)]
    need_mod1 = False
    if max(k0, k1) > 3:
        plan.append(("mod", 1))
    for phase, kk in ((0, k0), (1, k1)):
        if kk < 0:
            continue
        plan.append(("load", phase))
        for st in range(kk):
            l, sub = st // 3, st % 3
            if l == 1 and need_mod1:
                plan.append(("mod", 1))
                need_mod1 = False
            plan.append((("ffn1", "mixer", "ffn2")[sub], phase, l))
        plan.append(("store", phase))
    for it in plan:
        if it[0] == "mod":
            pass
        elif it[0] in ("ffn1", "ffn2"):
            f0 = 0
            for nf in FBLOCKS:
                add_units([("F", it[2], (0 if it[0] == "ffn1" else 1, f0, nf))])
                f0 += nf
        elif it[0] == "mixer":
            add_units([("WIN", it[2], 0), ("WIN", it[2], 1), ("DIAG", it[2], None), ("WOUT", it[2], None)])

    mod_q = []
    mod_loaded = [0]
    mod_done = [0]
    mod_list = []

    def mini_view(slot):
        return X[:, slot, 1024:2048].bitcast(BF16).rearrange("p (k f) -> p k f", k=8)

    def mini_toks(slot):
        return [("X", slot, 1024), ("X", slot, 1536)]

    def mod_enqueue(l):
        for m in range(36):
            mod_list.append((l, m))

    def _mod_load_ahead():
        while mod_loaded[0] < len(mod_list) and mod_loaded[0] < mod_done[0] + 8:
            l, m = mod_list[mod_loaded[0]]
            slot = mod_loaded[0] % 8
            src = modw_d[l].rearrange("(kc p) f -> p kc f", p=128)[:, :, m * 256:(m + 1) * 256]
            dma("pool", mini_view(slot), src, "mm%d" % slot, (), mini_toks(slot))
            mod_loaded[0] += 1

    def compute_AB(l, i):
        ng = "ngt" if l == 0 else "ngt1"
        for cj in range(2):
            sh = modv[l][:, (3 * i) * 8:(3 * i) * 8 + 8, cj]
            sc = modv[l][:, (3 * i + 1) * 8:(3 * i + 1) * 8 + 8, cj]
            gg = modv[l][:, (3 * i + 2) * 8:(3 * i + 2) * 8 + 8, cj]
            A = AB[:, l, cj, i, 0, :]
            B = AB[:, l, cj, i, 1, :]
            G = AB[:, l, cj, i, 2, :]
            stt(A, sc, 1.0, ngt[:, l, i, :], ALU.add, ALU.mult, [("modv", l), ng], [("AB", l)])
            vcopy(B, sh, [("modv", l)], [("AB", l)])
            ts(G, gg, 0.5 if i != 1 else 1.0, ALU.mult, [("modv", l)], [("AB", l)])

    def mod_pump(n):
        for _ in range(n):
            if mod_done[0] >= len(mod_list):
                return
            _mod_load_ahead()
            idx = mod_done[0]
            l, m = mod_list[idx]
            slot = idx % 8
            mv = mini_view(slot)
            mb = "modb" if l == 0 else "modb1"
            bank = 6 + (idx % 2)
            col0 = 0
            for cc in range(2):
                for k in range(NCH):
                    mm(ps[:, bank, col0 + 2 * cc:col0 + 2 * cc + 2], mv[:, k, cc * 128:(cc + 1) * 128], condb[:, k, :],
                       k == 0, k == NCH - 1, mini_toks(slot) + ["condb"], [("ps", bank)])
            for cc in range(2):
                cg = 2 * m + cc
                ts(modv[l][:, cg, :], ps[:, bank, col0 + 2 * cc:col0 + 2 * cc + 2], modb[:, l, cg:cg + 1], ALU.add,
                   [("ps", bank), mb], [("modv", l)])
            mod_done[0] += 1
            _mod_load_ahead()
            if m % 12 == 11:
                compute_AB(l, m // 12)

    def mod_ensure(l, i):
        while mod_done[0] < len(mod_list) and mod_list[mod_done[0]] <= (l, 12 * i + 11):
            mod_pump(1)

    def norm_mod(l, cj, i, t0, n, dst, dst_tok_fn):
        msb = 6
        for c in range(NCH):
            sq, sqt = tmp()
            sqv = sq.bitcast(BF16)[:, 0:n]
            act(sqv, X[:, c, t0:t0 + n], AF.Square, [("X", c, t0)], [sqt])
            mm(ps[:, msb, 0:n], ONES_MEAN, sqv, c == 0, c == NCH - 1, [sqt, "cmat"], [("ps", msb)])
        ln, lnt = tmp()
        act(ln[:, 0:n], ps[:, msb, 0:n], AF.Ln, [("ps", msb)], [lnt], bias=EPS)
        rs, rst = tmp()
        act(rs[:, 0:n], ln[:, 0:n], AF.Exp, [lnt], [rst], scale=-0.5)
        for c in range(NCH):
            t, ttok = tmp()
            stt(t[:, 0:n], X[:, c, t0:t0 + n], AB[:, l, cj, i, 0, c:c + 1], rs[:, 0:n], ALU.mult, ALU.mult,
                [("X", c, t0), ("AB", l), rst], [ttok])
            act(dst[:, c, 0:n] if dst is hgrp else dst[:, c, t0:t0 + n], t[:, 0:n], AF.Identity,
                [ttok, ("AB", l)], [dst_tok_fn(c)], bias=AB[:, l, cj, i, 1, c:c + 1])

    def ffn(l, which, cj, groups, hook=None):
        i = 0 if which == 0 else 2
        LOOK = 1
        for (t0, n) in groups[:LOOK]:
            norm_mod(l, cj, i, t0, n, Hv, lambda c, t0=t0: ("H", c, t0))
        items = []
        f0 = 0
        blk_info = []
        for bi, nf in enumerate(FBLOCKS):
            for gi, (t0, n) in enumerate(groups):
                items.append((bi, nf, f0, gi, t0, n))
            f0 += nf
        state = {"cur_blk": -1, "views": None, "tok": None}
        gu_ctr = [0]
        blk_views = {}

        def GU(it, ab):
            bi, nf, f0, gi, t0, n = it
            if bi not in blk_views:
                s, tok, u = next_unit("F")
                blk_views[bi] = (unit_views("F", s, nf), tok)
            (wi, wo), tok = blk_views[bi]
            for fc in range(nf):
                pr = gu_ctr[0] % 2
                gu_ctr[0] += 1
                bg, bu = 2 * pr, 2 * pr + 1
                for k in range(NCH):
                    mm(ps[:, bg, 0:n], wi[:, k, fc * 128:(fc + 1) * 128], Hv[:, k, t0:t0 + n], k == 0, k == NCH - 1,
                       [("slot", tok[1], "a"), ("H", k, t0)], [("ps", bg)])
                for k in range(NCH):
                    mm(ps[:, bu, 0:n], wi[:, k, (nf + fc) * 128:(nf + fc + 1) * 128], Hv[:, k, t0:t0 + n], k == 0,
                       k == NCH - 1, [("slot", tok[1], "a"), ("H", k, t0)], [("ps", bu)])
                sil, silt = tmp()
                act(sil[:, 0:n], ps[:, bg, 0:n], AF.Silu, [("ps", bg)], [silt])
                tt(actb[ab][:, fc, 0:n], ps[:, bu, 0:n], sil[:, 0:n], ALU.mult, [("ps", bu), silt], [("actb", ab, fc)])

        def WO(it, ab, last_of_block):
            bi, nf, f0, gi, t0, n = it
            (wi, wo), tok = blk_views[bi]
            for d in range(NCH):
                bo = 4 + (d % 2)
                for fc in range(nf):
                    mm(ps[:, bo, 0:n], wo[:, fc, d * 128:(d + 1) * 128], actb[ab][:, fc, 0:n], fc == 0, fc == nf - 1,
                       [("slot", tok[1], "b"), ("actb", ab, fc)], [("ps", bo)])
                stt(X[:, d, t0:t0 + n], ps[:, bo, 0:n], AB[:, l, cj, i, 2, d:d + 1], X[:, d, t0:t0 + n], ALU.mult, ALU.add,
                    [("ps", bo), ("AB", l), ("X", d, t0)], [("X", d, t0)])
            if last_of_block:
                release_unit()
                if hook is not None:
                    hook()

        ng = len(groups)
        for ii, it in enumerate(items):
            if it[0] == 0 and it[3] + LOOK < ng:
                (t0_, n_) = groups[it[3] + LOOK]
                norm_mod(l, cj, i, t0_, n_, Hv, lambda c, t0_=t0_: ("H", c, t0_))
            GU(it, ii % 2)
            if ii > 0:
                pit = items[ii - 1]
                WO(pit, (ii - 1) % 2, pit[3] == ng - 1)
        pit = items[-1]
        WO(pit, (len(items) - 1) % 2, True)

    def mixer(l, cj, groups, seqs, is_sample):
        T = sum(n for _, n in groups)
        ntiles = T // 128
        stok = "small%d" % 0

        def padcol(t):
            for si, (s0, sl) in enumerate(seqs):
                if s0 <= t < s0 + sl:
                    return t + PADW * (2 * si + 1)
            raise ValueError(t)

        dma("sp", smallt[:], small_d[l], "c_small", (), ["smallt"])
        dma("sp", dwt[:], dw_d[l], "c_dw", (), ["dwt"])
        dma("sp", sinkraw[:], sink_d[l], "c_sink", (), ["sinkraw"])
        act(esb[:], sinkraw[:], AF.Exp, ["sinkraw"], ["esb"])
        if is_sample:
            dma("pool", kz[0][0:64, TS:TS + PAST], ckT_d[l][0:64, :], "c_ck", (), [("kst", "cache")])
            dma("pool", kz[1][64:128, TS:TS + PAST], ckT_d[l][64:128, :], "c_ck1", (), [("kst", "cache")])
            cvv = cv_d[l].rearrange("(t p) f -> p t f", p=128)
            dma("pool", vaug[:, 18:22, 0, 0:64], cvv[:, :, 0:64], "c_cv0", (), [("vaug", "cache")])
            dma("pool", vaug[:, 18:22, 1, 64:128], cvv[:, :, 64:128], "c_cv1", (), [("vaug", "cache")])
        PSC = lambda c: smallt[:, c:c + 1]
        CB = lambda c: smallt[:, 2 + c:3 + c]
        CNG = lambda c: smallt[:, 4 + c:5 + c]
        QG = smallt[:, 6:7]
        KG = smallt[:, 7:8]

        sA, tokA, _ = next_unit("WIN")
        winA = unit_views("WIN", sA)
        sB, tokB, _ = next_unit("WIN")
        winB = unit_views("WIN", sB)

        def wcol(ch):
            if ch < 6:
                return winA[:, :, ch * 128:(ch + 1) * 128], tokA
            return winB[:, :, (ch - 6) * 128:(ch - 5) * 128], tokB

        HN = 264
        items = []
        for (g0, gn) in groups:
            for off in range(0, gn, 256):
                items.append((g0 + off, g0))
        all_tmp_toks = [("tmp", i) for i in range(NTMP)] + [("tmph", i, h) for i in range(NTMP) for h in range(2)]

        def tmp_barrier():
            P.add("dve", lambda e: e.memset(scr[:, :], 0.0), (), all_tmp_toks)

        tmp_barrier()
        hctr = {"n": 0, "c": 0}
        pools = {"n": [0, 1, 2], "c": [3, 4, 5, 6, 7]}

        def half(pool):
            lst = pools[pool]
            k = hctr[pool] % (2 * len(lst))
            hctr[pool] += 1
            ti, h = lst[k // 2], k % 2
            return tmps[ti][:, h * HN:h * HN + 256], tmps[ti].bitcast(BF16)[:, 2 * h * HN:2 * h * HN + 256], ("tmph", ti, h)

        def hb(i):
            return hgrp[:, :, (i % 2) * 256:(i % 2) * 256 + 256]

        def norm1(i):
            t0 = items[i][0]
            for c in range(NCH):
                _, sqv, sqt = half("n")
                act(sqv, X[:, c, t0:t0 + 256], AF.Square, [("X", c, items[i][1])], [sqt])
                mm(ps[:, 4, 0:256], ONES_MEAN, sqv, c == 0, c == NCH - 1, [sqt, "cmat"], [("ps", 4)])

        def norm2(i):
            t0 = items[i][0]
            lnv, _, lnt = half("c")
            act(lnv, ps[:, 4, 0:256], AF.Ln, [("ps", 4)], [lnt], bias=EPS)
            act(lnv, lnv, AF.Exp, [lnt], [lnt], scale=-0.5)
            for c in range(NCH):
                tv, _, ttok = half("n")
                stt(tv, X[:, c, t0:t0 + 256], AB[:, l, cj, 1, 0, c:c + 1], lnv, ALU.mult, ALU.mult,
                    [("X", c, items[i][1]), ("AB", l), lnt], [ttok])
                act(hb(i)[:, c, :], tv, AF.Identity, [ttok, ("AB", l)], [("hgrp", i % 2, c)], bias=AB[:, l, cj, 1, 1, c:c + 1])

        ubank = [0]

        def proj(i, ch):
            b = ubank[0] % 4
            ubank[0] += 1
            wv, wt = wcol(ch)
            for k in range(NCH):
                mm(ps[:, b, 0:256], wv[:, k, :], hb(i)[:, k, :], k == 0, k == NCH - 1, [wt, ("hgrp", i % 2, k)], [("ps", b)])
            return b

        def item_body(i, mid_hook):
            t0 = items[i][0]
            a = padcol(t0)
            if is_sample:
                rc_ = ropec[:, (i % 2) * 256:(i % 2) * 256 + 256]
                rs_ = ropes[:, (i % 2) * 256:(i % 2) * 256 + 256]
                vtk = [("vout", g_) for g_ in range(8)]
                dma("sp", rc_, cos_d[:, t0:t0 + 256], "c_rc%d" % (i % 2), (), [("ropec", i % 2)] + vtk)
                dma("sp", rs_, sin_d[:, t0:t0 + 256], "c_rs%d" % (i % 2), (), [("ropes", i % 2)] + vtk)
            for ch in (0, 1):
                b = proj(i, ch)
                act(upad[:, ch, a:a + 256], ps[:, b, 0:256], AF.Copy, [("ps", b)], [("upad", ch, items[i][1])])
            mid_hook()
            for cc in (0, 1):
                bgt = proj(i, 4 + cc)
                sgv, _, sgt = half("c")
                act(sgv, ps[:, bgt, 0:256], AF.Sigmoid, [("ps", bgt)], [sgt])
                ba = proj(i, 2 + cc)
                tt(gpad[:, cc, a:a + 256], ps[:, ba, 0:256], sgv, ALU.mult, [("ps", ba), sgt], [("gpad", cc, items[i][1])])
            st = {}

            def stA(ch):
                b = proj(i, ch)
                _, sqv, sqt = half("c")
                act(sqv, ps[:, b, 0:256], AF.Square, [("ps", b)], [sqt])
                st[ch] = dict(b=b, sqv=sqv, sqt=sqt)

            def stB(ch):
                d = st[ch]
                mm(ps[:, 5, 0:256], BLK64, d["sqv"], True, True, [d["sqt"], "cmat"], [("ps", 5)])
                lnv, _, lnt = half("c")
                act(lnv, ps[:, 5, 0:256], AF.Ln, [("ps", 5)], [lnt], bias=EPS)
                act(lnv, lnv, AF.Exp, [lnt], [lnt], scale=-0.5)
                gsc = KG if ch == 10 else QG
                qnv, _, qnt = half("c")
                stt(qnv, ps[:, d["b"], 0:256], gsc, lnv, ALU.mult, ALU.mult, [("ps", d["b"]), "smallt", lnt], [qnt])
                if ch == 10:
                    dstv, dtok = None, ("kst", items[i][1])
                else:
                    dstv, dtok = qst[:, ch - 6, t0:t0 + 256], ("qst", ch - 6, items[i][1])
                d.update(qnv=qnv, qnt=qnt, dstv=dstv, dtok=dtok)
                if not is_sample:
                    if ch == 10:
                        act(kz[0][0:64, t0:t0 + 256], qnv[0:64, :], AF.Copy, [qnt], [dtok])
                        act(kz[1][64:128, t0:t0 + 256], qnv[64:128, :], AF.Copy, [qnt], [dtok])
                        dma("sp", nk_d[l][:, t0:t0 + 256], qnv, "o_nk", [qnt], [])
                    else:
                        act(dstv, qnv, AF.Copy, [qnt], [dtok])
                else:
                    _, qbv, qbt = half("c")
                    act(qbv, qnv, AF.Copy, [qnt], [qbt])
                    d.update(qbv=qbv, qbt=qbt)

            def stC(ch):
                if not is_sample:
                    return
                d = st[ch]
                mm(ps[:, 6, 0:256], PERM, d["qbv"], True, True, [d["qbt"], "cmat"], [("ps", 6)])
                tt(d["qnv"], d["qnv"], rc_, ALU.mult, [d["qnt"], ("ropec", i % 2)], [d["qnt"]])
                t2v, _, t2t = half("c")
                tt(t2v, ps[:, 6, 0:256], rs_, ALU.mult, [("ps", 6), ("ropes", i % 2)], [t2t])
                if ch == 10:
                    tt(kz[0][0:64, t0:t0 + 256], d["qnv"][0:64, :], t2v[0:64, :], ALU.add, [d["qnt"], t2t], [d["dtok"]])
                    tt(kz[1][64:128, t0:t0 + 256], d["qnv"][64:128, :], t2v[64:128, :], ALU.add, [d["qnt"], t2t], [d["dtok"]])
                else:
                    tt(d["dstv"], d["qnv"], t2v, ALU.add, [d["qnt"], t2t], [d["dtok"]])

            chs = [6, 7, 8, 9, 10]
            for s_ in range(len(chs) + 2):
                if s_ < len(chs):
                    stA(chs[s_])
                if 0 <= s_ - 1 < len(chs):
                    stB(chs[s_ - 1])
                if 0 <= s_ - 2 < len(chs):
                    stC(chs[s_ - 2])
            wv, wt = wcol(11)
            for tl in range(2):
                for k in range(NCH):
                    mm(ps[:, 7, tl * 128:(tl + 1) * 128], hb(i)[:, k, tl * 128:(tl + 1) * 128], wv[:, k, :], k == 0,
                       k == NCH - 1, [wt, ("hgrp", i % 2, k)], [("ps", 7)])
            for tl in range(2):
                gt = (t0 // 128) + tl
                act(vaug[:, gt, 0, 0:64], ps[:, 7, tl * 128:tl * 128 + 64], AF.Copy, [("ps", 7)], [("vaug", gt)])
                act(vaug[:, gt, 1, 64:128], ps[:, 7, tl * 128 + 64:tl * 128 + 128], AF.Copy, [("ps", 7)], [("vaug", gt)])
                if not is_sample:
                    vcopy(vout[:, gt, :], ps[:, 7, tl * 128:(tl + 1) * 128], [("ps", 7)], [("vout", gt)])

        norm1(0)
        norm2(0)
        for i in range(len(items)):
            if i + 1 < len(items):
                norm1(i + 1)
                item_body(i, lambda i=i: norm2(i + 1))
            else:
                item_body(i, lambda: None)
        tmp_barrier()
        release_unit()
        release_unit()
        if not is_sample:
            dma("sp", nv_d[l].rearrange("(t p) f -> p t f", p=128), vout, "o_nv",
                [("vout", gt) for gt in range(8)], [])

        sD, tokD, _ = next_unit("DIAG")
        diag = unit_views("DIAG", sD)
        sW, tokW, _ = next_unit("WOUT")
        wout, pwv, pbv = unit_views("WOUT", sW)
        for cc in range(2):
            for j in range(31):
                act(diag[:, cc, j, :], IDENT, AF.Copy, ["cmat", "dwt"], [tokD], scale=dwt[:, cc, j:j + 1])

        def wout_part(kc0, src, src_tokfn, t0, n):
            for d in range(NCH):
                bo = 4 + (d % 2)
                for kk in range(4):
                    mm(ps[:, bo, 0:n], wout[:, kc0 + kk, d * 128:(d + 1) * 128], src[:, kk, 0:n], kk == 0, kk == 3,
                       [tokW, src_tokfn(kk)], [("ps", bo)])
                stt(X[:, d, t0:t0 + n], ps[:, bo, 0:n], AB[:, l, cj, 1, 2, d:d + 1], X[:, d, t0:t0 + n], ALU.mult, ALU.add,
                    [("ps", bo), ("AB", l), ("X", d, t0)], [("X", d, t0)])

        allsegs = []
        for (t0, n) in groups:
            t = t0
            while t < t0 + n:
                for (s0, sl) in seqs:
                    if s0 <= t < s0 + sl:
                        e = min(t0 + n, s0 + sl)
                        allsegs.append(dict(st=t, sn=e - t, at_start=(t == s0), at_end=(e == s0 + sl), t0=t0, n=n,
                                            last=(e == t0 + n)))
                        t = e
                        break
        gr_all = {cc: [("gpad", cc, g0) for (g0, _) in groups] for cc in (0, 1)}
        ur_all = {ch: [("upad", ch, g0) for (g0, _) in groups] for ch in (0, 1)}

        def conv_mms(si):
            sg_ = allsegs[si]
            a, sn = padcol(sg_["st"]), sg_["sn"]
            for cc in (0, 1):
                b = 2 * (si % 2) + cc
                for j in range(31):
                    mm(ps[:, b, 0:sn], diag[:, cc, j, :], gpad[:, cc, a + j - 15:a + j - 15 + sn], j == 0, j == 30,
                       [tokD] + gr_all[cc], [("ps", b)])

        def pool_dve(si):
            sg_ = allsegs[si]
            a, sn, at_start, at_end = padcol(sg_["st"]), sg_["sn"], sg_["at_start"], sg_["at_end"]
            outs = []
            for ch in (0, 1):
                ur = ur_all[ch]
                A_, At = tmp()
                tt(A_[:, 0:sn + 14], upad[:, ch, a - 8:a + sn + 6], upad[:, ch, a - 7:a + sn + 7], ALU.add, ur, [At])
                B_, Bt = tmp()
                tt(B_[:, 0:sn + 12], A_[:, 0:sn + 12], A_[:, 2:sn + 14], ALU.add, [At], [Bt])
                if ch == 0:
                    lo_src, lo_off, lo_w = A_, 7, 2
                    hi_src, hi_off, hi_w = B_, 6, 4
                    lot, hit = At, Bt
                else:
                    C_, Ct = tmp()
                    tt(C_[:, 0:sn + 8], B_[:, 0:sn + 8], B_[:, 4:sn + 12], ALU.add, [Bt], [Ct])
                    D_, Dt = tmp()
                    tt(D_[64:128, 0:sn], C_[64:128, 0:sn], C_[64:128, 8:sn + 8], ALU.add, [Ct], [Dt])
                    lo_src, lo_off, lo_w = C_, 4, 8
                    hi_src, hi_off, hi_w = D_, 0, 16
                    lot, hit = Ct, Dt
                mean, mt = tmp()
                ts(mean[0:64, 0:sn], lo_src[0:64, lo_off:lo_off + sn], 1.0 / lo_w, ALU.mult, [lot], [mt])
                ts(mean[64:128, 0:sn], hi_src[64:128, hi_off:hi_off + sn], 1.0 / hi_w, ALU.mult, [hit], [mt])
                if at_start:
                    tt(mean[0:64, 0:8], lo_src[0:64, lo_off:lo_off + 8], edget[0:64, ch, 0, :], ALU.mult,
                       [lot, "edget"], [mt])
                    tt(mean[64:128, 0:8], hi_src[64:128, hi_off:hi_off + 8], edget[64:128, ch, 0, :], ALU.mult,
                       [hit, "edget"], [mt])
                if at_end:
                    tt(mean[0:64, sn - 8:sn], lo_src[0:64, lo_off + sn - 8:lo_off + sn], edget[0:64, ch, 1, :],
                       ALU.mult, [lot, "edget"], [mt])
                    tt(mean[64:128, sn - 8:sn], hi_src[64:128, hi_off + sn - 8:hi_off + sn], edget[64:128, ch, 1, :],
                       ALU.mult, [hit, "edget"], [mt])
                plv = A_.bitcast(BF16)[:, 0:sn]
                tt(plv, mean[:, 0:sn], upad[:, ch, a:a + sn], ALU.subtract, [mt] + ur, [At])
                outs.append((plv, At, tctr[0]))
            return outs

        def pool_mm(si, outs):
            sg_ = allsegs[si]
            sn, off = sg_["sn"], sg_["st"] - sg_["t0"]
            for ch in (0, 1):
                plv, plt, ser = outs[ch]
                assert tctr[0] - ser < NTMP - 1, "tmp ring wrapped (pool)"
                b = 6 + ch
                mm(ps[:, b, 0:sn], pbv[:, ch, :], plv, True, True, [tokW, plt], [("ps", b)])
                act(opc[:, ch, off:off + sn], ps[:, b, 0:sn], AF.Copy, [("ps", b), "smallt"], [("omix", ch)],
                    scale=PSC(ch))

        def conv_post(si):
            sg_ = allsegs[si]
            sn, off = sg_["sn"], sg_["st"] - sg_["t0"]
            ybs = []
            for cc in (0, 1):
                b = 2 * (si % 2) + cc
                yb, ybt = tmp()
                act(yb[:, 0:sn], ps[:, b, 0:sn], AF.Identity, [("ps", b), "smallt"], [ybt], bias=CB(cc))
                sq, sqt = tmp()
                sqv = sq.bitcast(BF16)[:, 0:sn]
                act(sqv, yb[:, 0:sn], AF.Square, [ybt], [sqt])
                mm(ps[:, 6, 0:sn], ONES256, sqv, cc == 0, cc == 1, [sqt, "cmat"], [("ps", 6)])
                ybs.append((yb, ybt))
            ln, lnt = tmp()
            act(ln[:, 0:sn], ps[:, 6, 0:sn], AF.Ln, [("ps", 6)], [lnt], bias=EPS)
            act(ln[:, 0:sn], ln[:, 0:sn], AF.Exp, [lnt], [lnt], scale=-0.5)
            zs = []
            for cc in (0, 1):
                yb, ybt = ybs[cc]
                stt(yb[:, 0:sn], yb[:, 0:sn], CNG(cc), ln[:, 0:sn], ALU.mult, ALU.mult, [ybt, "smallt", lnt], [ybt])
                zb, zbt = tmp()
                zbv = zb.bitcast(BF16)[:, 0:sn]
                act(zbv, yb[:, 0:sn], AF.Silu, [ybt], [zbt])
                zs.append((zbv, zbt))
            for co in (0, 1):
                b = 6 + co
                for ci in (0, 1):
                    mm(ps[:, b, 0:sn], pwv[:, ci, co * 128:(co + 1) * 128], zs[ci][0], ci == 0, ci == 1,
                       [tokW, zs[ci][1]], [("ps", b)])
                act(opc[:, 2 + co, off:off + sn], ps[:, b, 0:sn], AF.Copy, [("ps", b)], [("omix", 2 + co)])

        conv_mms(0)
        for si in range(len(allsegs)):
            outs = pool_dve(si)
            if si + 1 < len(allsegs):
                conv_mms(si + 1)
            pool_mm(si, outs)
            conv_post(si)
            if allsegs[si]["last"]:
                wout_part(0, opc, lambda kk: ("omix", kk), allsegs[si]["t0"], allsegs[si]["n"])

        release_unit()
        sbank = [0]
        kall = [("kst", g0) for (g0, _) in groups] + ([("kst", "cache")] if is_sample else [])
        for (t0, n) in groups:
            qall = [("qst", c, t0) for c in range(4)]
            steps = []
            for tl in range(n // 128):
                gt = t0 // 128 + tl
                q0 = t0 + tl * 128
                chunks = []
                if is_sample:
                    if gt > 0:
                        chunks.append((q0 - 128, gt - 1, 0))
                    chunks.append((q0, gt, None))
                    if gt < ntiles - 1:
                        chunks.append((q0 + 128, gt + 1, 1))
                    for cti in range(4):
                        chunks.append((TS + cti * 128, 18 + cti, None))
                else:
                    for (s0, sl) in seqs:
                        if s0 <= q0 < s0 + sl:
                            for kt in range(sl // 128):
                                chunks.append((s0 + kt * 128, (s0 // 128) + kt, None))
                for kvh in range(2):
                    for ci, ch in enumerate(chunks):
                        steps.append((tl, gt, q0, ci, len(chunks), ch, kvh))

            def qk(step):
                tl, gt, q0, ci, nci, (kc0, vt, mk), kvh = step
                b = sbank[0] % 4
                sbank[0] += 1
                for hh in range(4):
                    mm(ps[:, b, hh * 128:(hh + 1) * 128], kz[kvh][:, kc0:kc0 + 128],
                       qst[:, hh, q0:q0 + 128], True, True, kall + qall, [("ps", b)])
                pt, ptt = tmp()
                ptv = pt.bitcast(BF16)[:, 0:512]
                act(ptv, ps[:, b, :], AF.Exp, [("ps", b)], [ptt], scale=0.125)
                if mk is not None:
                    pv4 = ptv.rearrange("p (h q) -> p h q", h=4)
                    tt(pv4, pv4, masks[:, mk, :].unsqueeze(1).broadcast_to([128, 4, 128]), ALU.mult, [ptt, "masks"], [ptt])
                return (ptv, ptt, tctr[0])

            def pv(step, pts):
                tl, gt, q0, ci, nci, (kc0, vt, mk), kvh = step
                pb = 4 + 2 * (gt % 2)
                assert tctr[0] - pts[2] < NTMP, "tmp ring wrapped"
                vtok = ("vaug", vt) if vt < 18 else ("vaug", "cache")
                mm(ps[:, pb + kvh, :], vaug[:, vt, kvh, :], pts[0], ci == 0, ci == nci - 1,
                   [vtok, pts[1]], [("ps", pb + kvh)])
                if ci == nci - 1:
                    dlo, nlo = (64, 0) if kvh == 0 else (0, 64)
                    rc2, rc2t = tmp()
                    if kvh == 0:
                        ln, lnt = tmp()
                        for hh in range(4):
                            act(ln[dlo:dlo + 64, hh * 128:(hh + 1) * 128], ps[dlo:dlo + 64, pb + kvh, hh * 128:(hh + 1) * 128],
                                AF.Ln, [("ps", pb + kvh), "esb"], [lnt], bias=esb[dlo:dlo + 64, 4 * kvh + hh:4 * kvh + hh + 1])
                        act(ln[dlo:dlo + 64, 0:512], ln[dlo:dlo + 64, 0:512], AF.Exp, [lnt], [lnt], scale=-1.0)
                        vcopy(rc2[nlo:nlo + 64, 0:512], ln[dlo:dlo + 64, 0:512], [lnt], [rc2t])
                    else:
                        for hh in range(4):
                            ts(rc2[nlo:nlo + 64, hh * 128:(hh + 1) * 128], ps[dlo:dlo + 64, pb + kvh, hh * 128:(hh + 1) * 128],
                               esb[dlo:dlo + 64, 4 * kvh + hh:4 * kvh + hh + 1], ALU.add, [("ps", pb + kvh), "esb"], [rc2t])
                        P.add("dve", lambda e, o_=rc2[nlo:nlo + 64, 0:512]: e.reciprocal(out=o_, in_=o_), [rc2t], [rc2t])
                    tt(oattn[nlo:nlo + 64, :, tl * 128:(tl + 1) * 128],
                       ps[nlo:nlo + 64, pb + kvh, :].rearrange("p (h q) -> p h q", h=4),
                       rc2[nlo:nlo + 64, 0:512].rearrange("p (h q) -> p h q", h=4), ALU.mult,
                       [("ps", pb + kvh), rc2t], [("omix", kk) for kk in range(4)])

            DEPTH = 3
            pend = []
            for step in steps:
                pend.append((step, qk(step)))
                if len(pend) > DEPTH:
                    pv(*pend.pop(0))
            while pend:
                pv(*pend.pop(0))
            for d in range(NCH):
                bo = d % 2
                for kk in range(4):
                    mm(ps[:, bo, 0:n], wout[:, 4 + kk, d * 128:(d + 1) * 128], oattn[:, kk, 0:n], kk == 0, kk == 3,
                       [tokW, ("omix", kk)], [("ps", bo)])
                stt(X[:, d, t0:t0 + n], ps[:, bo, 0:n], AB[:, l, cj, 1, 2, d:d + 1], X[:, d, t0:t0 + n], ALU.mult, ALU.add,
                    [("ps", bo), ("AB", l), ("X", d, t0)], [("X", d, t0)])
        release_unit()

    def mkgroups(T):
        g = []
        t = 0
        while t < T:
            n = min(512, T - t)
            g.append((t, n))
            t += n
        return g

    def big_switch():
        P.add("dve", lambda e: e.memset(scr[:, :], 0.0), (), ["BIG"])

    def init_big(is_sample, T):
        big_switch()
        seqs_ = [(0, TS)] if is_sample else [(s_ * 256, 256) for s_ in range(4)]
        ut = [("upad", ch, g0) for ch in (0, 1) for (g0, _) in mkgroups(T)]
        gt_ = [("gpad", ch, g0) for ch in (0, 1) for (g0, _) in mkgroups(T)]
        for si, (s0, sl) in enumerate(seqs_):
            a = s0 + PADW * (2 * si + 1)
            for (c0, c1) in ((a - PADW, a), (a + sl, a + sl + PADW)):
                vmemset(upad[:, :, c0:c1], 0.0, ut)
                vmemset(gpad[:, :, c0:c1], 0.0, gt_)
        vt = [("vaug", t) for t in range(18)] + [("vaug", "cache")]
        vmemset(vaug[:, :, 0, 64:128], 1.0, vt)
        vmemset(vaug[:, :, 1, 0:64], 1.0, vt)
        ktoks = [("kst", g0) for (g0, _) in mkgroups(T)] + [("kst", "cache")]
        vmemset(kz[0][64:128, :], 0.0, ktoks)
        vmemset(kz[1][0:64, :], 0.0, ktoks)

    for it in plan:
        if it[0] == "mod":
            mod_enqueue(it[1])
            if it[1] == 0:
                mod_pump(12)
            continue
        phase = it[1]
        is_sample = phase == 1
        T = TS if is_sample else TP
        xin = xs_d if is_sample else xp_d
        yout = ys_d if is_sample else yp_d
        groups = mkgroups(T)
        seqs = [(0, TS)] if is_sample else [(s * 256, 256) for s in range(4)]
        if it[0] == "load":
            if is_sample:
                mod_ensure(1, 2)
            for c in range(NCH):
                dma("sp", X[:, c, 0:T], xin[c], "x_in%d" % c, (), [("X", c, g0) for (g0, _) in groups])
        elif it[0] == "store":
            for c in range(NCH):
                dma("sp", yout[c], X[:, c, 0:T], "y_out%d" % c, [("X", c, g0) for (g0, _) in groups], [])
        elif it[0] == "ffn1":
            mod_ensure(it[2], 0)
            ffn(it[2], 0, phase, groups, hook=lambda: mod_pump(5))
        elif it[0] == "ffn2":
            mod_ensure(it[2], 2)
            ffn(it[2], 1, phase, groups, hook=lambda: mod_pump(5))
        elif it[0] == "mixer":
            mod_ensure(it[2], 1)
            init_big(is_sample, T)
            mixer(it[2], phase, groups, seqs, is_sample)
            big_switch()

    final_keys = ["y_out%d" % c for c in range(NCH)] + ["o_nk", "o_nv"]
    P.emit(nc, es, final_keys)
    es.close()
    nc._prog_stats = P.stats
    return nc


_NC_CACHE = {}


def _head_perm():
    cols = []
    for j in range(4):
        cols += list(range(j * 64, j * 64 + 64))
        cols += list(range((4 + j) * 64, (4 + j) * 64 + 64))
    return np.array(cols)


def _consts():
    cm = np.zeros((128, 5, 128), np.float32)
    cm[:, 0, :] = 1.0 / 1024
    for b in range(2):
        cm[64 * b:64 * b + 64, 1, 64 * b:64 * b + 64] = 1.0 / 64
    cm[:, 2, :] = 1.0 / 256
    cm[:, 3, :] = np.eye(128, dtype=np.float32)
    for m in range(128):
        d = m % 32
        partner = m + 16 if d < 16 else m - 16
        cm[partner, 4, m] = 1.0
    k = np.arange(128)[:, None]
    q = np.arange(128)[None, :]
    mprev = (k >= q).astype(np.float32)
    mnext = (k <= q).astype(np.float32)
    masks = np.stack([mprev, mnext], axis=1)
    sinkl = np.zeros((1, 2, 128), np.float32)
    sinkl[0, 0, 64:128] = 1.0
    sinkl[0, 1, 0:64] = 1.0
    edge = np.zeros((128, 2, 2, 8), np.float32)
    wins = {(0, 0): 2, (0, 1): 4, (1, 0): 8, (1, 1): 16}
    for (ch, half), w in wins.items():
        for i in range(8):
            cs = (i + w // 2) - max(i - w // 2, 0)
            r = 8 - i
            ce = min(w // 2, r) + w // 2
            edge[64 * half:64 * half + 64, ch, 0, i] = 1.0 / cs
            edge[64 * half:64 * half + 64, ch, 1, i] = 1.0 / ce
    return cm, masks, sinkl, edge


def _rope_tables(pos0):
    pos = pos0 + np.arange(TS)
    row = (pos // 64).astype(np.float32)
    col = (pos % 64).astype(np.float32)
    half = 32
    inv = (10000.0 ** (-np.arange(0, half, 2, dtype=np.float32) / half)).astype(np.float32)
    cos = np.zeros((128, TS), np.float32)
    sin = np.zeros((128, TS), np.float32)
    for p in range(128):
        d = p % 64
        posv = row if d < 32 else col
        dd = d % 32
        i = dd % 16
        ang = posv * inv[i]
        cos[p] = np.cos(ang)
        sin[p] = np.sin(ang) * (-1.0 if dd < 16 else 1.0)
    return cos, sin


def kernel(x_prompt, x_sample, cache_k, cache_v, c, c_ctx, mod_w, mod_b, norm_g,
           ffn1_wi, ffn1_wo, ffn2_wi, ffn2_wo, w_in, w_out, pool_w, pool_scale,
           conv_dw, conv_b, conv_norm_g, conv_pw, q_norm_g, k_norm_g, sink, _only_core=None, _debug_stop=None):
    f = lambda a: np.ascontiguousarray(np.asarray(a, dtype=np.float32))
    x_prompt, x_sample, cache_k, cache_v = f(x_prompt), f(x_sample), f(cache_k), f(cache_v)
    if "nc" not in _NC_CACHE:
        _NC_CACHE["nc"] = build_program()
    nc = _NC_CACHE["nc"]
    hp = _head_perm()
    w_in_p = f(w_in).copy()
    w_in_p[:, :, 768:1280] = f(w_in)[:, :, 768 + hp]
    w_out_p = f(w_out).copy()
    w_out_p[:, 512:1024, :] = f(w_out)[:, 512 + hp, :]
    modb_l = f(np.asarray(mod_b).reshape(2, 72, 128).transpose(0, 2, 1))
    ng_l = f(np.asarray(norm_g).reshape(2, 3, NCH, 128).transpose(0, 3, 1, 2))
    small = np.zeros((2, 128, 16), np.float32)
    small[:, :, 0:2] = np.asarray(pool_scale).reshape(2, 2, 128).transpose(0, 2, 1)
    small[:, :, 2:4] = np.asarray(conv_b).reshape(2, 2, 128).transpose(0, 2, 1)
    small[:, :, 4:6] = np.asarray(conv_norm_g).reshape(2, 2, 128).transpose(0, 2, 1)
    small[:, :, 6] = np.tile(np.asarray(q_norm_g), (1, 2))
    small[:, :, 7] = np.tile(np.asarray(k_norm_g), (1, 2))
    dw_l = f(np.asarray(conv_dw).reshape(2, 31, 2, 128).transpose(0, 3, 2, 1))
    sk = np.asarray(sink, dtype=np.float32)
    sink_b = f(np.broadcast_to(sk[:, None, :], (2, 128, 8)))
    cm, masks, sinkl, edge = _consts()
    shared = dict(mod_w=f(mod_w), mod_b=modb_l, norm_g=ng_l, ffn1_wi=f(ffn1_wi), ffn2_wi=f(ffn2_wi),
                  ffn1_wo=f(ffn1_wo), ffn2_wo=f(ffn2_wo), w_in=w_in_p, w_out=w_out_p, pool_w=f(pool_w),
                  small=small, conv_dw=dw_l, conv_pw=f(conv_pw), sink_b=sink_b, cmat=cm, masks=masks,
                  pool_edge=edge)
    in_maps = []
    starts = []
    for core in (range(8) if _only_core is None else [_only_core]):
        b, hf = core // 2, core % 2
        s0 = 0 if hf == 0 else 4096 - TS
        starts.append(s0)
        xp = x_prompt[4 * core:4 * core + 4].reshape(TP, D).T.reshape(NCH, 128, TP)
        xs = x_sample[b, s0:s0 + TS].T.reshape(NCH, 128, TS)
        cond = np.stack([np.asarray(c_ctx, np.float32), np.asarray(c, np.float32)[b]], axis=1)
        cond = cond.reshape(NCH, 128, 2).transpose(1, 0, 2)
        ckT = cache_k[b].reshape(2, PAST, 128).transpose(0, 2, 1)
        cv = cache_v[b].reshape(2, PAST, 128)
        cos, sin = _rope_tables(s0)
        m = dict(shared)
        m.update(xp=f(xp), xs=f(xs), cond=f(cond), cache_kT=f(ckT), cache_v=f(cv), rope_cos=cos, rope_sin=sin)
        in_maps.append(m)
    if _only_core is not None:
        nc = build_program(_debug_stop)
        res = run_bass_kernel_spmd(nc, in_maps, core_ids=[0])
        print("EXEC_NS", res.exec_time_ns, nc._prog_stats)
        return res.results[0]
    res = run_bass_kernel_spmd(nc, in_maps, core_ids=list(range(8)))
    y_prompt = np.zeros((32, 256, D), np.float32)
    y_sample = np.zeros((4, 4096, D), np.float32)
    nk = np.zeros((32, 2, 256, 2, 64), np.float32)
    nv = np.zeros((32, 2, 256, 2, 64), np.float32)
    for core in range(8):
        r = res.results[core]
        b, hf = core // 2, core % 2
        yp = np.asarray(r["yp"]).reshape(D, TP).T.reshape(4, 256, D)
        y_prompt[4 * core:4 * core + 4] = yp
        ys = np.asarray(r["ys"]).reshape(D, TS).T
        if hf == 0:
            y_sample[b, 0:2048] = ys[0:2048]
        else:
            y_sample[b, 2048:4096] = ys[TS - 2048:TS]
        k_ = np.asarray(r["nk"]).transpose(0, 2, 1).reshape(2, 4, 256, 2, 64)
        v_ = np.asarray(r["nv"]).reshape(2, 4, 256, 2, 64)
        nk[4 * core:4 * core + 4] = k_.transpose(1, 0, 2, 3, 4)
        nv[4 * core:4 * core + 4] = v_.transpose(1, 0, 2, 3, 4)
    return (y_prompt, y_sample, nk, nv)
```

```python
import numpy as np
from contextlib import ExitStack
import concourse.bass as bass
import concourse.mybir as mybir
from concourse.bass_utils import run_bass_kernel_spmd

F32 = mybir.dt.float32
BF16 = mybir.dt.bfloat16
AF = mybir.ActivationFunctionType
ALU = mybir.AluOpType

D = 1024
NCH = 8
DFF = 2816
NFC = 22
TP = 1024
TS = 2304
PAST = 512
PADW = 16
EPS = 1e-6
SLOT = 9216
FBLOCKS = [3, 3, 3, 3, 3, 3, 3, 1]
NTMP = 8


class Prog:
    def __init__(self):
        self.ops = []
        self.lastw = {}
        self.readers = {}
        self.ambient = True

    def add(self, eng, fn, reads=(), writes=(), dma=None):
        i = len(self.ops)
        def _exp(lst):
            o = []
            for t in lst:
                if isinstance(t, tuple) and len(t) == 2 and t[0] == "slotpair":
                    o += [("slot", t[1], "a"), ("slot", t[1], "b")]
                else:
                    o.append(t)
            return o
        reads = _exp(reads)
        writes = _exp(writes)
        if self.ambient and eng in ("pe", "act", "dve") and dma is None and "BIG" not in writes:
            reads.append("BIG")
        deps = set()
        for t in reads:
            w = self.lastw.get(t)
            if w is not None:
                deps.add(w)
        for t in writes:
            w = self.lastw.get(t)
            if w is not None:
                deps.add(w)
            deps.update(self.readers.get(t, ()))
        red = {}
        for d in deps:
            p = self.ops[d]
            k = ("dma", p["dma"]) if p["dma"] is not None else ("eng", p["eng"])
            if k not in red or d > red[k]:
                red[k] = d
        deps = set(red.values())
        self.ops.append(dict(eng=eng, fn=fn, deps=deps, dma=dma, val=None))
        for t in reads:
            self.readers.setdefault(t, []).append(i)
        for t in writes:
            self.lastw[t] = i
            self.readers[t] = []
        return i

    def emit(self, nc, es, final_keys):
        ops = self.ops
        needed = set()
        for op in ops:
            for d in op["deps"]:
                p = ops[d]
                if p["dma"] is None:
                    if p["eng"] == "pe" and op["eng"] == "pe" and op["dma"] is None:
                        continue
                    needed.add(d)
        cnt = {}
        dcnt = {}
        for i, op in enumerate(ops):
            if op["dma"] is not None:
                dcnt[op["dma"]] = dcnt.get(op["dma"], 0) + 16
                op["val"] = dcnt[op["dma"]]
            elif i in needed:
                cnt[op["eng"]] = cnt.get(op["eng"], 0) + 1
                op["val"] = cnt[op["eng"]]
        self.stats = dict(cnt=dict(cnt), dmax=max(dcnt.values()), nops=len(ops), ndma=len(dcnt))
        sems = {}
        for e in ["pe", "act", "dve", "pool", "sp"]:
            sems[("eng", e)] = es.enter_context(nc.semaphore("s_" + e))
        for k in dcnt:
            sems[("dma", k)] = es.enter_context(nc.semaphore("d_" + str(len(sems))))
        block = es.enter_context(nc.Block())
        per = {e: [] for e in ["pe", "act", "dve", "pool", "sp"]}
        for i, op in enumerate(ops):
            per[op["eng"]].append(i)

        def run(ename, eng):
            waited = {}
            for i in per[ename]:
                op = ops[i]
                need = {}
                for d in op["deps"]:
                    p = ops[d]
                    if p["dma"] is not None:
                        key = ("dma", p["dma"])
                    else:
                        if p["eng"] == "pe" and ename == "pe" and op["dma"] is None:
                            continue
                        key = ("eng", p["eng"])
                    v = p["val"]
                    if v > need.get(key, 0):
                        need[key] = v
                todo = []
                for key in sorted(need, key=str):
                    v = need[key]
                    if waited.get(key, 0) >= v:
                        continue
                    todo.append((key, v))
                    waited[key] = v
                for key, v in todo[:-1]:
                    eng.wait_ge(sems[key], v)
                ins = op["fn"](eng)
                if todo:
                    key, v = todo[-1]
                    ins.wait_op(sems[key], v, "sem-ge")
                if op["dma"] is not None:
                    ins.then_inc(sems[("dma", op["dma"])], 16)
                elif op["val"] is not None:
                    ins.then_inc(sems[("eng", ename)], 1)
            if ename == "sp":
                for k in final_keys:
                    if k in dcnt:
                        eng.wait_ge(sems[("dma", k)], dcnt[k])

        @block.tensor
        def _(e):
            run("pe", e)

        @block.scalar
        def _(e):
            run("act", e)

        @block.vector
        def _(e):
            run("dve", e)

        @block.gpsimd
        def _(e):
            run("pool", e)

        @block.sync
        def _(e):
            run("sp", e)


def build_program(debug_stop=None):
    nc = bass.Bass("TRN2", target_bir_lowering=False)
    P = Prog()
    es = ExitStack()

    def din(name, shape, dt=F32):
        return nc.dram_tensor(name, list(shape), dt, kind="ExternalInput").ap()

    def dout(name, shape):
        return nc.dram_tensor(name, list(shape), F32, kind="ExternalOutput").ap()

    xp_d = din("xp", [NCH, 128, TP])
    xs_d = din("xs", [NCH, 128, TS])
    cond_d = din("cond", [128, NCH, 2])
    modw_d = din("mod_w", [2, D, 9 * D])
    modb_d = din("mod_b", [2, 128, 72])
    ng_d = din("norm_g", [2, 128, 3, NCH])
    wi_d = [din("ffn1_wi", [2, D, 2 * DFF]), din("ffn2_wi", [2, D, 2 * DFF])]
    wo_d = [din("ffn1_wo", [2, DFF, D]), din("ffn2_wo", [2, DFF, D])]
    win_d = din("w_in", [2, D, 1536])
    wout_d = din("w_out", [2, D, D])
    poolw_d = din("pool_w", [2, 4, 64, 64])
    small_d = din("small", [2, 128, 16])
    dw_d = din("conv_dw", [2, 128, 2, 31])
    pw_d = din("conv_pw", [2, 256, 256])
    sink_d = din("sink_b", [2, 128, 8])
    ckT_d = din("cache_kT", [2, 128, PAST])
    cv_d = din("cache_v", [2, PAST, 128])
    cos_d = din("rope_cos", [128, TS])
    sin_d = din("rope_sin", [128, TS])
    cmat_d = din("cmat", [128, 5, 128])
    mask_d = din("masks", [128, 2, 128])
    edge_d = din("pool_edge", [128, 2, 2, 8])

    yp_d = dout("yp", [NCH, 128, TP])
    ys_d = dout("ys", [NCH, 128, TS])
    nk_d = dout("nk", [2, 128, TP])
    nv_d = dout("nv", [2, TP, 128])

    def sb(name, shape, dt):
        return es.enter_context(nc.sbuf_tensor(name, list(shape), dt))

    X = sb("X", [128, NCH, TS], F32)
    BIG = sb("BIG", [128, 29824], BF16)
    slots = [sb("slot0", [128, SLOT], BF16), sb("slot1", [128, SLOT], BF16)]
    tmps = [sb("tmp%d" % i, [128, 528], F32) for i in range(NTMP)]
    hgrp = sb("hgrp", [128, NCH, 512], BF16)
    cmat = sb("cmatb", [128, 5, 128], BF16)
    masks = sb("masksb", [128, 2, 128], BF16)
    esb = sb("esb", [128, 8], F32)
    sinkraw = sb("sinkraw", [128, 8], F32)
    condf = sb("condf", [128, NCH, 2], F32)
    condb = sb("condb", [128, NCH, 2], BF16)
    modv = [sb("modv%d" % l, [128, 72, 2], F32) for l in range(2)]
    modb = sb("modb", [128, 2, 72], F32)
    ngt = sb("ngt", [128, 2, 3, NCH], F32)
    AB = sb("ABt", [128, 2, 2, 3, 3, NCH], F32)
    smallt = sb("smallt", [128, 16], F32)
    dwt = sb("dwt", [128, 2, 31], F32)
    edget = sb("edget", [128, 2, 2, 8], F32)
    opc = sb("omix", [128, 4, 512], BF16)
    oattn = opc
    scr = sb("scr", [128, 2], F32)
    rv = sb("rv", [128, 1024], F32)
    ropec = rv[:, 0:512]
    ropes = rv[:, 512:1024]
    vout = rv[:, :].rearrange("p (t f) -> p t f", t=8)
    ps = es.enter_context(nc.psum_tensor("ps", [128, 8, 512], F32))

    Hv = BIG[:, 0:NCH * TS].rearrange("p (c t) -> p c t", c=NCH)
    actb = [BIG[:, NCH * TS + i * 1536:NCH * TS + (i + 1) * 1536].rearrange("p (c t) -> p c t", c=3) for i in range(2)]
    o = 0
    qst = BIG[:, o:o + 4 * TS].rearrange("p (c t) -> p c t", c=4); o += 4 * TS
    KW = TS + PAST
    kz = []
    for _ in range(2):
        kz.append(BIG[:, o:o + KW]); o += KW
    NVT = 22
    vaug = BIG[:, o:o + NVT * 256].rearrange("p (t g f) -> p t g f", t=NVT, g=2); o += NVT * 256
    TPAD = TS + 2 * PADW
    upad = BIG[:, o:o + 2 * TPAD].rearrange("p (c t) -> p c t", c=2); o += 2 * TPAD
    gpad = BIG[:, o:o + 2 * TPAD].rearrange("p (c t) -> p c t", c=2); o += 2 * TPAD
    assert o <= 29824, o

    tctr = [0]

    def tmp():
        i = tctr[0] % NTMP
        tctr[0] += 1
        return tmps[i], ("tmp", i)

    def tmpb(t):
        return t

    def mm(out, lhsT, rhs, start, stop, reads, writes):
        P.add("pe", lambda e: e.matmul(out, lhsT, rhs, start=start, stop=stop), reads, writes)

    def act(out, in_, func, reads, writes, scale=1.0, bias=0.0):
        P.add("act", lambda e: e.activation(out=out, in_=in_, func=func, scale=scale, bias=bias), reads, writes)

    def tt(out, in0, in1, op, reads, writes):
        P.add("dve", lambda e: e.tensor_tensor(out=out, in0=in0, in1=in1, op=op), reads, writes)

    def stt(out, in0, scalar, in1, op0, op1, reads, writes):
        P.add("dve", lambda e: e.scalar_tensor_tensor(out=out, in0=in0, scalar=scalar, in1=in1, op0=op0, op1=op1),
              reads, writes)

    def ts(out, in0, s1, op0, reads, writes, s2=None, op1=None):
        if op1 is None:
            P.add("dve", lambda e: e.tensor_scalar(out=out, in0=in0, scalar1=s1, scalar2=None, op0=op0), reads, writes)
        else:
            P.add("dve", lambda e: e.tensor_scalar(out=out, in0=in0, scalar1=s1, scalar2=s2, op0=op0, op1=op1),
                  reads, writes)

    def vcopy(out, in_, reads, writes):
        P.add("dve", lambda e: e.tensor_copy(out=out, in_=in_), reads, writes)

    def vmemset(ap, val, writes):
        P.add("dve", lambda e: e.memset(ap, val), (), writes)

    def dma(q, out, in_, key, reads, writes):
        P.add(q, lambda e: e.dma_start(out=out, in_=in_), reads, writes, dma=key)

    dma("pool", cmat[:], cmat_d[:, :, :], "c_cmat", (), ["cmat"])
    dma("pool", masks[:], mask_d[:, :, :], "c_mask", (), ["masks"])
    dma("sp", condf[:], cond_d[:, :, :], "c_cond", (), ["condf"])
    dma("sp", modb[:, 0, :], modb_d[0], "c_modb0", (), ["modb"])
    dma("sp", modb[:, 1, :], modb_d[1], "c_modb1", (), ["modb1"])
    dma("sp", ngt[:, 0], ng_d[0], "c_ng0", (), ["ngt"])
    dma("sp", ngt[:, 1], ng_d[1], "c_ng1", (), ["ngt1"])
    dma("sp", edget[:], edge_d[:, :, :, :], "c_edge", (), ["edget"])
    act(condb[:], condf[:], AF.Silu, ["condf"], ["condb"])
    ONES_MEAN = cmat[:, 0, :]
    BLK64 = cmat[:, 1, :]
    ONES256 = cmat[:, 2, :]
    IDENT = cmat[:, 3, :]
    PERM = cmat[:, 4, :]

    units = []

    def wi_src(l, which, f0, nf):
        v = wi_d[which][l].rearrange("(kc p) f -> p kc f", p=128)
        return v[:, :, f0 * 128:(f0 + nf) * 128], v[:, :, DFF + f0 * 128:DFF + (f0 + nf) * 128]

    for_units = []

    def plan_units():
        seq = []
        for l in range(1):
            pass
        return seq

    ucount = [0]
    ucursor = [0]
    ureleased = [0]

    def unit_views(kind, s, nf=3):
        sl = slots[s]
        if kind == "F":
            wi = sl[:, 0:8 * 2 * nf * 128].rearrange("p (k f) -> p k f", k=8)
            wo = sl[:, 6144:6144 + nf * 1024].rearrange("p (c f) -> p c f", c=nf)
            return wi, wo
        if kind == "M":
            return sl[:, 0:9216].rearrange("p (k f) -> p k f", k=8)
        if kind == "WIN":
            return sl[:, 0:6144].rearrange("p (k f) -> p k f", k=8)
        if kind == "WOUT":
            wout = sl[:, 0:8192].rearrange("p (k f) -> p k f", k=8)
            pw = sl[:, 8192:8704].rearrange("p (k f) -> p k f", k=2)
            pb = sl[:, 8704:8960].rearrange("p (k f) -> p k f", k=2)
            return wout, pw, pb
        if kind == "DIAG":
            return sl[:, 0:7936].rearrange("p (c j f) -> p c j f", c=2, j=31)
        raise ValueError(kind)

    def stoks(s):
        return [("slot", s, "a"), ("slot", s, "b")]

    def load_unit(u):
        idx, kind, l, arg = u
        s = idx % 2
        both = stoks(s)
        if kind == "F":
            which, f0, nf = arg
            wi, wo = unit_views("F", s, nf)
            g_src, u_src = wi_src(l, which, f0, nf)
            dma("pool", wi[:, :, 0:nf * 128], g_src, "u%d_a" % s, (), [("slot", s, "a")])
            dma("pool", wi[:, :, nf * 128:2 * nf * 128], u_src, "u%d_b" % s, (), [("slot", s, "a")])
            wsrc = wo_d[which][l][f0 * 128:(f0 + nf) * 128, :].rearrange("(c p) f -> p c f", p=128)
            dma("pool", wo, wsrc, "u%d_c" % s, (), [("slot", s, "b")])
        elif kind == "M":
            j = arg
            mv = unit_views("M", s)
            src = modw_d[l].rearrange("(kc p) f -> p kc f", p=128)[:, :, j * 1152:(j + 1) * 1152]
            dma("pool", mv, src, "u%d_a" % s, (), both)
        elif kind == "WIN":
            j = arg
            wv = unit_views("WIN", s)
            src = win_d[l].rearrange("(kc p) f -> p kc f", p=128)[:, :, j * 768:(j + 1) * 768]
            dma("pool", wv, src, "u%d_a" % s, (), both)
        elif kind == "WOUT":
            wout, pw, pb = unit_views("WOUT", s)
            dma("pool", wout, wout_d[l].rearrange("(kc p) f -> p kc f", p=128), "u%d_a" % s, (), both)
            dma("pool", pw, pw_d[l].rearrange("(kc p) f -> p kc f", p=128), "u%d_b" % s, (), both)
            P.add("pool", lambda e, pb=pb: e.memset(pb, 0.0), (), both)
            for gi in range(4):
                ch, half = gi // 2, gi % 2
                dma("pool", pb[64 * half:64 * half + 64, ch, 64 * half:64 * half + 64], poolw_d[l, gi],
                    "u%d_p%d" % (s, gi), (), both)
        elif kind == "DIAG":
            pass
        else:
            raise ValueError(kind)

    def _load_ready():
        while ucount[0] < len(units) and ucount[0] < ureleased[0] + 2:
            load_unit(units[ucount[0]])
            ucount[0] += 1

    def next_unit(expect_kind):
        u = units[ucursor[0]]
        assert u[1] == expect_kind, (u, expect_kind)
        _load_ready()
        assert ucount[0] > ucursor[0], ("unit not loadable yet", u, ureleased[0])
        ucursor[0] += 1
        return u[0] % 2, ("slotpair", u[0] % 2), u

    def release_unit():
        ureleased[0] += 1
        _load_ready()

    def add_units(kind_list):
        for (kind, l, arg) in kind_list:
            units.append((len(units), kind, l, arg))

    def layer_units(l):
        r = []
        f0 = 0
        for nf in FBLOCKS:
            r.append(("F", l, (0, f0, nf)))
            f0 += nf
        r += [("WIN", l, 0), ("WIN", l, 1), ("WOUT", l, None), ("DIAG", l, None)]
        f0 = 0
        for nf in FBLOCKS:
            r.append(("F", l, (1, f0, nf)))
            f0 += nf
        return r

    k0, k1 = (6, 6) if debug_stop is None else debug_stop
    plan = [("mod", 0)]
    need_mod1 = False
    if max(k0, k1) > 3:
        plan.append(("mod", 1))
    for phase, kk in ((0, k0), (1, k1)):
        if kk < 0:
            continue
        plan.append(("load", phase))
        for st in range(kk):
            l, sub = st // 3, st % 3
            if l == 1 and need_mod1:
                plan.append(("mod", 1))
                need_mod1 = False
            plan.append((("ffn1", "mixer", "ffn2")[sub], phase, l))
        plan.append(("store", phase))
    for it in plan:
        if it[0] == "mod":
            pass
        elif it[0] in ("ffn1", "ffn2"):
            f0 = 0
            for nf in FBLOCKS:
                add_units([("F", it[2], (0 if it[0] == "ffn1" else 1, f0, nf))])
                f0 += nf
        elif it[0] == "mixer":
            add_units([("WIN", it[2], 0), ("WIN", it[2], 1), ("DIAG", it[2], None), ("WOUT", it[2], None)])

    mod_q = []
    mod_loaded = [0]
    mod_done = [0]
    mod_list = []

    def mini_view(slot):
        return X[:, slot, 1024:2048].bitcast(BF16).rearrange("p (k f) -> p k f", k=8)

    def mini_toks(slot):
        return [("X", slot, 1024), ("X", slot, 1536)]

    def mod_enqueue(l):
        for m in range(36):
            mod_list.append((l, m))

    def _mod_load_ahead():
        while mod_loaded[0] < len(mod_list) and mod_loaded[0] < mod_done[0] + 8:
            l, m = mod_list[mod_loaded[0]]
            slot = mod_loaded[0] % 8
            src = modw_d[l].rearrange("(kc p) f -> p kc f", p=128)[:, :, m * 256:(m + 1) * 256]
            dma("pool", mini_view(slot), src, "mm%d" % slot, (), mini_toks(slot))
            mod_loaded[0] += 1

    def compute_AB(l, i):
        ng = "ngt" if l == 0 else "ngt1"
        for cj in range(2):
            sh = modv[l][:, (3 * i) * 8:(3 * i) * 8 + 8, cj]
            sc = modv[l][:, (3 * i + 1) * 8:(3 * i + 1) * 8 + 8, cj]
            gg = modv[l][:, (3 * i + 2) * 8:(3 * i + 2) * 8 + 8, cj]
            A = AB[:, l, cj, i, 0, :]
            B = AB[:, l, cj, i, 1, :]
            G = AB[:, l, cj, i, 2, :]
            stt(A, sc, 1.0, ngt[:, l, i, :], ALU.add, ALU.mult, [("modv", l), ng], [("AB", l)])
            vcopy(B, sh, [("modv", l)], [("AB", l)])
            ts(G, gg, 0.5 if i != 1 else 1.0, ALU.mult, [("modv", l)], [("AB", l)])

    def mod_pump(n):
        for _ in range(n):
            if mod_done[0] >= len(mod_list):
                return
            _mod_load_ahead()
            idx = mod_done[0]
            l, m = mod_list[idx]
            slot = idx % 8
            mv = mini_view(slot)
            mb = "modb" if l == 0 else "modb1"
            bank = 6 + (idx % 2)
            col0 = 0
            for cc in range(2):
                for k in range(NCH):
                    mm(ps[:, bank, col0 + 2 * cc:col0 + 2 * cc + 2], mv[:, k, cc * 128:(cc + 1) * 128], condb[:, k, :],
                       k == 0, k == NCH - 1, mini_toks(slot) + ["condb"], [("ps", bank)])
            for cc in range(2):
                cg = 2 * m + cc
                ts(modv[l][:, cg, :], ps[:, bank, col0 + 2 * cc:col0 + 2 * cc + 2], modb[:, l, cg:cg + 1], ALU.add,
                   [("ps", bank), mb], [("modv", l)])
            mod_done[0] += 1
            _mod_load_ahead()
            if m % 12 == 11:
                compute_AB(l, m // 12)

    def mod_ensure(l, i):
        while mod_done[0] < len(mod_list) and mod_list[mod_done[0]] <= (l, 12 * i + 11):
            mod_pump(1)

    def norm_mod(l, cj, i, t0, n, dst, dst_tok_fn):
        msb = 6
        for c in range(NCH):
            sq, sqt = tmp()
            sqv = sq.bitcast(BF16)[:, 0:n]
            act(sqv, X[:, c, t0:t0 + n], AF.Square, [("X", c, t0)], [sqt])
            mm(ps[:, msb, 0:n], ONES_MEAN, sqv, c == 0, c == NCH - 1, [sqt, "cmat"], [("ps", msb)])
        ln, lnt = tmp()
        act(ln[:, 0:n], ps[:, msb, 0:n], AF.Ln, [("ps", msb)], [lnt], bias=EPS)
        rs, rst = tmp()
        act(rs[:, 0:n], ln[:, 0:n], AF.Exp, [lnt], [rst], scale=-0.5)
        for c in range(NCH):
            t, ttok = tmp()
            stt(t[:, 0:n], X[:, c, t0:t0 + n], AB[:, l, cj, i, 0, c:c + 1], rs[:, 0:n], ALU.mult, ALU.mult,
                [("X", c, t0), ("AB", l), rst], [ttok])
            act(dst[:, c, 0:n] if dst is hgrp else dst[:, c, t0:t0 + n], t[:, 0:n], AF.Identity,
                [ttok, ("AB", l)], [dst_tok_fn(c)], bias=AB[:, l, cj, i, 1, c:c + 1])

    def ffn(l, which, cj, groups, hook=None):
        i = 0 if which == 0 else 2
        LOOK = 1
        for (t0, n) in groups[:LOOK]:
            norm_mod(l, cj, i, t0, n, Hv, lambda c, t0=t0: ("H", c, t0))
        items = []
        f0 = 0
        blk_info = []
        for bi, nf in enumerate(FBLOCKS):
            for gi, (t0, n) in enumerate(groups):
                items.append((bi, nf, f0, gi, t0, n))
            f0 += nf
        state = {"cur_blk": -1, "views": None, "tok": None}
        gu_ctr = [0]
        blk_views = {}

        def GU(it, ab):
            bi, nf, f0, gi, t0, n = it
            if bi not in blk_views:
                s, tok, u = next_unit("F")
                blk_views[bi] = (unit_views("F", s, nf), tok)
            (wi, wo), tok = blk_views[bi]
            for fc in range(nf):
                pr = gu_ctr[0] % 2
                gu_ctr[0] += 1
                bg, bu = 2 * pr, 2 * pr + 1
                for k in range(NCH):
                    mm(ps[:, bg, 0:n], wi[:, k, fc * 128:(fc + 1) * 128], Hv[:, k, t0:t0 + n], k == 0, k == NCH - 1,
                       [("slot", tok[1], "a"), ("H", k, t0)], [("ps", bg)])
                for k in range(NCH):
                    mm(ps[:, bu, 0:n], wi[:, k, (nf + fc) * 128:(nf + fc + 1) * 128], Hv[:, k, t0:t0 + n], k == 0,
                       k == NCH - 1, [("slot", tok[1], "a"), ("H", k, t0)], [("ps", bu)])
                sil, silt = tmp()
                act(sil[:, 0:n], ps[:, bg, 0:n], AF.Silu, [("ps", bg)], [silt])
                tt(actb[ab][:, fc, 0:n], ps[:, bu, 0:n], sil[:, 0:n], ALU.mult, [("ps", bu), silt], [("actb", ab, fc)])

        def WO(it, ab, last_of_block):
            bi, nf, f0, gi, t0, n = it
            (wi, wo), tok = blk_views[bi]
            for d in range(NCH):
                bo = 4 + (d % 2)
                for fc in range(nf):
                    mm(ps[:, bo, 0:n], wo[:, fc, d * 128:(d + 1) * 128], actb[ab][:, fc, 0:n], fc == 0, fc == nf - 1,
                       [("slot", tok[1], "b"), ("actb", ab, fc)], [("ps", bo)])
                stt(X[:, d, t0:t0 + n], ps[:, bo, 0:n], AB[:, l, cj, i, 2, d:d + 1], X[:, d, t0:t0 + n], ALU.mult, ALU.add,
                    [("ps", bo), ("AB", l), ("X", d, t0)], [("X", d, t0)])
            if last_of_block:
                release_unit()
                if hook is not None:
                    hook()

        ng = len(groups)
        for ii, it in enumerate(items):
            if it[0] == 0 and it[3] + LOOK < ng:
                (t0_, n_) = groups[it[3] + LOOK]
                norm_mod(l, cj, i, t0_, n_, Hv, lambda c, t0_=t0_: ("H", c, t0_))
            GU(it, ii % 2)
            if ii > 0:
                pit = items[ii - 1]
                WO(pit, (ii - 1) % 2, pit[3] == ng - 1)
        pit = items[-1]
        WO(pit, (len(items) - 1) % 2, True)

    def mixer(l, cj, groups, seqs, is_sample):
        T = sum(n for _, n in groups)
        ntiles = T // 128
        stok = "small%d" % 0

        def padcol(t):
            for si, (s0, sl) in enumerate(seqs):
                if s0 <= t < s0 + sl:
                    return t + PADW * (2 * si + 1)
            raise ValueError(t)

        dma("sp", smallt[:], small_d[l], "c_small", (), ["smallt"])
        dma("sp", dwt[:], dw_d[l], "c_dw", (), ["dwt"])
        dma("sp", sinkraw[:], sink_d[l], "c_sink", (), ["sinkraw"])
        act(esb[:], sinkraw[:], AF.Exp, ["sinkraw"], ["esb"])
        if is_sample:
            dma("pool", kz[0][0:64, TS:TS + PAST], ckT_d[l][0:64, :], "c_ck", (), [("kst", "cache")])
            dma("pool", kz[1][64:128, TS:TS + PAST], ckT_d[l][64:128, :], "c_ck1", (), [("kst", "cache")])
            cvv = cv_d[l].rearrange("(t p) f -> p t f", p=128)
            dma("pool", vaug[:, 18:22, 0, 0:64], cvv[:, :, 0:64], "c_cv0", (), [("vaug", "cache")])
            dma("pool", vaug[:, 18:22, 1, 64:128], cvv[:, :, 64:128], "c_cv1", (), [("vaug", "cache")])
        PSC = lambda c: smallt[:, c:c + 1]
        CB = lambda c: smallt[:, 2 + c:3 + c]
        CNG = lambda c: smallt[:, 4 + c:5 + c]
        QG = smallt[:, 6:7]
        KG = smallt[:, 7:8]

        sA, tokA, _ = next_unit("WIN")
        winA = unit_views("WIN", sA)
        sB, tokB, _ = next_unit("WIN")
        winB = unit_views("WIN", sB)

        def wcol(ch):
            if ch < 6:
                return winA[:, :, ch * 128:(ch + 1) * 128], tokA
            return winB[:, :, (ch - 6) * 128:(ch - 5) * 128], tokB

        HN = 264
        items = []
        for (g0, gn) in groups:
            for off in range(0, gn, 256):
                items.append((g0 + off, g0))
        all_tmp_toks = [("tmp", i) for i in range(NTMP)] + [("tmph", i, h) for i in range(NTMP) for h in range(2)]

        def tmp_barrier():
            P.add("dve", lambda e: e.memset(scr[:, :], 0.0), (), all_tmp_toks)

        tmp_barrier()
        hctr = {"n": 0, "c": 0}
        pools = {"n": [0, 1, 2], "c": [3, 4, 5, 6, 7]}

        def half(pool):
            lst = pools[pool]
            k = hctr[pool] % (2 * len(lst))
            hctr[pool] += 1
            ti, h = lst[k // 2], k % 2
            return tmps[ti][:, h * HN:h * HN + 256], tmps[ti].bitcast(BF16)[:, 2 * h * HN:2 * h * HN + 256], ("tmph", ti, h)

        def hb(i):
            return hgrp[:, :, (i % 2) * 256:(i % 2) * 256 + 256]

        def norm1(i):
            t0 = items[i][0]
            for c in range(NCH):
                _, sqv, sqt = half("n")
                act(sqv, X[:, c, t0:t0 + 256], AF.Square, [("X", c, items[i][1])], [sqt])
                mm(ps[:, 4, 0:256], ONES_MEAN, sqv, c == 0, c == NCH - 1, [sqt, "cmat"], [("ps", 4)])

        def norm2(i):
            t0 = items[i][0]
            lnv, _, lnt = half("c")
            act(lnv, ps[:, 4, 0:256], AF.Ln, [("ps", 4)], [lnt], bias=EPS)
            act(lnv, lnv, AF.Exp, [lnt], [lnt], scale=-0.5)
            for c in range(NCH):
                tv, _, ttok = half("n")
                stt(tv, X[:, c, t0:t0 + 256], AB[:, l, cj, 1, 0, c:c + 1], lnv, ALU.mult, ALU.mult,
                    [("X", c, items[i][1]), ("AB", l), lnt], [ttok])
                act(hb(i)[:, c, :], tv, AF.Identity, [ttok, ("AB", l)], [("hgrp", i % 2, c)], bias=AB[:, l, cj, 1, 1, c:c + 1])

        ubank = [0]

        def proj(i, ch):
            b = ubank[0] % 4
            ubank[0] += 1
            wv, wt = wcol(ch)
            for k in range(NCH):
                mm(ps[:, b, 0:256], wv[:, k, :], hb(i)[:, k, :], k == 0, k == NCH - 1, [wt, ("hgrp", i % 2, k)], [("ps", b)])
            return b

        def item_body(i, mid_hook):
            t0 = items[i][0]
            a = padcol(t0)
            if is_sample:
                rc_ = ropec[:, (i % 2) * 256:(i % 2) * 256 + 256]
                rs_ = ropes[:, (i % 2) * 256:(i % 2) * 256 + 256]
                vtk = [("vout", g_) for g_ in range(8)]
                dma("sp", rc_, cos_d[:, t0:t0 + 256], "c_rc%d" % (i % 2), (), [("ropec", i % 2)] + vtk)
                dma("sp", rs_, sin_d[:, t0:t0 + 256], "c_rs%d" % (i % 2), (), [("ropes", i % 2)] + vtk)
            for ch in (0, 1):
                b = proj(i, ch)
                act(upad[:, ch, a:a + 256], ps[:, b, 0:256], AF.Copy, [("ps", b)], [("upad", ch, items[i][1])])
            mid_hook()
            for cc in (0, 1):
                bgt = proj(i, 4 + cc)
                sgv, _, sgt = half("c")
                act(sgv, ps[:, bgt, 0:256], AF.Sigmoid, [("ps", bgt)], [sgt])
                ba = proj(i, 2 + cc)
                tt(gpad[:, cc, a:a + 256], ps[:, ba, 0:256], sgv, ALU.mult, [("ps", ba), sgt], [("gpad", cc, items[i][1])])
            st = {}

            def stA(ch):
                b = proj(i, ch)
                _, sqv, sqt = half("c")
                act(sqv, ps[:, b, 0:256], AF.Square, [("ps", b)], [sqt])
                st[ch] = dict(b=b, sqv=sqv, sqt=sqt)

            def stB(ch):
                d = st[ch]
                mm(ps[:, 5, 0:256], BLK64, d["sqv"], True, True, [d["sqt"], "cmat"], [("ps", 5)])
                lnv, _, lnt = half("c")
                act(lnv, ps[:, 5, 0:256], AF.Ln, [("ps", 5)], [lnt], bias=EPS)
                act(lnv, lnv, AF.Exp, [lnt], [lnt], scale=-0.5)
                gsc = KG if ch == 10 else QG
                if ch == 10:
                    dstv, dtok = None, ("kst", items[i][1])
                else:
                    dstv, dtok = qst[:, ch - 6, t0:t0 + 256], ("qst", ch - 6, items[i][1])
                if not is_sample and ch != 10:
                    stt(dstv, ps[:, d["b"], 0:256], gsc, lnv, ALU.mult, ALU.mult, [("ps", d["b"]), "smallt", lnt], [dtok])
                    return
                qnv, _, qnt = half("c")
                stt(qnv, ps[:, d["b"], 0:256], gsc, lnv, ALU.mult, ALU.mult, [("ps", d["b"]), "smallt", lnt], [qnt])
                d.update(qnv=qnv, qnt=qnt, dstv=dstv, dtok=dtok)
                if not is_sample:
                    vcopy(kz[0][0:64, t0:t0 + 256], qnv[0:64, :], [qnt], [dtok])
                    vcopy(kz[1][64:128, t0:t0 + 256], qnv[64:128, :], [qnt], [dtok])
                    dma("sp", nk_d[l][:, t0:t0 + 256], qnv, "o_nk", [qnt], [])
                else:
                    _, qbv, qbt = half("c")
                    vcopy(qbv, qnv, [qnt], [qbt])
                    d.update(qbv=qbv, qbt=qbt)

            def stC(ch):
                if not is_sample:
                    return
                d = st[ch]
                mm(ps[:, 6, 0:256], PERM, d["qbv"], True, True, [d["qbt"], "cmat"], [("ps", 6)])
                tt(d["qnv"], d["qnv"], rc_, ALU.mult, [d["qnt"], ("ropec", i % 2)], [d["qnt"]])
                t2v, _, t2t = half("c")
                tt(t2v, ps[:, 6, 0:256], rs_, ALU.mult, [("ps", 6), ("ropes", i % 2)], [t2t])
                if ch == 10:
                    tt(kz[0][0:64, t0:t0 + 256], d["qnv"][0:64, :], t2v[0:64, :], ALU.add, [d["qnt"], t2t], [d["dtok"]])
                    tt(kz[1][64:128, t0:t0 + 256], d["qnv"][64:128, :], t2v[64:128, :], ALU.add, [d["qnt"], t2t], [d["dtok"]])
                else:
                    tt(d["dstv"], d["qnv"], t2v, ALU.add, [d["qnt"], t2t], [d["dtok"]])

            chs = [6, 7, 8, 9, 10]
            for s_ in range(len(chs) + 2):
                if s_ < len(chs):
                    stA(chs[s_])
                if 0 <= s_ - 1 < len(chs):
                    stB(chs[s_ - 1])
                if 0 <= s_ - 2 < len(chs):
                    stC(chs[s_ - 2])
            wv, wt = wcol(11)
            for tl in range(2):
                for k in range(NCH):
                    mm(ps[:, 7, tl * 128:(tl + 1) * 128], hb(i)[:, k, tl * 128:(tl + 1) * 128], wv[:, k, :], k == 0,
                       k == NCH - 1, [wt, ("hgrp", i % 2, k)], [("ps", 7)])
            for tl in range(2):
                gt = (t0 // 128) + tl
                act(vaug[:, gt, 0, 0:64], ps[:, 7, tl * 128:tl * 128 + 64], AF.Copy, [("ps", 7)], [("vaug", gt)])
                act(vaug[:, gt, 1, 64:128], ps[:, 7, tl * 128 + 64:tl * 128 + 128], AF.Copy, [("ps", 7)], [("vaug", gt)])
                if not is_sample:
                    vcopy(vout[:, gt, :], ps[:, 7, tl * 128:(tl + 1) * 128], [("ps", 7)], [("vout", gt)])

        norm1(0)
        norm2(0)
        for i in range(len(items)):
            if i + 1 < len(items):
                norm1(i + 1)
                item_body(i, lambda i=i: norm2(i + 1))
            else:
                item_body(i, lambda: None)
        tmp_barrier()
        release_unit()
        release_unit()
        if not is_sample:
            dma("sp", nv_d[l].rearrange("(t p) f -> p t f", p=128), vout, "o_nv",
                [("vout", gt) for gt in range(8)], [])

        sD, tokD, _ = next_unit("DIAG")
        diag = unit_views("DIAG", sD)
        sW, tokW, _ = next_unit("WOUT")
        wout, pwv, pbv = unit_views("WOUT", sW)
        for cc in range(2):
            for j in range(31):
                act(diag[:, cc, j, :], IDENT, AF.Copy, ["cmat", "dwt"], [tokD], scale=dwt[:, cc, j:j + 1])

        def wout_part(kc0, src, src_tokfn, t0, n):
            for d in range(NCH):
                bo = 4 + (d % 2)
                for kk in range(4):
                    mm(ps[:, bo, 0:n], wout[:, kc0 + kk, d * 128:(d + 1) * 128], src[:, kk, 0:n], kk == 0, kk == 3,
                       [tokW, src_tokfn(kk)], [("ps", bo)])
                stt(X[:, d, t0:t0 + n], ps[:, bo, 0:n], AB[:, l, cj, 1, 2, d:d + 1], X[:, d, t0:t0 + n], ALU.mult, ALU.add,
                    [("ps", bo), ("AB", l), ("X", d, t0)], [("X", d, t0)])

        allsegs = []
        for (t0, n) in groups:
            t = t0
            while t < t0 + n:
                for (s0, sl) in seqs:
                    if s0 <= t < s0 + sl:
                        e = min(t0 + n, s0 + sl)
                        allsegs.append(dict(st=t, sn=e - t, at_start=(t == s0), at_end=(e == s0 + sl), t0=t0, n=n,
                                            last=(e == t0 + n)))
                        t = e
                        break
        gr_all = {cc: [("gpad", cc, g0) for (g0, _) in groups] for cc in (0, 1)}
        ur_all = {ch: [("upad", ch, g0) for (g0, _) in groups] for ch in (0, 1)}

        def conv_mms(si):
            sg_ = allsegs[si]
            a, sn = padcol(sg_["st"]), sg_["sn"]
            for cc in (0, 1):
                b = 2 * (si % 2) + cc
                for j in range(31):
                    mm(ps[:, b, 0:sn], diag[:, cc, j, :], gpad[:, cc, a + j - 15:a + j - 15 + sn], j == 0, j == 30,
                       [tokD] + gr_all[cc], [("ps", b)])

        def pool_dve(si):
            sg_ = allsegs[si]
            a, sn, at_start, at_end = padcol(sg_["st"]), sg_["sn"], sg_["at_start"], sg_["at_end"]
            outs = []
            for ch in (0, 1):
                ur = ur_all[ch]
                A_, At = tmp()
                tt(A_[:, 0:sn + 14], upad[:, ch, a - 8:a + sn + 6], upad[:, ch, a - 7:a + sn + 7], ALU.add, ur, [At])
                B_, Bt = tmp()
                tt(B_[:, 0:sn + 12], A_[:, 0:sn + 12], A_[:, 2:sn + 14], ALU.add, [At], [Bt])
                if ch == 0:
                    lo_src, lo_off, lo_w = A_, 7, 2
                    hi_src, hi_off, hi_w = B_, 6, 4
                    lot, hit = At, Bt
                else:
                    C_, Ct = tmp()
                    tt(C_[:, 0:sn + 8], B_[:, 0:sn + 8], B_[:, 4:sn + 12], ALU.add, [Bt], [Ct])
                    D_, Dt = tmp()
                    tt(D_[64:128, 0:sn], C_[64:128, 0:sn], C_[64:128, 8:sn + 8], ALU.add, [Ct], [Dt])
                    lo_src, lo_off, lo_w = C_, 4, 8
                    hi_src, hi_off, hi_w = D_, 0, 16
                    lot, hit = Ct, Dt
                mean, mt = tmp()
                ts(mean[0:64, 0:sn], lo_src[0:64, lo_off:lo_off + sn], 1.0 / lo_w, ALU.mult, [lot], [mt])
                ts(mean[64:128, 0:sn], hi_src[64:128, hi_off:hi_off + sn], 1.0 / hi_w, ALU.mult, [hit], [mt])
                if at_start:
                    tt(mean[0:64, 0:8], lo_src[0:64, lo_off:lo_off + 8], edget[0:64, ch, 0, :], ALU.mult,
                       [lot, "edget"], [mt])
                    tt(mean[64:128, 0:8], hi_src[64:128, hi_off:hi_off + 8], edget[64:128, ch, 0, :], ALU.mult,
                       [hit, "edget"], [mt])
                if at_end:
                    tt(mean[0:64, sn - 8:sn], lo_src[0:64, lo_off + sn - 8:lo_off + sn], edget[0:64, ch, 1, :],
                       ALU.mult, [lot, "edget"], [mt])
                    tt(mean[64:128, sn - 8:sn], hi_src[64:128, hi_off + sn - 8:hi_off + sn], edget[64:128, ch, 1, :],
                       ALU.mult, [hit, "edget"], [mt])
                plv = A_.bitcast(BF16)[:, 0:sn]
                tt(plv, mean[:, 0:sn], upad[:, ch, a:a + sn], ALU.subtract, [mt] + ur, [At])
                outs.append((plv, At, tctr[0]))
            return outs

        def pool_mm(si, outs):
            sg_ = allsegs[si]
            sn, off = sg_["sn"], sg_["st"] - sg_["t0"]
            for ch in (0, 1):
                plv, plt, ser = outs[ch]
                assert tctr[0] - ser < NTMP - 1, "tmp ring wrapped (pool)"
                b = 6 + ch
                mm(ps[:, b, 0:sn], pbv[:, ch, :], plv, True, True, [tokW, plt], [("ps", b)])
                act(opc[:, ch, off:off + sn], ps[:, b, 0:sn], AF.Copy, [("ps", b), "smallt"], [("omix", ch)],
                    scale=PSC(ch))

        def conv_post(si):
            sg_ = allsegs[si]
            sn, off = sg_["sn"], sg_["st"] - sg_["t0"]
            ybs = []
            for cc in (0, 1):
                b = 2 * (si % 2) + cc
                yb, ybt = tmp()
                act(yb[:, 0:sn], ps[:, b, 0:sn], AF.Identity, [("ps", b), "smallt"], [ybt], bias=CB(cc))
                sq, sqt = tmp()
                sqv = sq.bitcast(BF16)[:, 0:sn]
                act(sqv, yb[:, 0:sn], AF.Square, [ybt], [sqt])
                mm(ps[:, 6, 0:sn], ONES256, sqv, cc == 0, cc == 1, [sqt, "cmat"], [("ps", 6)])
                ybs.append((yb, ybt))
            ln, lnt = tmp()
            act(ln[:, 0:sn], ps[:, 6, 0:sn], AF.Ln, [("ps", 6)], [lnt], bias=EPS)
            act(ln[:, 0:sn], ln[:, 0:sn], AF.Exp, [lnt], [lnt], scale=-0.5)
            zs = []
            for cc in (0, 1):
                yb, ybt = ybs[cc]
                stt(yb[:, 0:sn], yb[:, 0:sn], CNG(cc), ln[:, 0:sn], ALU.mult, ALU.mult, [ybt, "smallt", lnt], [ybt])
                zb, zbt = tmp()
                zbv = zb.bitcast(BF16)[:, 0:sn]
                act(zbv, yb[:, 0:sn], AF.Silu, [ybt], [zbt])
                zs.append((zbv, zbt))
            for co in (0, 1):
                b = 6 + co
                for ci in (0, 1):
                    mm(ps[:, b, 0:sn], pwv[:, ci, co * 128:(co + 1) * 128], zs[ci][0], ci == 0, ci == 1,
                       [tokW, zs[ci][1]], [("ps", b)])
                act(opc[:, 2 + co, off:off + sn], ps[:, b, 0:sn], AF.Copy, [("ps", b)], [("omix", 2 + co)])

        conv_mms(0)
        for si in range(len(allsegs)):
            outs = pool_dve(si)
            if si + 1 < len(allsegs):
                conv_mms(si + 1)
            pool_mm(si, outs)
            conv_post(si)
            if allsegs[si]["last"]:
                wout_part(0, opc, lambda kk: ("omix", kk), allsegs[si]["t0"], allsegs[si]["n"])

        release_unit()
        sbank = [0]
        kall = [("kst", g0) for (g0, _) in groups] + ([("kst", "cache")] if is_sample else [])
        allsteps = []
        ginfo = []
        for (t0, n) in groups:
            qall = [("qst", c, t0) for c in range(4)]
            first = len(allsteps)
            ntile0 = None
            for tl in range(n // 128):
                gt = t0 // 128 + tl
                q0 = t0 + tl * 128
                chunks = []
                if is_sample:
                    if gt > 0:
                        chunks.append((q0 - 128, gt - 1, 0))
                    chunks.append((q0, gt, None))
                    if gt < ntiles - 1:
                        chunks.append((q0 + 128, gt + 1, 1))
                    for cti in range(4):
                        chunks.append((TS + cti * 128, 18 + cti, None))
                else:
                    for (s0, sl) in seqs:
                        if s0 <= q0 < s0 + sl:
                            for kt in range(sl // 128):
                                chunks.append((s0 + kt * 128, (s0 // 128) + kt, None))
                for kvh in range(2):
                    for ci, ch in enumerate(chunks):
                        allsteps.append((tl, gt, q0, ci, len(chunks), ch, kvh, t0, qall))
                if ntile0 is None:
                    ntile0 = len(allsteps) - first
            ginfo.append((first, ntile0, t0, n, len(allsteps) - 1))

        def qk(step):
            tl, gt, q0, ci, nci, (kc0, vt, mk), kvh, t0, qall = step
            b = sbank[0] % 4
            sbank[0] += 1
            for hh in range(4):
                mm(ps[:, b, hh * 128:(hh + 1) * 128], kz[kvh][:, kc0:kc0 + 128],
                   qst[:, hh, q0:q0 + 128], True, True, kall + qall, [("ps", b)])
            pt, ptt = tmp()
            ptv = pt.bitcast(BF16)[:, 0:512]
            act(ptv, ps[:, b, :], AF.Exp, [("ps", b)], [ptt], scale=0.125)
            if mk is not None:
                pv4 = ptv.rearrange("p (h q) -> p h q", h=4)
                tt(pv4, pv4, masks[:, mk, :].unsqueeze(1).broadcast_to([128, 4, 128]), ALU.mult, [ptt, "masks"], [ptt])
            return (ptv, ptt, tctr[0])

        def pv(step, pts):
            tl, gt, q0, ci, nci, (kc0, vt, mk), kvh, t0, qall = step
            pb = 4 + 2 * (gt % 2)
            assert tctr[0] - pts[2] < NTMP, "tmp ring wrapped"
            vtok = ("vaug", vt) if vt < 18 else ("vaug", "cache")
            mm(ps[:, pb + kvh, :], vaug[:, vt, kvh, :], pts[0], ci == 0, ci == nci - 1,
               [vtok, pts[1]], [("ps", pb + kvh)])
            if ci == nci - 1:
                dlo, nlo = (64, 0) if kvh == 0 else (0, 64)
                rc2, rc2t = tmp()
                if kvh == 0:
                    ln, lnt = tmp()
                    for hh in range(4):
                        act(ln[dlo:dlo + 64, hh * 128:(hh + 1) * 128], ps[dlo:dlo + 64, pb + kvh, hh * 128:(hh + 1) * 128],
                            AF.Ln, [("ps", pb + kvh), "esb"], [lnt], bias=esb[dlo:dlo + 64, 4 * kvh + hh:4 * kvh + hh + 1])
                    act(ln[dlo:dlo + 64, 0:512], ln[dlo:dlo + 64, 0:512], AF.Exp, [lnt], [lnt], scale=-1.0)
                    vcopy(rc2[nlo:nlo + 64, 0:512], ln[dlo:dlo + 64, 0:512], [lnt], [rc2t])
                else:
                    for hh in range(4):
                        ts(rc2[nlo:nlo + 64, hh * 128:(hh + 1) * 128], ps[dlo:dlo + 64, pb + kvh, hh * 128:(hh + 1) * 128],
                           esb[dlo:dlo + 64, 4 * kvh + hh:4 * kvh + hh + 1], ALU.add, [("ps", pb + kvh), "esb"], [rc2t])
                    P.add("dve", lambda e, o_=rc2[nlo:nlo + 64, 0:512]: e.reciprocal(out=o_, in_=o_), [rc2t], [rc2t])
                tt(oattn[nlo:nlo + 64, :, tl * 128:(tl + 1) * 128],
                   ps[nlo:nlo + 64, pb + kvh, :].rearrange("p (h q) -> p h q", h=4),
                   rc2[nlo:nlo + 64, 0:512].rearrange("p (h q) -> p h q", h=4), ALU.mult,
                   [("ps", pb + kvh), rc2t], [("omix", kk) for kk in range(4)])


        def wout_attn(t0, n):
            for d in range(NCH):
                bo = 6 + (d % 2)
                for kk in range(4):
                    mm(ps[:, bo, 0:n], wout[:, 4 + kk, d * 128:(d + 1) * 128], oattn[:, kk, 0:n], kk == 0, kk == 3,
                       [tokW, ("omix", kk)], [("ps", bo)])
                stt(X[:, d, t0:t0 + n], ps[:, bo, 0:n], AB[:, l, cj, 1, 2, d:d + 1], X[:, d, t0:t0 + n], ALU.mult, ALU.add,
                    [("ps", bo), ("AB", l), ("X", d, t0)], [("X", d, t0)])

        DEPTH = 3
        pend = []
        due = []
        for si, step in enumerate(allsteps):
            while due and due[0][0] <= si:
                _, t0_, n_ = due.pop(0)
                wout_attn(t0_, n_)
            pend.append((si, step, qk(step)))
            if len(pend) > DEPTH:
                pi, pstep, ppts = pend.pop(0)
                pv(pstep, ppts)
                for gi, (first, ntile0, t0_, n_, last) in enumerate(ginfo):
                    if pi == last:
                        if gi + 1 < len(ginfo):
                            nfirst, nnt0 = ginfo[gi + 1][0], ginfo[gi + 1][1]
                            due.append((nfirst + min(DEPTH + 4, nnt0 // 2 - 1 + DEPTH), t0_, n_))
                        else:
                            due.append((10 ** 9, t0_, n_))
        while pend:
            pi, pstep, ppts = pend.pop(0)
            pv(pstep, ppts)
            for gi, (first, ntile0, t0_, n_, last) in enumerate(ginfo):
                if pi == last:
                    due.append((10 ** 9, t0_, n_))
        for (_, t0_, n_) in due:
            wout_attn(t0_, n_)
        release_unit()

    def mkgroups(T):
        g = []
        t = 0
        while t < T:
            n = min(512, T - t)
            g.append((t, n))
            t += n
        return g

    def big_switch():
        P.add("dve", lambda e: e.memset(scr[:, :], 0.0), (), ["BIG"])

    def init_big(is_sample, T):
        big_switch()
        seqs_ = [(0, TS)] if is_sample else [(s_ * 256, 256) for s_ in range(4)]
        ut = [("upad", ch, g0) for ch in (0, 1) for (g0, _) in mkgroups(T)]
        gt_ = [("gpad", ch, g0) for ch in (0, 1) for (g0, _) in mkgroups(T)]
        for si, (s0, sl) in enumerate(seqs_):
            a = s0 + PADW * (2 * si + 1)
            for (c0, c1) in ((a - PADW, a), (a + sl, a + sl + PADW)):
                vmemset(upad[:, :, c0:c1], 0.0, ut)
                vmemset(gpad[:, :, c0:c1], 0.0, gt_)
        vt = [("vaug", t) for t in range(18)] + [("vaug", "cache")]
        vmemset(vaug[:, :, 0, 64:128], 1.0, vt)
        vmemset(vaug[:, :, 1, 0:64], 1.0, vt)
        ktoks = [("kst", g0) for (g0, _) in mkgroups(T)] + [("kst", "cache")]
        vmemset(kz[0][64:128, :], 0.0, ktoks)
        vmemset(kz[1][0:64, :], 0.0, ktoks)

    for it in plan:
        if it[0] == "mod":
            mod_enqueue(it[1])
            if it[1] == 0:
                mod_pump(12)
            continue
        phase = it[1]
        is_sample = phase == 1
        T = TS if is_sample else TP
        xin = xs_d if is_sample else xp_d
        yout = ys_d if is_sample else yp_d
        groups = mkgroups(T)
        seqs = [(0, TS)] if is_sample else [(s * 256, 256) for s in range(4)]
        if it[0] == "load":
            if is_sample:
                mod_ensure(1, 2)
            for c in range(NCH):
                dma("sp", X[:, c, 0:T], xin[c], "x_in%d" % c, (), [("X", c, g0) for (g0, _) in groups])
        elif it[0] == "store":
            for c in range(NCH):
                dma("sp", yout[c], X[:, c, 0:T], "y_out%d" % c, [("X", c, g0) for (g0, _) in groups], [])
        elif it[0] == "ffn1":
            mod_ensure(it[2], 0)
            ffn(it[2], 0, phase, groups, hook=lambda: mod_pump(5))
        elif it[0] == "ffn2":
            mod_ensure(it[2], 2)
            ffn(it[2], 1, phase, groups, hook=lambda: mod_pump(5))
        elif it[0] == "mixer":
            mod_ensure(it[2], 1)
            init_big(is_sample, T)
            mixer(it[2], phase, groups, seqs, is_sample)
            big_switch()

    final_keys = ["y_out%d" % c for c in range(NCH)] + ["o_nk", "o_nv"]
    P.emit(nc, es, final_keys)
    es.close()
    nc._prog_stats = P.stats
    return nc


_NC_CACHE = {}


def _head_perm():
    cols = []
    for j in range(4):
        cols += list(range(j * 64, j * 64 + 64))
        cols += list(range((4 + j) * 64, (4 + j) * 64 + 64))
    return np.array(cols)


def _consts():
    cm = np.zeros((128, 5, 128), np.float32)
    cm[:, 0, :] = 1.0 / 1024
    for b in range(2):
        cm[64 * b:64 * b + 64, 1, 64 * b:64 * b + 64] = 1.0 / 64
    cm[:, 2, :] = 1.0 / 256
    cm[:, 3, :] = np.eye(128, dtype=np.float32)
    for m in range(128):
        d = m % 32
        partner = m + 16 if d < 16 else m - 16
        cm[partner, 4, m] = 1.0
    k = np.arange(128)[:, None]
    q = np.arange(128)[None, :]
    mprev = (k >= q).astype(np.float32)
    mnext = (k <= q).astype(np.float32)
    masks = np.stack([mprev, mnext], axis=1)
    sinkl = np.zeros((1, 2, 128), np.float32)
    sinkl[0, 0, 64:128] = 1.0
    sinkl[0, 1, 0:64] = 1.0
    edge = np.zeros((128, 2, 2, 8), np.float32)
    wins = {(0, 0): 2, (0, 1): 4, (1, 0): 8, (1, 1): 16}
    for (ch, half), w in wins.items():
        for i in range(8):
            cs = (i + w // 2) - max(i - w // 2, 0)
            r = 8 - i
            ce = min(w // 2, r) + w // 2
            edge[64 * half:64 * half + 64, ch, 0, i] = 1.0 / cs
            edge[64 * half:64 * half + 64, ch, 1, i] = 1.0 / ce
    return cm, masks, sinkl, edge


def _rope_tables(pos0):
    pos = pos0 + np.arange(TS)
    row = (pos // 64).astype(np.float64)
    col = (pos % 64).astype(np.float64)
    half = 32
    inv = 10000.0 ** (-np.arange(0, half, 2, dtype=np.float64) / half)
    cos = np.zeros((128, TS), np.float32)
    sin = np.zeros((128, TS), np.float32)
    for p in range(128):
        d = p % 64
        posv = row if d < 32 else col
        dd = d % 32
        i = dd % 16
        ang = posv * inv[i]
        cos[p] = np.cos(ang)
        sin[p] = np.sin(ang) * (-1.0 if dd < 16 else 1.0)
    return cos, sin


def kernel(x_prompt, x_sample, cache_k, cache_v, c, c_ctx, mod_w, mod_b, norm_g,
           ffn1_wi, ffn1_wo, ffn2_wi, ffn2_wo, w_in, w_out, pool_w, pool_scale,
           conv_dw, conv_b, conv_norm_g, conv_pw, q_norm_g, k_norm_g, sink, _only_core=None, _debug_stop=None):
    f = lambda a: np.ascontiguousarray(np.asarray(a, dtype=np.float32))
    x_prompt, x_sample, cache_k, cache_v = f(x_prompt), f(x_sample), f(cache_k), f(cache_v)
    if "nc" not in _NC_CACHE:
        _NC_CACHE["nc"] = build_program()
    nc = _NC_CACHE["nc"]
    hp = _head_perm()
    w_in_p = f(w_in).copy()
    w_in_p[:, :, 768:1280] = f(w_in)[:, :, 768 + hp]
    w_out_p = f(w_out).copy()
    w_out_p[:, 512:1024, :] = f(w_out)[:, 512 + hp, :]
    modb_l = f(np.asarray(mod_b).reshape(2, 72, 128).transpose(0, 2, 1))
    ng_l = f(np.asarray(norm_g).reshape(2, 3, NCH, 128).transpose(0, 3, 1, 2))
    small = np.zeros((2, 128, 16), np.float32)
    small[:, :, 0:2] = np.asarray(pool_scale).reshape(2, 2, 128).transpose(0, 2, 1)
    small[:, :, 2:4] = np.asarray(conv_b).reshape(2, 2, 128).transpose(0, 2, 1)
    small[:, :, 4:6] = np.asarray(conv_norm_g).reshape(2, 2, 128).transpose(0, 2, 1)
    small[:, :, 6] = np.tile(np.asarray(q_norm_g), (1, 2))
    small[:, :, 7] = np.tile(np.asarray(k_norm_g), (1, 2))
    dw_l = f(np.asarray(conv_dw).reshape(2, 31, 2, 128).transpose(0, 3, 2, 1))
    sk = np.asarray(sink, dtype=np.float32)
    sink_b = f(np.broadcast_to(sk[:, None, :], (2, 128, 8)))
    cm, masks, sinkl, edge = _consts()
    shared = dict(mod_w=f(mod_w), mod_b=modb_l, norm_g=ng_l, ffn1_wi=f(ffn1_wi), ffn2_wi=f(ffn2_wi),
                  ffn1_wo=f(ffn1_wo), ffn2_wo=f(ffn2_wo), w_in=w_in_p, w_out=w_out_p, pool_w=f(pool_w),
                  small=small, conv_dw=dw_l, conv_pw=f(conv_pw), sink_b=sink_b, cmat=cm, masks=masks,
                  pool_edge=edge)
    in_maps = []
    starts = []
    for core in (range(8) if _only_core is None else [_only_core]):
        b, hf = core // 2, core % 2
        s0 = 0 if hf == 0 else 4096 - TS
        starts.append(s0)
        xp = x_prompt[4 * core:4 * core + 4].reshape(TP, D).T.reshape(NCH, 128, TP)
        xs = x_sample[b, s0:s0 + TS].T.reshape(NCH, 128, TS)
        cond = np.stack([np.asarray(c_ctx, np.float32), np.asarray(c, np.float32)[b]], axis=1)
        cond = cond.reshape(NCH, 128, 2).transpose(1, 0, 2)
        ckT = cache_k[b].reshape(2, PAST, 128).transpose(0, 2, 1)
        cv = cache_v[b].reshape(2, PAST, 128)
        cos, sin = _rope_tables(s0)
        m = dict(shared)
        m.update(xp=f(xp), xs=f(xs), cond=f(cond), cache_kT=f(ckT), cache_v=f(cv), rope_cos=cos, rope_sin=sin)
        in_maps.append(m)
    if _only_core is not None:
        nc = build_program(_debug_stop)
        res = run_bass_kernel_spmd(nc, in_maps, core_ids=[0])
        print("EXEC_NS", res.exec_time_ns, nc._prog_stats)
        return res.results[0]
    res = run_bass_kernel_spmd(nc, in_maps, core_ids=list(range(8)))
    y_prompt = np.zeros((32, 256, D), np.float32)
    y_sample = np.zeros((4, 4096, D), np.float32)
    nk = np.zeros((32, 2, 256, 2, 64), np.float32)
    nv = np.zeros((32, 2, 256, 2, 64), np.float32)
    for core in range(8):
        r = res.results[core]
        b, hf = core // 2, core % 2
        yp = np.asarray(r["yp"]).reshape(D, TP).T.reshape(4, 256, D)
        y_prompt[4 * core:4 * core + 4] = yp
        ys = np.asarray(r["ys"]).reshape(D, TS).T
        if hf == 0:
            y_sample[b, 0:2048] = ys[0:2048]
        else:
            y_sample[b, 2048:4096] = ys[TS - 2048:TS]
        k_ = np.asarray(r["nk"]).transpose(0, 2, 1).reshape(2, 4, 256, 2, 64)
        v_ = np.asarray(r["nv"]).reshape(2, 4, 256, 2, 64)
        nk[4 * core:4 * core + 4] = k_.transpose(1, 0, 2, 3, 4)
        nv[4 * core:4 * core + 4] = v_.transpose(1, 0, 2, 3, 4)
    return (y_prompt, y_sample, nk, nv)
```

```python
import numpy as np
from contextlib import ExitStack
import concourse.bass as bass
import concourse.mybir as mybir
from concourse.bass_utils import run_bass_kernel_spmd

F32 = mybir.dt.float32
BF16 = mybir.dt.bfloat16
AF = mybir.ActivationFunctionType
ALU = mybir.AluOpType

D = 1024
NCH = 8
DFF = 2816
NFC = 22
TP = 1024
TS = 2304
PAST = 512
PADW = 16
EPS = 1e-6
SLOT = 9216
FBLOCKS = [3, 3, 3, 3, 3, 3, 3, 1]
NTMP = 8


class Prog:
    def __init__(self):
        self.ops = []
        self.lastw = {}
        self.readers = {}
        self.ambient = True

    def add(self, eng, fn, reads=(), writes=(), dma=None):
        i = len(self.ops)
        def _exp(lst):
            o = []
            for t in lst:
                if isinstance(t, tuple) and len(t) == 2 and t[0] == "slotpair":
                    o += [("slot", t[1], "a"), ("slot", t[1], "b")]
                else:
                    o.append(t)
            return o
        reads = _exp(reads)
        writes = _exp(writes)
        if self.ambient and eng in ("pe", "act", "dve") and dma is None and "BIG" not in writes:
            reads.append("BIG")
        deps = set()
        for t in reads:
            w = self.lastw.get(t)
            if w is not None:
                deps.add(w)
        for t in writes:
            w = self.lastw.get(t)
            if w is not None:
                deps.add(w)
            deps.update(self.readers.get(t, ()))
        red = {}
        for d in deps:
            p = self.ops[d]
            k = ("dma", p["dma"]) if p["dma"] is not None else ("eng", p["eng"])
            if k not in red or d > red[k]:
                red[k] = d
        deps = set(red.values())
        self.ops.append(dict(eng=eng, fn=fn, deps=deps, dma=dma, val=None))
        for t in reads:
            self.readers.setdefault(t, []).append(i)
        for t in writes:
            self.lastw[t] = i
            self.readers[t] = []
        return i

    def emit(self, nc, es, final_keys):
        ops = self.ops
        needed = set()
        for op in ops:
            for d in op["deps"]:
                p = ops[d]
                if p["dma"] is None:
                    if p["eng"] == "pe" and op["eng"] == "pe" and op["dma"] is None:
                        continue
                    needed.add(d)
        cnt = {}
        dcnt = {}
        for i, op in enumerate(ops):
            if op["dma"] is not None:
                dcnt[op["dma"]] = dcnt.get(op["dma"], 0) + 16
                op["val"] = dcnt[op["dma"]]
            elif i in needed:
                cnt[op["eng"]] = cnt.get(op["eng"], 0) + 1
                op["val"] = cnt[op["eng"]]
        self.stats = dict(cnt=dict(cnt), dmax=max(dcnt.values()), nops=len(ops), ndma=len(dcnt))
        sems = {}
        for e in ["pe", "act", "dve", "pool", "sp"]:
            sems[("eng", e)] = es.enter_context(nc.semaphore("s_" + e))
        for k in dcnt:
            sems[("dma", k)] = es.enter_context(nc.semaphore("d_" + str(len(sems))))
        block = es.enter_context(nc.Block())
        per = {e: [] for e in ["pe", "act", "dve", "pool", "sp"]}
        for i, op in enumerate(ops):
            per[op["eng"]].append(i)

        def run(ename, eng):
            waited = {}
            for i in per[ename]:
                op = ops[i]
                need = {}
                for d in op["deps"]:
                    p = ops[d]
                    if p["dma"] is not None:
                        key = ("dma", p["dma"])
                    else:
                        if p["eng"] == "pe" and ename == "pe" and op["dma"] is None:
                            continue
                        key = ("eng", p["eng"])
                    v = p["val"]
                    if v > need.get(key, 0):
                        need[key] = v
                todo = []
                for key in sorted(need, key=str):
                    v = need[key]
                    if waited.get(key, 0) >= v:
                        continue
                    todo.append((key, v))
                    waited[key] = v
                for key, v in todo[:-1]:
                    eng.wait_ge(sems[key], v)
                ins = op["fn"](eng)
                if todo:
                    key, v = todo[-1]
                    ins.wait_op(sems[key], v, "sem-ge")
                if op["dma"] is not None:
                    ins.then_inc(sems[("dma", op["dma"])], 16)
                elif op["val"] is not None:
                    ins.then_inc(sems[("eng", ename)], 1)
            if ename == "sp":
                for k in final_keys:
                    if k in dcnt:
                        eng.wait_ge(sems[("dma", k)], dcnt[k])

        @block.tensor
        def _(e):
            run("pe", e)

        @block.scalar
        def _(e):
            run("act", e)

        @block.vector
        def _(e):
            run("dve", e)

        @block.gpsimd
        def _(e):
            run("pool", e)

        @block.sync
        def _(e):
            run("sp", e)


def build_program(debug_stop=None):
    nc = bass.Bass("TRN2", target_bir_lowering=False)
    P = Prog()
    es = ExitStack()

    def din(name, shape, dt=F32):
        return nc.dram_tensor(name, list(shape), dt, kind="ExternalInput").ap()

    def dout(name, shape):
        return nc.dram_tensor(name, list(shape), F32, kind="ExternalOutput").ap()

    xp_d = din("xp", [NCH, 128, TP])
    xs_d = din("xs", [NCH, 128, TS])
    cond_d = din("cond", [128, NCH, 2])
    modw_d = din("mod_w", [2, D, 9 * D])
    modb_d = din("mod_b", [2, 128, 72])
    ng_d = din("norm_g", [2, 128, 3, NCH])
    wi_d = [din("ffn1_wi", [2, D, 2 * DFF]), din("ffn2_wi", [2, D, 2 * DFF])]
    wo_d = [din("ffn1_wo", [2, DFF, D]), din("ffn2_wo", [2, DFF, D])]
    win_d = din("w_in", [2, D, 1536])
    wout_d = din("w_out", [2, D, D])
    poolw_d = din("pool_w", [2, 4, 64, 64])
    small_d = din("small", [2, 128, 16])
    dw_d = din("conv_dw", [2, 128, 2, 31])
    pw_d = din("conv_pw", [2, 256, 256])
    sink_d = din("sink_b", [2, 128, 8])
    ckT_d = din("cache_kT", [2, 128, PAST])
    cv_d = din("cache_v", [2, PAST, 128])
    cos_d = din("rope_cos", [128, TS])
    sin_d = din("rope_sin", [128, TS])
    cmat_d = din("cmat", [128, 5, 128])
    mask_d = din("masks", [128, 2, 128])
    edge_d = din("pool_edge", [128, 2, 2, 8])

    yp_d = dout("yp", [NCH, 128, TP])
    ys_d = dout("ys", [NCH, 128, TS])
    nk_d = dout("nk", [2, 128, TP])
    nv_d = dout("nv", [2, TP, 128])

    def sb(name, shape, dt):
        return es.enter_context(nc.sbuf_tensor(name, list(shape), dt))

    X = sb("X", [128, NCH, TS], F32)
    BIG = sb("BIG", [128, 29824], BF16)
    slots = [sb("slot0", [128, SLOT], BF16), sb("slot1", [128, SLOT], BF16)]
    tmps = [sb("tmp%d" % i, [128, 528], F32) for i in range(NTMP)]
    hgrp = sb("hgrp", [128, NCH, 512], BF16)
    cmat = sb("cmatb", [128, 5, 128], BF16)
    masks = sb("masksb", [128, 2, 128], BF16)
    esb = sb("esb", [128, 8], F32)
    sinkraw = sb("sinkraw", [128, 8], F32)
    condf = sb("condf", [128, NCH, 2], F32)
    condb = sb("condb", [128, NCH, 2], BF16)
    modv = [sb("modv%d" % l, [128, 72, 2], F32) for l in range(2)]
    modb = sb("modb", [128, 2, 72], F32)
    ngt = sb("ngt", [128, 2, 3, NCH], F32)
    AB = sb("ABt", [128, 2, 2, 3, 3, NCH], F32)
    smallt = sb("smallt", [128, 16], F32)
    dwt = sb("dwt", [128, 2, 31], F32)
    edget = sb("edget", [128, 2, 2, 8], F32)
    opc = sb("omix", [128, 4, 512], BF16)
    oattn = opc
    scr = sb("scr", [128, 2], F32)
    rv = sb("rv", [128, 1024], F32)
    ropec = rv[:, 0:512]
    ropes = rv[:, 512:1024]
    vout = rv[:, :].rearrange("p (t f) -> p t f", t=8)
    ps = es.enter_context(nc.psum_tensor("ps", [128, 8, 512], F32))

    Hv = BIG[:, 0:NCH * TS].rearrange("p (c t) -> p c t", c=NCH)
    actb = [BIG[:, NCH * TS + i * 1536:NCH * TS + (i + 1) * 1536].rearrange("p (c t) -> p c t", c=3) for i in range(2)]
    o = 0
    qst = BIG[:, o:o + 4 * TS].rearrange("p (c t) -> p c t", c=4); o += 4 * TS
    KW = TS + PAST
    kz = []
    for _ in range(2):
        kz.append(BIG[:, o:o + KW]); o += KW
    NVT = 22
    vaug = BIG[:, o:o + NVT * 256].rearrange("p (t g f) -> p t g f", t=NVT, g=2); o += NVT * 256
    TPAD = TS + 2 * PADW
    upad = BIG[:, o:o + 2 * TPAD].rearrange("p (c t) -> p c t", c=2); o += 2 * TPAD
    gpad = BIG[:, o:o + 2 * TPAD].rearrange("p (c t) -> p c t", c=2); o += 2 * TPAD
    assert o <= 29824, o

    tctr = [0]

    def tmp():
        i = tctr[0] % NTMP
        tctr[0] += 1
        return tmps[i], ("tmp", i)

    def tmpb(t):
        return t

    def mm(out, lhsT, rhs, start, stop, reads, writes):
        P.add("pe", lambda e: e.matmul(out, lhsT, rhs, start=start, stop=stop), reads, writes)

    def act(out, in_, func, reads, writes, scale=1.0, bias=0.0):
        P.add("act", lambda e: e.activation(out=out, in_=in_, func=func, scale=scale, bias=bias), reads, writes)

    def tt(out, in0, in1, op, reads, writes):
        P.add("dve", lambda e: e.tensor_tensor(out=out, in0=in0, in1=in1, op=op), reads, writes)

    def stt(out, in0, scalar, in1, op0, op1, reads, writes):
        P.add("dve", lambda e: e.scalar_tensor_tensor(out=out, in0=in0, scalar=scalar, in1=in1, op0=op0, op1=op1),
              reads, writes)

    def ts(out, in0, s1, op0, reads, writes, s2=None, op1=None):
        if op1 is None:
            P.add("dve", lambda e: e.tensor_scalar(out=out, in0=in0, scalar1=s1, scalar2=None, op0=op0), reads, writes)
        else:
            P.add("dve", lambda e: e.tensor_scalar(out=out, in0=in0, scalar1=s1, scalar2=s2, op0=op0, op1=op1),
                  reads, writes)

    def vcopy(out, in_, reads, writes):
        P.add("dve", lambda e: e.tensor_copy(out=out, in_=in_), reads, writes)

    def vmemset(ap, val, writes):
        P.add("dve", lambda e: e.memset(ap, val), (), writes)

    def dma(q, out, in_, key, reads, writes):
        P.add(q, lambda e: e.dma_start(out=out, in_=in_), reads, writes, dma=key)

    dma("pool", cmat[:], cmat_d[:, :, :], "c_cmat", (), ["cmat"])
    dma("pool", masks[:], mask_d[:, :, :], "c_mask", (), ["masks"])
    dma("sp", condf[:], cond_d[:, :, :], "c_cond", (), ["condf"])
    dma("sp", modb[:, 0, :], modb_d[0], "c_modb0", (), ["modb"])
    dma("sp", modb[:, 1, :], modb_d[1], "c_modb1", (), ["modb1"])
    dma("sp", ngt[:, 0], ng_d[0], "c_ng0", (), ["ngt"])
    dma("sp", ngt[:, 1], ng_d[1], "c_ng1", (), ["ngt1"])
    dma("sp", edget[:], edge_d[:, :, :, :], "c_edge", (), ["edget"])
    act(condb[:], condf[:], AF.Silu, ["condf"], ["condb"])
    ONES_MEAN = cmat[:, 0, :]
    BLK64 = cmat[:, 1, :]
    ONES256 = cmat[:, 2, :]
    IDENT = cmat[:, 3, :]
    PERM = cmat[:, 4, :]

    units = []

    def wi_src(l, which, f0, nf):
        v = wi_d[which][l].rearrange("(kc p) f -> p kc f", p=128)
        return v[:, :, f0 * 128:(f0 + nf) * 128], v[:, :, DFF + f0 * 128:DFF + (f0 + nf) * 128]

    for_units = []

    def plan_units():
        seq = []
        for l in range(1):
            pass
        return seq

    ucount = [0]
    ucursor = [0]
    ureleased = [0]

    def unit_views(kind, s, nf=3):
        sl = slots[s]
        if kind == "F":
            wi = sl[:, 0:8 * 2 * nf * 128].rearrange("p (k f) -> p k f", k=8)
            wo = sl[:, 6144:6144 + nf * 1024].rearrange("p (c f) -> p c f", c=nf)
            return wi, wo
        if kind == "M":
            return sl[:, 0:9216].rearrange("p (k f) -> p k f", k=8)
        if kind == "WIN":
            return sl[:, 0:6144].rearrange("p (k f) -> p k f", k=8)
        if kind == "WOUT":
            wout = sl[:, 0:8192].rearrange("p (k f) -> p k f", k=8)
            pw = sl[:, 8192:8704].rearrange("p (k f) -> p k f", k=2)
            pb = sl[:, 8704:8960].rearrange("p (k f) -> p k f", k=2)
            return wout, pw, pb
        if kind == "DIAG":
            return sl[:, 0:7936].rearrange("p (c j f) -> p c j f", c=2, j=31)
        raise ValueError(kind)

    def stoks(s):
        return [("slot", s, "a"), ("slot", s, "b")]

    def load_unit(u):
        idx, kind, l, arg = u
        s = idx % 2
        both = stoks(s)
        if kind == "F":
            which, f0, nf = arg
            wi, wo = unit_views("F", s, nf)
            g_src, u_src = wi_src(l, which, f0, nf)
            dma("pool", wi[:, :, 0:nf * 128], g_src, "u%d_a" % s, (), [("slot", s, "a")])
            dma("pool", wi[:, :, nf * 128:2 * nf * 128], u_src, "u%d_b" % s, (), [("slot", s, "a")])
            wsrc = wo_d[which][l][f0 * 128:(f0 + nf) * 128, :].rearrange("(c p) f -> p c f", p=128)
            dma("pool", wo, wsrc, "u%d_c" % s, (), [("slot", s, "b")])
        elif kind == "M":
            j = arg
            mv = unit_views("M", s)
            src = modw_d[l].rearrange("(kc p) f -> p kc f", p=128)[:, :, j * 1152:(j + 1) * 1152]
            dma("pool", mv, src, "u%d_a" % s, (), both)
        elif kind == "WIN":
            j = arg
            wv = unit_views("WIN", s)
            src = win_d[l].rearrange("(kc p) f -> p kc f", p=128)[:, :, j * 768:(j + 1) * 768]
            dma("pool", wv, src, "u%d_a" % s, (), both)
        elif kind == "WOUT":
            wout, pw, pb = unit_views("WOUT", s)
            dma("pool", wout, wout_d[l].rearrange("(kc p) f -> p kc f", p=128), "u%d_a" % s, (), both)
            dma("pool", pw, pw_d[l].rearrange("(kc p) f -> p kc f", p=128), "u%d_b" % s, (), both)
            P.add("pool", lambda e, pb=pb: e.memset(pb, 0.0), (), both)
            for gi in range(4):
                ch, half = gi // 2, gi % 2
                dma("pool", pb[64 * half:64 * half + 64, ch, 64 * half:64 * half + 64], poolw_d[l, gi],
                    "u%d_p%d" % (s, gi), (), both)
        elif kind == "DIAG":
            pass
        else:
            raise ValueError(kind)

    def _load_ready():
        while ucount[0] < len(units) and ucount[0] < ureleased[0] + 2:
            load_unit(units[ucount[0]])
            ucount[0] += 1

    def next_unit(expect_kind):
        u = units[ucursor[0]]
        assert u[1] == expect_kind, (u, expect_kind)
        _load_ready()
        assert ucount[0] > ucursor[0], ("unit not loadable yet", u, ureleased[0])
        ucursor[0] += 1
        return u[0] % 2, ("slotpair", u[0] % 2), u

    def release_unit():
        ureleased[0] += 1
        _load_ready()

    def add_units(kind_list):
        for (kind, l, arg) in kind_list:
            units.append((len(units), kind, l, arg))

    def layer_units(l):
        r = []
        f0 = 0
        for nf in FBLOCKS:
            r.append(("F", l, (0, f0, nf)))
            f0 += nf
        r += [("WIN", l, 0), ("WIN", l, 1), ("WOUT", l, None), ("DIAG", l, None)]
        f0 = 0
        for nf in FBLOCKS:
            r.append(("F", l, (1, f0, nf)))
            f0 += nf
        return r

    k0, k1 = (6, 6) if debug_stop is None else debug_stop
    plan = [("mod", 0)]
    need_mod1 = False
    if max(k0, k1) > 3:
        plan.append(("mod", 1))
    for phase, kk in ((0, k0), (1, k1)):
        if kk < 0:
            continue
        plan.append(("load", phase))
        for st in range(kk):
            l, sub = st // 3, st % 3
            if l == 1 and need_mod1:
                plan.append(("mod", 1))
                need_mod1 = False
            plan.append((("ffn1", "mixer", "ffn2")[sub], phase, l))
        plan.append(("store", phase))
    for it in plan:
        if it[0] == "mod":
            pass
        elif it[0] in ("ffn1", "ffn2"):
            f0 = 0
            for nf in FBLOCKS:
                add_units([("F", it[2], (0 if it[0] == "ffn1" else 1, f0, nf))])
                f0 += nf
        elif it[0] == "mixer":
            add_units([("WIN", it[2], 0), ("WIN", it[2], 1), ("DIAG", it[2], None), ("WOUT", it[2], None)])

    mod_q = []
    mod_loaded = [0]
    mod_done = [0]
    mod_list = []

    def mini_view(slot):
        return X[:, slot, 1024:2048].bitcast(BF16).rearrange("p (k f) -> p k f", k=8)

    def mini_toks(slot):
        return [("X", slot, 1024), ("X", slot, 1536)]

    def mod_enqueue(l):
        for m in range(36):
            mod_list.append((l, m))

    mod_cap = [12]

    def _mod_load_ahead():
        while mod_loaded[0] < len(mod_list) and mod_loaded[0] < min(mod_done[0] + 8, mod_cap[0]):
            l, m = mod_list[mod_loaded[0]]
            slot = mod_loaded[0] % 8
            src = modw_d[l].rearrange("(kc p) f -> p kc f", p=128)[:, :, m * 256:(m + 1) * 256]
            dma("pool", mini_view(slot), src, "mm%d" % slot, (), mini_toks(slot))
            mod_loaded[0] += 1

    def compute_AB(l, i):
        ng = "ngt" if l == 0 else "ngt1"
        for cj in range(2):
            sh = modv[l][:, (3 * i) * 8:(3 * i) * 8 + 8, cj]
            sc = modv[l][:, (3 * i + 1) * 8:(3 * i + 1) * 8 + 8, cj]
            gg = modv[l][:, (3 * i + 2) * 8:(3 * i + 2) * 8 + 8, cj]
            A = AB[:, l, cj, i, 0, :]
            B = AB[:, l, cj, i, 1, :]
            G = AB[:, l, cj, i, 2, :]
            stt(A, sc, 1.0, ngt[:, l, i, :], ALU.add, ALU.mult, [("modv", l), ng], [("AB", l)])
            vcopy(B, sh, [("modv", l)], [("AB", l)])
            ts(G, gg, 0.5 if i != 1 else 1.0, ALU.mult, [("modv", l)], [("AB", l)])

    def mod_pump(n):
        for _ in range(n):
            if mod_done[0] >= len(mod_list):
                return
            _mod_load_ahead()
            idx = mod_done[0]
            l, m = mod_list[idx]
            slot = idx % 8
            mv = mini_view(slot)
            mb = "modb" if l == 0 else "modb1"
            bank = 6 + (idx % 2)
            col0 = 0
            for cc in range(2):
                for k in range(NCH):
                    mm(ps[:, bank, col0 + 2 * cc:col0 + 2 * cc + 2], mv[:, k, cc * 128:(cc + 1) * 128], condb[:, k, :],
                       k == 0, k == NCH - 1, mini_toks(slot) + ["condb"], [("ps", bank)])
            for cc in range(2):
                cg = 2 * m + cc
                ts(modv[l][:, cg, :], ps[:, bank, col0 + 2 * cc:col0 + 2 * cc + 2], modb[:, l, cg:cg + 1], ALU.add,
                   [("ps", bank), mb], [("modv", l)])
            mod_done[0] += 1
            _mod_load_ahead()
            if m % 12 == 11:
                compute_AB(l, m // 12)

    def mod_ensure(l, i):
        while mod_done[0] < len(mod_list) and mod_list[mod_done[0]] <= (l, 12 * i + 11):
            mod_pump(1)

    def norm_mod(l, cj, i, t0, n, dst, dst_tok_fn):
        msb = 6
        for c in range(NCH):
            sq, sqt = tmp()
            sqv = sq.bitcast(BF16)[:, 0:n]
            act(sqv, X[:, c, t0:t0 + n], AF.Square, [("X", c, t0)], [sqt])
            mm(ps[:, msb, 0:n], ONES_MEAN, sqv, c == 0, c == NCH - 1, [sqt, "cmat"], [("ps", msb)])
        ln, lnt = tmp()
        act(ln[:, 0:n], ps[:, msb, 0:n], AF.Ln, [("ps", msb)], [lnt], bias=EPS)
        rs, rst = tmp()
        act(rs[:, 0:n], ln[:, 0:n], AF.Exp, [lnt], [rst], scale=-0.5)
        for c in range(NCH):
            t, ttok = tmp()
            stt(t[:, 0:n], X[:, c, t0:t0 + n], AB[:, l, cj, i, 0, c:c + 1], rs[:, 0:n], ALU.mult, ALU.mult,
                [("X", c, t0), ("AB", l), rst], [ttok])
            act(dst[:, c, 0:n] if dst is hgrp else dst[:, c, t0:t0 + n], t[:, 0:n], AF.Identity,
                [ttok, ("AB", l)], [dst_tok_fn(c)], bias=AB[:, l, cj, i, 1, c:c + 1])

    def ffn(l, which, cj, groups, hook=None):
        i = 0 if which == 0 else 2
        LOOK = 1
        for (t0, n) in groups[:LOOK]:
            norm_mod(l, cj, i, t0, n, Hv, lambda c, t0=t0: ("H", c, t0))
        items = []
        f0 = 0
        blk_info = []
        for bi, nf in enumerate(FBLOCKS):
            for gi, (t0, n) in enumerate(groups):
                items.append((bi, nf, f0, gi, t0, n))
            f0 += nf
        state = {"cur_blk": -1, "views": None, "tok": None}
        gu_ctr = [0]
        blk_views = {}

        def GU(it, ab):
            bi, nf, f0, gi, t0, n = it
            if bi not in blk_views:
                s, tok, u = next_unit("F")
                blk_views[bi] = (unit_views("F", s, nf), tok)
            (wi, wo), tok = blk_views[bi]
            for fc in range(nf):
                pr = gu_ctr[0] % 2
                gu_ctr[0] += 1
                bg, bu = 2 * pr, 2 * pr + 1
                for k in range(NCH):
                    mm(ps[:, bg, 0:n], wi[:, k, fc * 128:(fc + 1) * 128], Hv[:, k, t0:t0 + n], k == 0, k == NCH - 1,
                       [("slot", tok[1], "a"), ("H", k, t0)], [("ps", bg)])
                for k in range(NCH):
                    mm(ps[:, bu, 0:n], wi[:, k, (nf + fc) * 128:(nf + fc + 1) * 128], Hv[:, k, t0:t0 + n], k == 0,
                       k == NCH - 1, [("slot", tok[1], "a"), ("H", k, t0)], [("ps", bu)])
                sil, silt = tmp()
                act(sil[:, 0:n], ps[:, bg, 0:n], AF.Silu, [("ps", bg)], [silt])
                tt(actb[ab][:, fc, 0:n], ps[:, bu, 0:n], sil[:, 0:n], ALU.mult, [("ps", bu), silt], [("actb", ab, fc)])

        def WO(it, ab, last_of_block):
            bi, nf, f0, gi, t0, n = it
            (wi, wo), tok = blk_views[bi]
            for d in range(NCH):
                bo = 4 + (d % 2)
                for fc in range(nf):
                    mm(ps[:, bo, 0:n], wo[:, fc, d * 128:(d + 1) * 128], actb[ab][:, fc, 0:n], fc == 0, fc == nf - 1,
                       [("slot", tok[1], "b"), ("actb", ab, fc)], [("ps", bo)])
                stt(X[:, d, t0:t0 + n], ps[:, bo, 0:n], AB[:, l, cj, i, 2, d:d + 1], X[:, d, t0:t0 + n], ALU.mult, ALU.add,
                    [("ps", bo), ("AB", l), ("X", d, t0)], [("X", d, t0)])
            if last_of_block:
                release_unit()
                if hook is not None:
                    hook()

        ng = len(groups)
        for ii, it in enumerate(items):
            if it[0] == 0 and it[3] + LOOK < ng:
                (t0_, n_) = groups[it[3] + LOOK]
                norm_mod(l, cj, i, t0_, n_, Hv, lambda c, t0_=t0_: ("H", c, t0_))
            GU(it, ii % 2)
            if ii > 0:
                pit = items[ii - 1]
                WO(pit, (ii - 1) % 2, pit[3] == ng - 1)
        pit = items[-1]
        WO(pit, (len(items) - 1) % 2, True)

    def mixer(l, cj, groups, seqs, is_sample):
        T = sum(n for _, n in groups)
        ntiles = T // 128
        stok = "small%d" % 0

        def padcol(t):
            for si, (s0, sl) in enumerate(seqs):
                if s0 <= t < s0 + sl:
                    return t + PADW * (2 * si + 1)
            raise ValueError(t)

        dma("sp", smallt[:], small_d[l], "c_small", (), ["smallt"])
        dma("sp", dwt[:], dw_d[l], "c_dw", (), ["dwt"])
        dma("sp", sinkraw[:], sink_d[l], "c_sink", (), ["sinkraw"])
        act(esb[:], sinkraw[:], AF.Exp, ["sinkraw"], ["esb"])
        def load_cache():
            if is_sample:
                dma("pool", kz[0][0:64, TS:TS + PAST], ckT_d[l][0:64, :], "c_ck", (), [("kst", "cache")])
                dma("pool", kz[1][64:128, TS:TS + PAST], ckT_d[l][64:128, :], "c_ck1", (), [("kst", "cache")])
                cvv = cv_d[l].rearrange("(t p) f -> p t f", p=128)
                dma("pool", vaug[:, 18:22, 0, 0:64], cvv[:, :, 0:64], "c_cv0", (), [("vaug", "cache")])
                dma("pool", vaug[:, 18:22, 1, 64:128], cvv[:, :, 64:128], "c_cv1", (), [("vaug", "cache")])

        PSC = lambda c: smallt[:, c:c + 1]
        CB = lambda c: smallt[:, 2 + c:3 + c]
        CNG = lambda c: smallt[:, 4 + c:5 + c]
        QG = smallt[:, 6:7]
        KG = smallt[:, 7:8]

        sA, tokA, _ = next_unit("WIN")
        winA = unit_views("WIN", sA)
        sB, tokB, _ = next_unit("WIN")
        winB = unit_views("WIN", sB)

        def wcol(ch):
            if ch < 6:
                return winA[:, :, ch * 128:(ch + 1) * 128], tokA
            return winB[:, :, (ch - 6) * 128:(ch - 5) * 128], tokB

        HN = 264
        items = []
        for (g0, gn) in groups:
            for off in range(0, gn, 256):
                items.append((g0 + off, g0))
        all_tmp_toks = [("tmp", i) for i in range(NTMP)] + [("tmph", i, h) for i in range(NTMP) for h in range(2)]

        def tmp_barrier():
            P.add("dve", lambda e: e.memset(scr[:, :], 0.0), (), all_tmp_toks)

        tmp_barrier()
        hctr = {"n": 0, "c": 0}
        pools = {"n": [0, 1, 2], "c": [3, 4, 5, 6, 7]}

        def half(pool):
            lst = pools[pool]
            k = hctr[pool] % (2 * len(lst))
            hctr[pool] += 1
            ti, h = lst[k // 2], k % 2
            return tmps[ti][:, h * HN:h * HN + 256], tmps[ti].bitcast(BF16)[:, 2 * h * HN:2 * h * HN + 256], ("tmph", ti, h)

        def hb(i):
            return hgrp[:, :, (i % 2) * 256:(i % 2) * 256 + 256]

        def norm1(i):
            t0 = items[i][0]
            for c in range(NCH):
                _, sqv, sqt = half("n")
                act(sqv, X[:, c, t0:t0 + 256], AF.Square, [("X", c, items[i][1])], [sqt])
                mm(ps[:, 4, 0:256], ONES_MEAN, sqv, c == 0, c == NCH - 1, [sqt, "cmat"], [("ps", 4)])

        def norm2(i):
            t0 = items[i][0]
            lnv, _, lnt = half("c")
            act(lnv, ps[:, 4, 0:256], AF.Ln, [("ps", 4)], [lnt], bias=EPS)
            act(lnv, lnv, AF.Exp, [lnt], [lnt], scale=-0.5)
            for c in range(NCH):
                tv, _, ttok = half("n")
                stt(tv, X[:, c, t0:t0 + 256], AB[:, l, cj, 1, 0, c:c + 1], lnv, ALU.mult, ALU.mult,
                    [("X", c, items[i][1]), ("AB", l), lnt], [ttok])
                ts(hb(i)[:, c, :], tv, AB[:, l, cj, 1, 1, c:c + 1], ALU.add, [ttok, ("AB", l)], [("hgrp", i % 2, c)])

        ubank = [0]

        def proj(i, ch):
            b = ubank[0] % 4
            ubank[0] += 1
            wv, wt = wcol(ch)
            for k in range(NCH):
                mm(ps[:, b, 0:256], wv[:, k, :], hb(i)[:, k, :], k == 0, k == NCH - 1, [wt, ("hgrp", i % 2, k)], [("ps", b)])
            return b

        def item_body(i, mid_hook):
            t0 = items[i][0]
            a = padcol(t0)
            if is_sample:
                rc_ = ropec[:, (i % 2) * 256:(i % 2) * 256 + 256]
                rs_ = ropes[:, (i % 2) * 256:(i % 2) * 256 + 256]
                vtk = [("vout", g_) for g_ in range(8)]
                dma("sp", rc_, cos_d[:, t0:t0 + 256], "c_rc%d" % (i % 2), (), [("ropec", i % 2)] + vtk)
                dma("sp", rs_, sin_d[:, t0:t0 + 256], "c_rs%d" % (i % 2), (), [("ropes", i % 2)] + vtk)
            for ch in (0, 1):
                b = proj(i, ch)
                act(upad[:, ch, a:a + 256], ps[:, b, 0:256], AF.Copy, [("ps", b)], [("upad", ch, items[i][1])])
            mid_hook()
            for cc in (0, 1):
                bgt = proj(i, 4 + cc)
                sgv, _, sgt = half("c")
                act(sgv, ps[:, bgt, 0:256], AF.Exp, [("ps", bgt)], [sgt], scale=-1.0)
                act(sgv, sgv, AF.Ln, [sgt], [sgt], bias=1.0)
                act(sgv, sgv, AF.Exp, [sgt], [sgt], scale=-1.0)
                ba = proj(i, 2 + cc)
                tt(gpad[:, cc, a:a + 256], ps[:, ba, 0:256], sgv, ALU.mult, [("ps", ba), sgt], [("gpad", cc, items[i][1])])
            st = {}

            def stA(ch):
                b = proj(i, ch)
                _, sqv, sqt = half("c")
                act(sqv, ps[:, b, 0:256], AF.Square, [("ps", b)], [sqt])
                st[ch] = dict(b=b, sqv=sqv, sqt=sqt)

            def stB(ch):
                d = st[ch]
                mm(ps[:, 5, 0:256], BLK64, d["sqv"], True, True, [d["sqt"], "cmat"], [("ps", 5)])
                lnv, _, lnt = half("c")
                act(lnv, ps[:, 5, 0:256], AF.Ln, [("ps", 5)], [lnt], bias=EPS)
                act(lnv, lnv, AF.Exp, [lnt], [lnt], scale=-0.5)
                gsc = KG if ch == 10 else QG
                if ch == 10:
                    dstv, dtok = None, ("kst", items[i][1])
                else:
                    dstv, dtok = qst[:, ch - 6, t0:t0 + 256], ("qst", ch - 6, items[i][1])
                if not is_sample and ch != 10:
                    stt(dstv, ps[:, d["b"], 0:256], gsc, lnv, ALU.mult, ALU.mult, [("ps", d["b"]), "smallt", lnt], [dtok])
                    return
                qnv, _, qnt = half("c")
                stt(qnv, ps[:, d["b"], 0:256], gsc, lnv, ALU.mult, ALU.mult, [("ps", d["b"]), "smallt", lnt], [qnt])
                d.update(qnv=qnv, qnt=qnt, dstv=dstv, dtok=dtok)
                if not is_sample:
                    vcopy(kz[0][0:64, t0:t0 + 256], qnv[0:64, :], [qnt], [dtok])
                    vcopy(kz[1][64:128, t0:t0 + 256], qnv[64:128, :], [qnt], [dtok])
                    dma("sp", nk_d[l][:, t0:t0 + 256], qnv, "o_nk", [qnt], [])
                else:
                    _, qbv, qbt = half("c")
                    vcopy(qbv, qnv, [qnt], [qbt])
                    d.update(qbv=qbv, qbt=qbt)

            def stC(ch):
                if not is_sample:
                    return
                d = st[ch]
                mm(ps[:, 6, 0:256], PERM, d["qbv"], True, True, [d["qbt"], "cmat"], [("ps", 6)])
                tt(d["qnv"], d["qnv"], rc_, ALU.mult, [d["qnt"], ("ropec", i % 2)], [d["qnt"]])
                t2v, _, t2t = half("c")
                tt(t2v, ps[:, 6, 0:256], rs_, ALU.mult, [("ps", 6), ("ropes", i % 2)], [t2t])
                if ch == 10:
                    tt(kz[0][0:64, t0:t0 + 256], d["qnv"][0:64, :], t2v[0:64, :], ALU.add, [d["qnt"], t2t], [d["dtok"]])
                    tt(kz[1][64:128, t0:t0 + 256], d["qnv"][64:128, :], t2v[64:128, :], ALU.add, [d["qnt"], t2t], [d["dtok"]])
                else:
                    tt(d["dstv"], d["qnv"], t2v, ALU.add, [d["qnt"], t2t], [d["dtok"]])

            chs = [6, 7, 8, 9, 10]
            for s_ in range(len(chs) + 2):
                if s_ < len(chs):
                    stA(chs[s_])
                if 0 <= s_ - 1 < len(chs):
                    stB(chs[s_ - 1])
                if 0 <= s_ - 2 < len(chs):
                    stC(chs[s_ - 2])
            wv, wt = wcol(11)
            for tl in range(2):
                for k in range(NCH):
                    mm(ps[:, 7, tl * 128:(tl + 1) * 128], hb(i)[:, k, tl * 128:(tl + 1) * 128], wv[:, k, :], k == 0,
                       k == NCH - 1, [wt, ("hgrp", i % 2, k)], [("ps", 7)])
            for tl in range(2):
                gt = (t0 // 128) + tl
                act(vaug[:, gt, 0, 0:64], ps[:, 7, tl * 128:tl * 128 + 64], AF.Copy, [("ps", 7)], [("vaug", gt)])
                act(vaug[:, gt, 1, 64:128], ps[:, 7, tl * 128 + 64:tl * 128 + 128], AF.Copy, [("ps", 7)], [("vaug", gt)])
                if not is_sample:
                    vcopy(vout[:, gt, :], ps[:, 7, tl * 128:(tl + 1) * 128], [("ps", 7)], [("vout", gt)])

        norm1(0)
        norm2(0)
        init_big(is_sample, T)
        load_cache()
        for i in range(len(items)):
            if i + 1 < len(items):
                norm1(i + 1)
                item_body(i, lambda i=i: norm2(i + 1))
            else:
                item_body(i, lambda: None)
        tmp_barrier()
        release_unit()
        release_unit()
        if not is_sample:
            dma("sp", nv_d[l].rearrange("(t p) f -> p t f", p=128), vout, "o_nv",
                [("vout", gt) for gt in range(8)], [])

        sD, tokD, _ = next_unit("DIAG")
        diag = unit_views("DIAG", sD)
        sW, tokW, _ = next_unit("WOUT")
        wout, pwv, pbv = unit_views("WOUT", sW)
        for cc in range(2):
            for j in range(31):
                act(diag[:, cc, j, :], IDENT, AF.Copy, ["cmat", "dwt"], [tokD], scale=dwt[:, cc, j:j + 1])

        def wout_part(kc0, src, src_tokfn, t0, n):
            for d in range(NCH):
                bo = 4 + (d % 2)
                for kk in range(4):
                    mm(ps[:, bo, 0:n], wout[:, kc0 + kk, d * 128:(d + 1) * 128], src[:, kk, 0:n], kk == 0, kk == 3,
                       [tokW, src_tokfn(kk)], [("ps", bo)])
                stt(X[:, d, t0:t0 + n], ps[:, bo, 0:n], AB[:, l, cj, 1, 2, d:d + 1], X[:, d, t0:t0 + n], ALU.mult, ALU.add,
                    [("ps", bo), ("AB", l), ("X", d, t0)], [("X", d, t0)])

        allsegs = []
        for (t0, n) in groups:
            t = t0
            while t < t0 + n:
                for (s0, sl) in seqs:
                    if s0 <= t < s0 + sl:
                        e = min(t0 + n, s0 + sl)
                        allsegs.append(dict(st=t, sn=e - t, at_start=(t == s0), at_end=(e == s0 + sl), t0=t0, n=n,
                                            last=(e == t0 + n)))
                        t = e
                        break
        gr_all = {cc: [("gpad", cc, g0) for (g0, _) in groups] for cc in (0, 1)}
        ur_all = {ch: [("upad", ch, g0) for (g0, _) in groups] for ch in (0, 1)}

        def conv_mms(si):
            sg_ = allsegs[si]
            a, sn = padcol(sg_["st"]), sg_["sn"]
            for cc in (0, 1):
                b = 2 * (si % 2) + cc
                for j in range(31):
                    mm(ps[:, b, 0:sn], diag[:, cc, j, :], gpad[:, cc, a + j - 15:a + j - 15 + sn], j == 0, j == 30,
                       [tokD] + gr_all[cc], [("ps", b)])

        def pool_dve(si):
            sg_ = allsegs[si]
            a, sn, at_start, at_end = padcol(sg_["st"]), sg_["sn"], sg_["at_start"], sg_["at_end"]
            outs = []
            for ch in (0, 1):
                ur = ur_all[ch]
                A_, At = tmp()
                tt(A_[:, 0:sn + 14], upad[:, ch, a - 8:a + sn + 6], upad[:, ch, a - 7:a + sn + 7], ALU.add, ur, [At])
                B_, Bt = tmp()
                tt(B_[:, 0:sn + 12], A_[:, 0:sn + 12], A_[:, 2:sn + 14], ALU.add, [At], [Bt])
                if ch == 0:
                    lo_src, lo_off, lo_w = A_, 7, 2
                    hi_src, hi_off, hi_w = B_, 6, 4
                    lot, hit = At, Bt
                else:
                    C_, Ct = tmp()
                    tt(C_[:, 0:sn + 8], B_[:, 0:sn + 8], B_[:, 4:sn + 12], ALU.add, [Bt], [Ct])
                    D_, Dt = tmp()
                    tt(D_[64:128, 0:sn], C_[64:128, 0:sn], C_[64:128, 8:sn + 8], ALU.add, [Ct], [Dt])
                    lo_src, lo_off, lo_w = C_, 4, 8
                    hi_src, hi_off, hi_w = D_, 0, 16
                    lot, hit = Ct, Dt
                mean, mt = tmp()
                ts(mean[0:64, 0:sn], lo_src[0:64, lo_off:lo_off + sn], 1.0 / lo_w, ALU.mult, [lot], [mt])
                ts(mean[64:128, 0:sn], hi_src[64:128, hi_off:hi_off + sn], 1.0 / hi_w, ALU.mult, [hit], [mt])
                if at_start:
                    tt(mean[0:64, 0:8], lo_src[0:64, lo_off:lo_off + 8], edget[0:64, ch, 0, :], ALU.mult,
                       [lot, "edget"], [mt])
                    tt(mean[64:128, 0:8], hi_src[64:128, hi_off:hi_off + 8], edget[64:128, ch, 0, :], ALU.mult,
                       [hit, "edget"], [mt])
                if at_end:
                    tt(mean[0:64, sn - 8:sn], lo_src[0:64, lo_off + sn - 8:lo_off + sn], edget[0:64, ch, 1, :],
                       ALU.mult, [lot, "edget"], [mt])
                    tt(mean[64:128, sn - 8:sn], hi_src[64:128, hi_off + sn - 8:hi_off + sn], edget[64:128, ch, 1, :],
                       ALU.mult, [hit, "edget"], [mt])
                plv = A_.bitcast(BF16)[:, 0:sn]
                tt(plv, mean[:, 0:sn], upad[:, ch, a:a + sn], ALU.subtract, [mt] + ur, [At])
                outs.append((plv, At, tctr[0]))
            return outs

        def pool_mm(si, outs):
            sg_ = allsegs[si]
            sn, off = sg_["sn"], sg_["st"] - sg_["t0"]
            for ch in (0, 1):
                plv, plt, ser = outs[ch]
                assert tctr[0] - ser < NTMP - 1, "tmp ring wrapped (pool)"
                b = 6 + ch
                mm(ps[:, b, 0:sn], pbv[:, ch, :], plv, True, True, [tokW, plt], [("ps", b)])
                act(opc[:, ch, off:off + sn], ps[:, b, 0:sn], AF.Copy, [("ps", b), "smallt"], [("omix", ch)],
                    scale=PSC(ch))

        def conv_post(si):
            sg_ = allsegs[si]
            sn, off = sg_["sn"], sg_["st"] - sg_["t0"]
            ybs = []
            for cc in (0, 1):
                b = 2 * (si % 2) + cc
                yb, ybt = tmp()
                act(yb[:, 0:sn], ps[:, b, 0:sn], AF.Identity, [("ps", b), "smallt"], [ybt], bias=CB(cc))
                sq, sqt = tmp()
                sqv = sq.bitcast(BF16)[:, 0:sn]
                act(sqv, yb[:, 0:sn], AF.Square, [ybt], [sqt])
                mm(ps[:, 6, 0:sn], ONES256, sqv, cc == 0, cc == 1, [sqt, "cmat"], [("ps", 6)])
                ybs.append((yb, ybt))
            ln, lnt = tmp()
            act(ln[:, 0:sn], ps[:, 6, 0:sn], AF.Ln, [("ps", 6)], [lnt], bias=EPS)
            act(ln[:, 0:sn], ln[:, 0:sn], AF.Exp, [lnt], [lnt], scale=-0.5)
            zs = []
            for cc in (0, 1):
                yb, ybt = ybs[cc]
                stt(yb[:, 0:sn], yb[:, 0:sn], CNG(cc), ln[:, 0:sn], ALU.mult, ALU.mult, [ybt, "smallt", lnt], [ybt])
                zb, zbt = tmp()
                zbv = zb.bitcast(BF16)[:, 0:sn]
                act(zbv, yb[:, 0:sn], AF.Silu, [ybt], [zbt])
                zs.append((zbv, zbt))
            for co in (0, 1):
                b = 6 + co
                for ci in (0, 1):
                    mm(ps[:, b, 0:sn], pwv[:, ci, co * 128:(co + 1) * 128], zs[ci][0], ci == 0, ci == 1,
                       [tokW, zs[ci][1]], [("ps", b)])
                act(opc[:, 2 + co, off:off + sn], ps[:, b, 0:sn], AF.Copy, [("ps", b)], [("omix", 2 + co)])

        conv_mms(0)
        for si in range(len(allsegs)):
            outs = pool_dve(si)
            if si + 1 < len(allsegs):
                conv_mms(si + 1)
            pool_mm(si, outs)
            conv_post(si)
            if allsegs[si]["last"]:
                wout_part(0, opc, lambda kk: ("omix", kk), allsegs[si]["t0"], allsegs[si]["n"])

        release_unit()
        sbank = [0]
        kall = [("kst", g0) for (g0, _) in groups] + ([("kst", "cache")] if is_sample else [])
        allsteps = []
        ginfo = []
        for (t0, n) in groups:
            qall = [("qst", c, t0) for c in range(4)]
            first = len(allsteps)
            ntile0 = None
            for tl in range(n // 128):
                gt = t0 // 128 + tl
                q0 = t0 + tl * 128
                chunks = []
                if is_sample:
                    if gt > 0:
                        chunks.append((q0 - 128, gt - 1, 0))
                    chunks.append((q0, gt, None))
                    if gt < ntiles - 1:
                        chunks.append((q0 + 128, gt + 1, 1))
                    for cti in range(4):
                        chunks.append((TS + cti * 128, 18 + cti, None))
                else:
                    for (s0, sl) in seqs:
                        if s0 <= q0 < s0 + sl:
                            for kt in range(sl // 128):
                                chunks.append((s0 + kt * 128, (s0 // 128) + kt, None))
                for kvh in range(2):
                    for ci, ch in enumerate(chunks):
                        allsteps.append((tl, gt, q0, ci, len(chunks), ch, kvh, t0, qall))
                if ntile0 is None:
                    ntile0 = len(allsteps) - first
            ginfo.append((first, ntile0, t0, n, len(allsteps) - 1))

        def qk(step):
            tl, gt, q0, ci, nci, (kc0, vt, mk), kvh, t0, qall = step
            b = sbank[0] % 4
            sbank[0] += 1
            for hh in range(4):
                mm(ps[:, b, hh * 128:(hh + 1) * 128], kz[kvh][:, kc0:kc0 + 128],
                   qst[:, hh, q0:q0 + 128], True, True, kall + qall, [("ps", b)])
            pt, ptt = tmp()
            ptv = pt.bitcast(BF16)[:, 0:512]
            act(ptv, ps[:, b, :], AF.Exp, [("ps", b)], [ptt], scale=0.125)
            if mk is not None:
                pv4 = ptv.rearrange("p (h q) -> p h q", h=4)
                tt(pv4, pv4, masks[:, mk, :].unsqueeze(1).broadcast_to([128, 4, 128]), ALU.mult, [ptt, "masks"], [ptt])
            return (ptv, ptt, tctr[0])

        def pv(step, pts):
            tl, gt, q0, ci, nci, (kc0, vt, mk), kvh, t0, qall = step
            pb = 4 + 2 * (gt % 2)
            assert tctr[0] - pts[2] < NTMP, "tmp ring wrapped"
            vtok = ("vaug", vt) if vt < 18 else ("vaug", "cache")
            mm(ps[:, pb + kvh, :], vaug[:, vt, kvh, :], pts[0], ci == 0, ci == nci - 1,
               [vtok, pts[1]], [("ps", pb + kvh)])
            if ci == nci - 1:
                dlo, nlo = (64, 0) if kvh == 0 else (0, 64)
                rc2, rc2t = tmp()
                if kvh == 0:
                    ln, lnt = tmp()
                    for hh in range(4):
                        act(ln[dlo:dlo + 64, hh * 128:(hh + 1) * 128], ps[dlo:dlo + 64, pb + kvh, hh * 128:(hh + 1) * 128],
                            AF.Ln, [("ps", pb + kvh), "esb"], [lnt], bias=esb[dlo:dlo + 64, 4 * kvh + hh:4 * kvh + hh + 1])
                    act(ln[dlo:dlo + 64, 0:512], ln[dlo:dlo + 64, 0:512], AF.Exp, [lnt], [lnt], scale=-1.0)
                    vcopy(rc2[nlo:nlo + 64, 0:512], ln[dlo:dlo + 64, 0:512], [lnt], [rc2t])
                else:
                    for hh in range(4):
                        ts(rc2[nlo:nlo + 64, hh * 128:(hh + 1) * 128], ps[dlo:dlo + 64, pb + kvh, hh * 128:(hh + 1) * 128],
                           esb[dlo:dlo + 64, 4 * kvh + hh:4 * kvh + hh + 1], ALU.add, [("ps", pb + kvh), "esb"], [rc2t])
                    P.add("dve", lambda e, o_=rc2[nlo:nlo + 64, 0:512]: e.reciprocal(out=o_, in_=o_), [rc2t], [rc2t])
                tt(oattn[nlo:nlo + 64, :, tl * 128:(tl + 1) * 128],
                   ps[nlo:nlo + 64, pb + kvh, :].rearrange("p (h q) -> p h q", h=4),
                   rc2[nlo:nlo + 64, 0:512].rearrange("p (h q) -> p h q", h=4), ALU.mult,
                   [("ps", pb + kvh), rc2t], [("omix", kk) for kk in range(4)])


        def wout_attn(t0, n):
            for d in range(NCH):
                bo = 6 + (d % 2)
                for kk in range(4):
                    mm(ps[:, bo, 0:n], wout[:, 4 + kk, d * 128:(d + 1) * 128], oattn[:, kk, 0:n], kk == 0, kk == 3,
                       [tokW, ("omix", kk)], [("ps", bo)])
                stt(X[:, d, t0:t0 + n], ps[:, bo, 0:n], AB[:, l, cj, 1, 2, d:d + 1], X[:, d, t0:t0 + n], ALU.mult, ALU.add,
                    [("ps", bo), ("AB", l), ("X", d, t0)], [("X", d, t0)])

        DEPTH = 3
        pend = []
        due = []
        for si, step in enumerate(allsteps):
            while due and due[0][0] <= si:
                _, t0_, n_ = due.pop(0)
                wout_attn(t0_, n_)
            pend.append((si, step, qk(step)))
            if len(pend) > DEPTH:
                pi, pstep, ppts = pend.pop(0)
                pv(pstep, ppts)
                for gi, (first, ntile0, t0_, n_, last) in enumerate(ginfo):
                    if pi == last:
                        if gi + 1 < len(ginfo):
                            nfirst, nnt0 = ginfo[gi + 1][0], ginfo[gi + 1][1]
                            due.append((nfirst + min(DEPTH + 4, nnt0 // 2 - 1 + DEPTH), t0_, n_))
                        else:
                            due.append((10 ** 9, t0_, n_))
        while pend:
            pi, pstep, ppts = pend.pop(0)
            pv(pstep, ppts)
            for gi, (first, ntile0, t0_, n_, last) in enumerate(ginfo):
                if pi == last:
                    due.append((10 ** 9, t0_, n_))
        for (_, t0_, n_) in due:
            wout_attn(t0_, n_)
        release_unit()

    def mkgroups(T):
        g = []
        t = 0
        while t < T:
            n = min(512, T - t)
            g.append((t, n))
            t += n
        return g

    def big_switch():
        P.add("dve", lambda e: e.memset(scr[:, :], 0.0), (), ["BIG"])

    def init_big(is_sample, T):
        big_switch()
        seqs_ = [(0, TS)] if is_sample else [(s_ * 256, 256) for s_ in range(4)]
        ut = [("upad", ch, g0) for ch in (0, 1) for (g0, _) in mkgroups(T)]
        gt_ = [("gpad", ch, g0) for ch in (0, 1) for (g0, _) in mkgroups(T)]
        for si, (s0, sl) in enumerate(seqs_):
            a = s0 + PADW * (2 * si + 1)
            for (c0, c1) in ((a - PADW, a), (a + sl, a + sl + PADW)):
                vmemset(upad[:, :, c0:c1], 0.0, ut)
                vmemset(gpad[:, :, c0:c1], 0.0, gt_)
        vt = [("vaug", t) for t in range(18)] + [("vaug", "cache")]
        vmemset(vaug[:, :, 0, 64:128], 1.0, vt)
        vmemset(vaug[:, :, 1, 0:64], 1.0, vt)
        ktoks = [("kst", g0) for (g0, _) in mkgroups(T)] + [("kst", "cache")]
        vmemset(kz[0][64:128, :], 0.0, ktoks)
        vmemset(kz[1][0:64, :], 0.0, ktoks)

    for it in plan:
        if it[0] == "mod":
            mod_enqueue(it[1])
            if it[1] == 0:
                mod_pump(12)
                _load_ready()
                mod_cap[0] = 10 ** 9
            continue
        phase = it[1]
        is_sample = phase == 1
        T = TS if is_sample else TP
        xin = xs_d if is_sample else xp_d
        yout = ys_d if is_sample else yp_d
        groups = mkgroups(T)
        seqs = [(0, TS)] if is_sample else [(s * 256, 256) for s in range(4)]
        if it[0] == "load":
            if is_sample:
                mod_ensure(1, 2)
            for c in range(NCH):
                dma("sp", X[:, c, 0:T], xin[c], "x_in%d" % c, (), [("X", c, g0) for (g0, _) in groups])
        elif it[0] == "store":
            for c in range(NCH):
                dma("sp", yout[c], X[:, c, 0:T], "y_out%d" % c, [("X", c, g0) for (g0, _) in groups], [])
        elif it[0] == "ffn1":
            mod_ensure(it[2], 0)
            ffn(it[2], 0, phase, groups, hook=lambda: mod_pump(5))
        elif it[0] == "ffn2":
            mod_ensure(it[2], 2)
            ffn(it[2], 1, phase, groups, hook=lambda: mod_pump(5))
        elif it[0] == "mixer":
            mod_ensure(it[2], 1)
            mixer(it[2], phase, groups, seqs, is_sample)
            big_switch()

    final_keys = ["y_out%d" % c for c in range(NCH)] + ["o_nk", "o_nv"]
    P.emit(nc, es, final_keys)
    es.close()
    nc._prog_stats = P.stats
    return nc


_NC_CACHE = {}


def _head_perm():
    cols = []
    for j in range(4):
        cols += list(range(j * 64, j * 64 + 64))
        cols += list(range((4 + j) * 64, (4 + j) * 64 + 64))
    return np.array(cols)


def _consts():
    cm = np.zeros((128, 5, 128), np.float32)
    cm[:, 0, :] = 1.0 / 1024
    for b in range(2):
        cm[64 * b:64 * b + 64, 1, 64 * b:64 * b + 64] = 1.0 / 64
    cm[:, 2, :] = 1.0 / 256
    cm[:, 3, :] = np.eye(128, dtype=np.float32)
    for m in range(128):
        d = m % 32
        partner = m + 16 if d < 16 else m - 16
        cm[partner, 4, m] = 1.0
    k = np.arange(128)[:, None]
    q = np.arange(128)[None, :]
    mprev = (k >= q).astype(np.float32)
    mnext = (k <= q).astype(np.float32)
    masks = np.stack([mprev, mnext], axis=1)
    sinkl = np.zeros((1, 2, 128), np.float32)
    sinkl[0, 0, 64:128] = 1.0
    sinkl[0, 1, 0:64] = 1.0
    edge = np.zeros((128, 2, 2, 8), np.float32)
    wins = {(0, 0): 2, (0, 1): 4, (1, 0): 8, (1, 1): 16}
    for (ch, half), w in wins.items():
        for i in range(8):
            cs = (i + w // 2) - max(i - w // 2, 0)
            r = 8 - i
            ce = min(w // 2, r) + w // 2
            edge[64 * half:64 * half + 64, ch, 0, i] = 1.0 / cs
            edge[64 * half:64 * half + 64, ch, 1, i] = 1.0 / ce
    return cm, masks, sinkl, edge


def _rope_tables(pos0):
    pos = pos0 + np.arange(TS)
    row = (pos // 64).astype(np.float64)
    col = (pos % 64).astype(np.float64)
    half = 32
    inv = 10000.0 ** (-np.arange(0, half, 2, dtype=np.float64) / half)
    cos = np.zeros((128, TS), np.float32)
    sin = np.zeros((128, TS), np.float32)
    for p in range(128):
        d = p % 64
        posv = row if d < 32 else col
        dd = d % 32
        i = dd % 16
        ang = posv * inv[i]
        cos[p] = np.cos(ang)
        sin[p] = np.sin(ang) * (-1.0 if dd < 16 else 1.0)
    return cos, sin


def kernel(x_prompt, x_sample, cache_k, cache_v, c, c_ctx, mod_w, mod_b, norm_g,
           ffn1_wi, ffn1_wo, ffn2_wi, ffn2_wo, w_in, w_out, pool_w, pool_scale,
           conv_dw, conv_b, conv_norm_g, conv_pw, q_norm_g, k_norm_g, sink, _only_core=None, _debug_stop=None):
    f = lambda a: np.ascontiguousarray(np.asarray(a, dtype=np.float32))
    x_prompt, x_sample, cache_k, cache_v = f(x_prompt), f(x_sample), f(cache_k), f(cache_v)
    if "nc" not in _NC_CACHE:
        _NC_CACHE["nc"] = build_program()
    nc = _NC_CACHE["nc"]
    hp = _head_perm()
    w_in_p = f(w_in).copy()
    w_in_p[:, :, 768:1280] = f(w_in)[:, :, 768 + hp]
    w_out_p = f(w_out).copy()
    w_out_p[:, 512:1024, :] = f(w_out)[:, 512 + hp, :]
    modb_l = f(np.asarray(mod_b).reshape(2, 72, 128).transpose(0, 2, 1))
    ng_l = f(np.asarray(norm_g).reshape(2, 3, NCH, 128).transpose(0, 3, 1, 2))
    small = np.zeros((2, 128, 16), np.float32)
    small[:, :, 0:2] = np.asarray(pool_scale).reshape(2, 2, 128).transpose(0, 2, 1)
    small[:, :, 2:4] = np.asarray(conv_b).reshape(2, 2, 128).transpose(0, 2, 1)
    small[:, :, 4:6] = np.asarray(conv_norm_g).reshape(2, 2, 128).transpose(0, 2, 1)
    small[:, :, 6] = np.tile(np.asarray(q_norm_g), (1, 2))
    small[:, :, 7] = np.tile(np.asarray(k_norm_g), (1, 2))
    dw_l = f(np.asarray(conv_dw).reshape(2, 31, 2, 128).transpose(0, 3, 2, 1))
    sk = np.asarray(sink, dtype=np.float32)
    sink_b = f(np.broadcast_to(sk[:, None, :], (2, 128, 8)))
    cm, masks, sinkl, edge = _consts()
    shared = dict(mod_w=f(mod_w), mod_b=modb_l, norm_g=ng_l, ffn1_wi=f(ffn1_wi), ffn2_wi=f(ffn2_wi),
                  ffn1_wo=f(ffn1_wo), ffn2_wo=f(ffn2_wo), w_in=w_in_p, w_out=w_out_p, pool_w=f(pool_w),
                  small=small, conv_dw=dw_l, conv_pw=f(conv_pw), sink_b=sink_b, cmat=cm, masks=masks,
                  pool_edge=edge)
    in_maps = []
    starts = []
    for core in (range(8) if _only_core is None else [_only_core]):
        b, hf = core // 2, core % 2
        s0 = 0 if hf == 0 else 4096 - TS
        starts.append(s0)
        xp = x_prompt[4 * core:4 * core + 4].reshape(TP, D).T.reshape(NCH, 128, TP)
        xs = x_sample[b, s0:s0 + TS].T.reshape(NCH, 128, TS)
        cond = np.stack([np.asarray(c_ctx, np.float32), np.asarray(c, np.float32)[b]], axis=1)
        cond = cond.reshape(NCH, 128, 2).transpose(1, 0, 2)
        ckT = cache_k[b].reshape(2, PAST, 128).transpose(0, 2, 1)
        cv = cache_v[b].reshape(2, PAST, 128)
        cos, sin = _rope_tables(s0)
        m = dict(shared)
        m.update(xp=f(xp), xs=f(xs), cond=f(cond), cache_kT=f(ckT), cache_v=f(cv), rope_cos=cos, rope_sin=sin)
        in_maps.append(m)
    if _only_core is not None:
        nc = build_program(_debug_stop)
        res = run_bass_kernel_spmd(nc, in_maps, core_ids=[0])
        print("EXEC_NS", res.exec_time_ns, nc._prog_stats)
        return res.results[0]
    res = run_bass_kernel_spmd(nc, in_maps, core_ids=list(range(8)))
    y_prompt = np.zeros((32, 256, D), np.float32)
    y_sample = np.zeros((4, 4096, D), np.float32)
    nk = np.zeros((32, 2, 256, 2, 64), np.float32)
    nv = np.zeros((32, 2, 256, 2, 64), np.float32)
    for core in range(8):
        r = res.results[core]
        b, hf = core // 2, core % 2
        yp = np.asarray(r["yp"]).reshape(D, TP).T.reshape(4, 256, D)
        y_prompt[4 * core:4 * core + 4] = yp
        ys = np.asarray(r["ys"]).reshape(D, TS).T
        if hf == 0:
            y_sample[b, 0:2048] = ys[0:2048]
        else:
            y_sample[b, 2048:4096] = ys[TS - 2048:TS]
        k_ = np.asarray(r["nk"]).transpose(0, 2, 1).reshape(2, 4, 256, 2, 64)
        v_ = np.asarray(r["nv"]).reshape(2, 4, 256, 2, 64)
        nk[4 * core:4 * core + 4] = k_.transpose(1, 0, 2, 3, 4)
        nv[4 * core:4 * core + 4] = v_.transpose(1, 0, 2, 3, 4)
    return (y_prompt, y_sample, nk, nv)
```

```python
import numpy as np
from contextlib import ExitStack
import concourse.bass as bass
import concourse.mybir as mybir
from concourse.bass_utils import run_bass_kernel_spmd

F32 = mybir.dt.float32
BF16 = mybir.dt.bfloat16
AF = mybir.ActivationFunctionType
ALU = mybir.AluOpType

D = 1024
NCH = 8
DFF = 2816
NFC = 22
TP = 1024
TS = 2304
PAST = 512
PADW = 16
EPS = 1e-6
SLOT = 9216
FBLOCKS = [3, 3, 3, 3, 3, 3, 3, 1]
NTMP = 8


class Prog:
    def __init__(self):
        self.ops = []
        self.lastw = {}
        self.readers = {}
        self.ambient = True

    def add(self, eng, fn, reads=(), writes=(), dma=None):
        i = len(self.ops)
        def _exp(lst):
            o = []
            for t in lst:
                if isinstance(t, tuple) and len(t) == 2 and t[0] == "slotpair":
                    o += [("slot", t[1], "a"), ("slot", t[1], "b")]
                else:
                    o.append(t)
            return o
        reads = _exp(reads)
        writes = _exp(writes)
        if self.ambient and eng in ("pe", "act", "dve") and dma is None and "BIG" not in writes:
            reads.append("BIG")
        deps = set()
        for t in reads:
            w = self.lastw.get(t)
            if w is not None:
                deps.add(w)
        for t in writes:
            w = self.lastw.get(t)
            if w is not None:
                deps.add(w)
            deps.update(self.readers.get(t, ()))
        red = {}
        for d in deps:
            p = self.ops[d]
            k = ("dma", p["dma"]) if p["dma"] is not None else ("eng", p["eng"])
            if k not in red or d > red[k]:
                red[k] = d
        deps = set(red.values())
        self.ops.append(dict(eng=eng, fn=fn, deps=deps, dma=dma, val=None))
        for t in reads:
            self.readers.setdefault(t, []).append(i)
        for t in writes:
            self.lastw[t] = i
            self.readers[t] = []
        return i

    def emit(self, nc, es, final_keys):
        ops = self.ops
        needed = set()
        for op in ops:
            for d in op["deps"]:
                p = ops[d]
                if p["dma"] is None:
                    if p["eng"] == "pe" and op["eng"] == "pe" and op["dma"] is None:
                        continue
                    needed.add(d)
        cnt = {}
        dcnt = {}
        for i, op in enumerate(ops):
            if op["dma"] is not None:
                dcnt[op["dma"]] = dcnt.get(op["dma"], 0) + 16
                op["val"] = dcnt[op["dma"]]
            elif i in needed:
                cnt[op["eng"]] = cnt.get(op["eng"], 0) + 1
                op["val"] = cnt[op["eng"]]
        self.stats = dict(cnt=dict(cnt), dmax=max(dcnt.values()), nops=len(ops), ndma=len(dcnt))
        sems = {}
        for e in ["pe", "act", "dve", "pool", "sp"]:
            sems[("eng", e)] = es.enter_context(nc.semaphore("s_" + e))
        for k in dcnt:
            sems[("dma", k)] = es.enter_context(nc.semaphore("d_" + str(len(sems))))
        block = es.enter_context(nc.Block())
        per = {e: [] for e in ["pe", "act", "dve", "pool", "sp"]}
        for i, op in enumerate(ops):
            per[op["eng"]].append(i)

        def run(ename, eng):
            waited = {}
            for i in per[ename]:
                op = ops[i]
                need = {}
                for d in op["deps"]:
                    p = ops[d]
                    if p["dma"] is not None:
                        key = ("dma", p["dma"])
                    else:
                        if p["eng"] == "pe" and ename == "pe" and op["dma"] is None:
                            continue
                        key = ("eng", p["eng"])
                    v = p["val"]
                    if v > need.get(key, 0):
                        need[key] = v
                todo = []
                for key in sorted(need, key=str):
                    v = need[key]
                    if waited.get(key, 0) >= v:
                        continue
                    todo.append((key, v))
                    waited[key] = v
                for key, v in todo[:-1]:
                    eng.wait_ge(sems[key], v)
                ins = op["fn"](eng)
                if todo:
                    key, v = todo[-1]
                    ins.wait_op(sems[key], v, "sem-ge")
                if op["dma"] is not None:
                    ins.then_inc(sems[("dma", op["dma"])], 16)
                elif op["val"] is not None:
                    ins.then_inc(sems[("eng", ename)], 1)
            if ename == "sp":
                for k in final_keys:
                    if k in dcnt:
                        eng.wait_ge(sems[("dma", k)], dcnt[k])

        @block.tensor
        def _(e):
            run("pe", e)

        @block.scalar
        def _(e):
            run("act", e)

        @block.vector
        def _(e):
            run("dve", e)

        @block.gpsimd
        def _(e):
            run("pool", e)

        @block.sync
        def _(e):
            run("sp", e)


def build_program(debug_stop=None):
    nc = bass.Bass("TRN2", target_bir_lowering=False)
    P = Prog()
    es = ExitStack()

    def din(name, shape, dt=F32):
        return nc.dram_tensor(name, list(shape), dt, kind="ExternalInput").ap()

    def dout(name, shape):
        return nc.dram_tensor(name, list(shape), F32, kind="ExternalOutput").ap()

    xp_d = din("xp", [NCH, 128, TP])
    xs_d = din("xs", [NCH, 128, TS])
    cond_d = din("cond", [128, NCH, 2])
    modw_d = din("mod_w", [2, D, 9 * D])
    modb_d = din("mod_b", [2, 128, 72])
    ng_d = din("norm_g", [2, 128, 3, NCH])
    wi_d = [din("ffn1_wi", [2, D, 2 * DFF]), din("ffn2_wi", [2, D, 2 * DFF])]
    wo_d = [din("ffn1_wo", [2, DFF, D]), din("ffn2_wo", [2, DFF, D])]
    win_d = din("w_in", [2, D, 1536])
    wout_d = din("w_out", [2, D, D])
    poolw_d = din("pool_w", [2, 4, 64, 64])
    small_d = din("small", [2, 128, 16])
    dw_d = din("conv_dw", [2, 128, 2, 31])
    pw_d = din("conv_pw", [2, 256, 256])
    sink_d = din("sink_b", [2, 128, 8])
    ckT_d = din("cache_kT", [2, 128, PAST])
    cv_d = din("cache_v", [2, PAST, 128])
    cos_d = din("rope_cos", [128, TS])
    sin_d = din("rope_sin", [128, TS])
    cmat_d = din("cmat", [128, 5, 128])
    mask_d = din("masks", [128, 2, 128])
    edge_d = din("pool_edge", [128, 2, 2, 8])

    yp_d = dout("yp", [NCH, 128, TP])
    ys_d = dout("ys", [NCH, 128, TS])
    nk_d = dout("nk", [2, 128, TP])
    nv_d = dout("nv", [2, TP, 128])

    def sb(name, shape, dt):
        return es.enter_context(nc.sbuf_tensor(name, list(shape), dt))

    X = sb("X", [128, NCH, TS], F32)
    BIG = sb("BIG", [128, 29824], BF16)
    slots = [sb("slot0", [128, SLOT], BF16), sb("slot1", [128, SLOT], BF16)]
    tmps = [sb("tmp%d" % i, [128, 528], F32) for i in range(NTMP)]
    hgrp = sb("hgrp", [128, NCH, 512], BF16)
    cmat = sb("cmatb", [128, 5, 128], BF16)
    masks = sb("masksb", [128, 2, 128], BF16)
    esb = sb("esb", [128, 8], F32)
    sinkraw = sb("sinkraw", [128, 8], F32)
    condf = sb("condf", [128, NCH, 2], F32)
    condb = sb("condb", [128, NCH, 2], BF16)
    modv = [sb("modv%d" % l, [128, 72, 2], F32) for l in range(2)]
    modb = sb("modb", [128, 2, 72], F32)
    ngt = sb("ngt", [128, 2, 3, NCH], F32)
    AB = sb("ABt", [128, 2, 2, 3, 3, NCH], F32)
    smallt = sb("smallt", [128, 16], F32)
    dwt = sb("dwt", [128, 2, 31], F32)
    edget = sb("edget", [128, 2, 2, 8], F32)
    opc = sb("omix", [128, 4, 512], BF16)
    oattn = opc
    scr = sb("scr", [128, 2], F32)
    rv = sb("rv", [128, 1024], F32)
    ropec = rv[:, 0:512]
    ropes = rv[:, 512:1024]
    vout = rv[:, :].rearrange("p (t f) -> p t f", t=8)
    ps = es.enter_context(nc.psum_tensor("ps", [128, 8, 512], F32))

    Hv = BIG[:, 0:NCH * TS].rearrange("p (c t) -> p c t", c=NCH)
    actb = [BIG[:, NCH * TS + i * 1536:NCH * TS + (i + 1) * 1536].rearrange("p (c t) -> p c t", c=3) for i in range(2)]
    o = 0
    qst = BIG[:, o:o + 4 * TS].rearrange("p (c t) -> p c t", c=4); o += 4 * TS
    KW = TS + PAST
    kz = []
    for _ in range(2):
        kz.append(BIG[:, o:o + KW]); o += KW
    NVT = 22
    vaug = BIG[:, o:o + NVT * 256].rearrange("p (t g f) -> p t g f", t=NVT, g=2); o += NVT * 256
    TPAD = TS + 2 * PADW
    upad = BIG[:, o:o + 2 * TPAD].rearrange("p (c t) -> p c t", c=2); o += 2 * TPAD
    gpad = BIG[:, o:o + 2 * TPAD].rearrange("p (c t) -> p c t", c=2); o += 2 * TPAD
    assert o <= 29824, o

    tctr = [0]

    def tmp():
        i = tctr[0] % NTMP
        tctr[0] += 1
        return tmps[i], ("tmp", i)

    def tmpb(t):
        return t

    def mm(out, lhsT, rhs, start, stop, reads, writes):
        P.add("pe", lambda e: e.matmul(out, lhsT, rhs, start=start, stop=stop), reads, writes)

    def act(out, in_, func, reads, writes, scale=1.0, bias=0.0):
        P.add("act", lambda e: e.activation(out=out, in_=in_, func=func, scale=scale, bias=bias), reads, writes)

    def tt(out, in0, in1, op, reads, writes):
        P.add("dve", lambda e: e.tensor_tensor(out=out, in0=in0, in1=in1, op=op), reads, writes)

    def stt(out, in0, scalar, in1, op0, op1, reads, writes):
        P.add("dve", lambda e: e.scalar_tensor_tensor(out=out, in0=in0, scalar=scalar, in1=in1, op0=op0, op1=op1),
              reads, writes)

    def ts(out, in0, s1, op0, reads, writes, s2=None, op1=None):
        if op1 is None:
            P.add("dve", lambda e: e.tensor_scalar(out=out, in0=in0, scalar1=s1, scalar2=None, op0=op0), reads, writes)
        else:
            P.add("dve", lambda e: e.tensor_scalar(out=out, in0=in0, scalar1=s1, scalar2=s2, op0=op0, op1=op1),
                  reads, writes)

    def vcopy(out, in_, reads, writes):
        P.add("dve", lambda e: e.tensor_copy(out=out, in_=in_), reads, writes)

    def vmemset(ap, val, writes):
        P.add("dve", lambda e: e.memset(ap, val), (), writes)

    def dma(q, out, in_, key, reads, writes):
        P.add(q, lambda e: e.dma_start(out=out, in_=in_), reads, writes, dma=key)

    dma("pool", cmat[:], cmat_d[:, :, :], "c_cmat", (), ["cmat"])
    dma("pool", masks[:], mask_d[:, :, :], "c_mask", (), ["masks"])
    dma("sp", condf[:], cond_d[:, :, :], "c_cond", (), ["condf"])
    dma("sp", modb[:, 0, :], modb_d[0], "c_modb0", (), ["modb"])
    dma("sp", modb[:, 1, :], modb_d[1], "c_modb1", (), ["modb1"])
    dma("sp", ngt[:, 0], ng_d[0], "c_ng0", (), ["ngt"])
    dma("sp", ngt[:, 1], ng_d[1], "c_ng1", (), ["ngt1"])
    dma("sp", edget[:], edge_d[:, :, :, :], "c_edge", (), ["edget"])
    act(condb[:], condf[:], AF.Silu, ["condf"], ["condb"])
    ONES_MEAN = cmat[:, 0, :]
    BLK64 = cmat[:, 1, :]
    ONES256 = cmat[:, 2, :]
    IDENT = cmat[:, 3, :]
    PERM = cmat[:, 4, :]

    units = []

    def wi_src(l, which, f0, nf):
        v = wi_d[which][l].rearrange("(kc p) f -> p kc f", p=128)
        return v[:, :, f0 * 128:(f0 + nf) * 128], v[:, :, DFF + f0 * 128:DFF + (f0 + nf) * 128]

    for_units = []

    def plan_units():
        seq = []
        for l in range(1):
            pass
        return seq

    ucount = [0]
    ucursor = [0]
    ureleased = [0]

    def unit_views(kind, s, nf=3):
        sl = slots[s]
        if kind == "F":
            wi = sl[:, 0:8 * 2 * nf * 128].rearrange("p (k f) -> p k f", k=8)
            wo = sl[:, 6144:6144 + nf * 1024].rearrange("p (c f) -> p c f", c=nf)
            return wi, wo
        if kind == "M":
            return sl[:, 0:9216].rearrange("p (k f) -> p k f", k=8)
        if kind == "WIN":
            return sl[:, 0:6144].rearrange("p (k f) -> p k f", k=8)
        if kind == "WOUT":
            wout = sl[:, 0:8192].rearrange("p (k f) -> p k f", k=8)
            pw = sl[:, 8192:8704].rearrange("p (k f) -> p k f", k=2)
            pb = sl[:, 8704:8960].rearrange("p (k f) -> p k f", k=2)
            return wout, pw, pb
        if kind == "DIAG":
            return sl[:, 0:7936].rearrange("p (c j f) -> p c j f", c=2, j=31)
        raise ValueError(kind)

    def stoks(s):
        return [("slot", s, "a"), ("slot", s, "b")]

    def load_unit(u):
        idx, kind, l, arg = u
        s = idx % 2
        both = stoks(s)
        if kind == "F":
            which, f0, nf = arg
            wi, wo = unit_views("F", s, nf)
            g_src, u_src = wi_src(l, which, f0, nf)
            dma("pool", wi[:, :, 0:nf * 128], g_src, "u%d_a" % s, (), [("slot", s, "a")])
            dma("pool", wi[:, :, nf * 128:2 * nf * 128], u_src, "u%d_b" % s, (), [("slot", s, "a")])
            wsrc = wo_d[which][l][f0 * 128:(f0 + nf) * 128, :].rearrange("(c p) f -> p c f", p=128)
            dma("pool", wo, wsrc, "u%d_c" % s, (), [("slot", s, "b")])
        elif kind == "M":
            j = arg
            mv = unit_views("M", s)
            src = modw_d[l].rearrange("(kc p) f -> p kc f", p=128)[:, :, j * 1152:(j + 1) * 1152]
            dma("pool", mv, src, "u%d_a" % s, (), both)
        elif kind == "WIN":
            j = arg
            wv = unit_views("WIN", s)
            src = win_d[l].rearrange("(kc p) f -> p kc f", p=128)[:, :, j * 768:(j + 1) * 768]
            dma("pool", wv, src, "u%d_a" % s, (), both)
        elif kind == "WOUT":
            wout, pw, pb = unit_views("WOUT", s)
            dma("pool", wout, wout_d[l].rearrange("(kc p) f -> p kc f", p=128), "u%d_a" % s, (), both)
            dma("pool", pw, pw_d[l].rearrange("(kc p) f -> p kc f", p=128), "u%d_b" % s, (), both)
            P.add("pool", lambda e, pb=pb: e.memset(pb, 0.0), (), both)
            for gi in range(4):
                ch, half = gi // 2, gi % 2
                dma("pool", pb[64 * half:64 * half + 64, ch, 64 * half:64 * half + 64], poolw_d[l, gi],
                    "u%d_p%d" % (s, gi), (), both)
        elif kind == "DIAG":
            pass
        else:
            raise ValueError(kind)

    def _load_ready():
        while ucount[0] < len(units) and ucount[0] < ureleased[0] + 2:
            load_unit(units[ucount[0]])
            ucount[0] += 1

    def next_unit(expect_kind):
        u = units[ucursor[0]]
        assert u[1] == expect_kind, (u, expect_kind)
        _load_ready()
        assert ucount[0] > ucursor[0], ("unit not loadable yet", u, ureleased[0])
        ucursor[0] += 1
        return u[0] % 2, ("slotpair", u[0] % 2), u

    def release_unit():
        ureleased[0] += 1
        _load_ready()

    def add_units(kind_list):
        for (kind, l, arg) in kind_list:
            units.append((len(units), kind, l, arg))

    def layer_units(l):
        r = []
        f0 = 0
        for nf in FBLOCKS:
            r.append(("F", l, (0, f0, nf)))
            f0 += nf
        r += [("WIN", l, 0), ("WIN", l, 1), ("WOUT", l, None), ("DIAG", l, None)]
        f0 = 0
        for nf in FBLOCKS:
            r.append(("F", l, (1, f0, nf)))
            f0 += nf
        return r

    k0, k1 = (6, 6) if debug_stop is None else debug_stop
    plan = [("mod", 0)]
    need_mod1 = False
    if max(k0, k1) > 3:
        plan.append(("mod", 1))
    for phase, kk in ((0, k0), (1, k1)):
        if kk < 0:
            continue
        plan.append(("load", phase))
        for st in range(kk):
            l, sub = st // 3, st % 3
            if l == 1 and need_mod1:
                plan.append(("mod", 1))
                need_mod1 = False
            plan.append((("ffn1", "mixer", "ffn2")[sub], phase, l))
        plan.append(("store", phase))
    for it in plan:
        if it[0] == "mod":
            pass
        elif it[0] in ("ffn1", "ffn2"):
            f0 = 0
            for nf in FBLOCKS:
                add_units([("F", it[2], (0 if it[0] == "ffn1" else 1, f0, nf))])
                f0 += nf
        elif it[0] == "mixer":
            add_units([("WIN", it[2], 0), ("WIN", it[2], 1), ("DIAG", it[2], None), ("WOUT", it[2], None)])

    mod_q = []
    mod_loaded = [0]
    mod_done = [0]
    mod_list = []

    def mini_view(slot):
        return X[:, slot, 1024:2048].bitcast(BF16).rearrange("p (k f) -> p k f", k=8)

    def mini_toks(slot):
        return [("X", slot, 1024), ("X", slot, 1536)]

    def mod_enqueue(l):
        for m in range(36):
            mod_list.append((l, m))

    mod_cap = [12]

    def _mod_load_ahead():
        while mod_loaded[0] < len(mod_list) and mod_loaded[0] < min(mod_done[0] + 8, mod_cap[0]):
            l, m = mod_list[mod_loaded[0]]
            slot = mod_loaded[0] % 8
            src = modw_d[l].rearrange("(kc p) f -> p kc f", p=128)[:, :, m * 256:(m + 1) * 256]
            dma("pool", mini_view(slot), src, "mm%d" % slot, (), mini_toks(slot))
            mod_loaded[0] += 1

    def compute_AB(l, i):
        ng = "ngt" if l == 0 else "ngt1"
        for cj in range(2):
            sh = modv[l][:, (3 * i) * 8:(3 * i) * 8 + 8, cj]
            sc = modv[l][:, (3 * i + 1) * 8:(3 * i + 1) * 8 + 8, cj]
            gg = modv[l][:, (3 * i + 2) * 8:(3 * i + 2) * 8 + 8, cj]
            A = AB[:, l, cj, i, 0, :]
            B = AB[:, l, cj, i, 1, :]
            G = AB[:, l, cj, i, 2, :]
            stt(A, sc, 1.0, ngt[:, l, i, :], ALU.add, ALU.mult, [("modv", l), ng], [("AB", l)])
            vcopy(B, sh, [("modv", l)], [("AB", l)])
            ts(G, gg, 0.5 if i != 1 else 1.0, ALU.mult, [("modv", l)], [("AB", l)])

    def mod_pump(n):
        for _ in range(n):
            if mod_done[0] >= len(mod_list):
                return
            _mod_load_ahead()
            idx = mod_done[0]
            l, m = mod_list[idx]
            slot = idx % 8
            mv = mini_view(slot)
            mb = "modb" if l == 0 else "modb1"
            bank = 6 + (idx % 2)
            col0 = 0
            for cc in range(2):
                for k in range(NCH):
                    mm(ps[:, bank, col0 + 2 * cc:col0 + 2 * cc + 2], mv[:, k, cc * 128:(cc + 1) * 128], condb[:, k, :],
                       k == 0, k == NCH - 1, mini_toks(slot) + ["condb"], [("ps", bank)])
            for cc in range(2):
                cg = 2 * m + cc
                ts(modv[l][:, cg, :], ps[:, bank, col0 + 2 * cc:col0 + 2 * cc + 2], modb[:, l, cg:cg + 1], ALU.add,
                   [("ps", bank), mb], [("modv", l)])
            mod_done[0] += 1
            _mod_load_ahead()
            if m % 12 == 11:
                compute_AB(l, m // 12)

    def mod_ensure(l, i):
        while mod_done[0] < len(mod_list) and mod_list[mod_done[0]] <= (l, 12 * i + 11):
            mod_pump(1)

    def norm_mod(l, cj, i, t0, n, dst, dst_tok_fn):
        msb = 6
        for c in range(NCH):
            sq, sqt = tmp()
            sqv = sq.bitcast(BF16)[:, 0:n]
            act(sqv, X[:, c, t0:t0 + n], AF.Square, [("X", c, t0)], [sqt])
            mm(ps[:, msb, 0:n], ONES_MEAN, sqv, c == 0, c == NCH - 1, [sqt, "cmat"], [("ps", msb)])
        ln, lnt = tmp()
        act(ln[:, 0:n], ps[:, msb, 0:n], AF.Ln, [("ps", msb)], [lnt], bias=EPS)
        rs, rst = tmp()
        act(rs[:, 0:n], ln[:, 0:n], AF.Exp, [lnt], [rst], scale=-0.5)
        for c in range(NCH):
            t, ttok = tmp()
            stt(t[:, 0:n], X[:, c, t0:t0 + n], AB[:, l, cj, i, 0, c:c + 1], rs[:, 0:n], ALU.mult, ALU.mult,
                [("X", c, t0), ("AB", l), rst], [ttok])
            act(dst[:, c, 0:n] if dst is hgrp else dst[:, c, t0:t0 + n], t[:, 0:n], AF.Identity,
                [ttok, ("AB", l)], [dst_tok_fn(c)], bias=AB[:, l, cj, i, 1, c:c + 1])

    def ffn(l, which, cj, groups, hook=None):
        i = 0 if which == 0 else 2
        LOOK = 1
        for (t0, n) in groups[:LOOK]:
            norm_mod(l, cj, i, t0, n, Hv, lambda c, t0=t0: ("H", c, t0))
        items = []
        f0 = 0
        blk_info = []
        for bi, nf in enumerate(FBLOCKS):
            for gi, (t0, n) in enumerate(groups):
                items.append((bi, nf, f0, gi, t0, n))
            f0 += nf
        state = {"cur_blk": -1, "views": None, "tok": None}
        gu_ctr = [0]
        blk_views = {}

        def GU(it, ab):
            bi, nf, f0, gi, t0, n = it
            if bi not in blk_views:
                s, tok, u = next_unit("F")
                blk_views[bi] = (unit_views("F", s, nf), tok)
            (wi, wo), tok = blk_views[bi]
            for fc in range(nf):
                pr = gu_ctr[0] % 2
                gu_ctr[0] += 1
                bg, bu = 2 * pr, 2 * pr + 1
                for k in range(NCH):
                    mm(ps[:, bg, 0:n], wi[:, k, fc * 128:(fc + 1) * 128], Hv[:, k, t0:t0 + n], k == 0, k == NCH - 1,
                       [("slot", tok[1], "a"), ("H", k, t0)], [("ps", bg)])
                for k in range(NCH):
                    mm(ps[:, bu, 0:n], wi[:, k, (nf + fc) * 128:(nf + fc + 1) * 128], Hv[:, k, t0:t0 + n], k == 0,
                       k == NCH - 1, [("slot", tok[1], "a"), ("H", k, t0)], [("ps", bu)])
                sil, silt = tmp()
                act(sil[:, 0:n], ps[:, bg, 0:n], AF.Silu, [("ps", bg)], [silt])
                tt(actb[ab][:, fc, 0:n], ps[:, bu, 0:n], sil[:, 0:n], ALU.mult, [("ps", bu), silt], [("actb", ab, fc)])

        def WO(it, ab, last_of_block):
            bi, nf, f0, gi, t0, n = it
            (wi, wo), tok = blk_views[bi]
            for d in range(NCH):
                bo = 4 + (d % 2)
                for fc in range(nf):
                    mm(ps[:, bo, 0:n], wo[:, fc, d * 128:(d + 1) * 128], actb[ab][:, fc, 0:n], fc == 0, fc == nf - 1,
                       [("slot", tok[1], "b"), ("actb", ab, fc)], [("ps", bo)])
                stt(X[:, d, t0:t0 + n], ps[:, bo, 0:n], AB[:, l, cj, i, 2, d:d + 1], X[:, d, t0:t0 + n], ALU.mult, ALU.add,
                    [("ps", bo), ("AB", l), ("X", d, t0)], [("X", d, t0)])
            if last_of_block:
                release_unit()
                if hook is not None:
                    hook()

        ng = len(groups)
        for ii, it in enumerate(items):
            if it[0] == 0 and it[3] + LOOK < ng:
                (t0_, n_) = groups[it[3] + LOOK]
                norm_mod(l, cj, i, t0_, n_, Hv, lambda c, t0_=t0_: ("H", c, t0_))
            GU(it, ii % 2)
            if ii > 0:
                pit = items[ii - 1]
                WO(pit, (ii - 1) % 2, pit[3] == ng - 1)
        pit = items[-1]
        WO(pit, (len(items) - 1) % 2, True)

    def mixer(l, cj, groups, seqs, is_sample):
        T = sum(n for _, n in groups)
        ntiles = T // 128
        stok = "small%d" % 0

        def padcol(t):
            for si, (s0, sl) in enumerate(seqs):
                if s0 <= t < s0 + sl:
                    return t + PADW * (2 * si + 1)
            raise ValueError(t)

        dma("sp", smallt[:], small_d[l], "c_small", (), ["smallt"])
        dma("sp", dwt[:], dw_d[l], "c_dw", (), ["dwt"])
        dma("sp", sinkraw[:], sink_d[l], "c_sink", (), ["sinkraw"])
        act(esb[:], sinkraw[:], AF.Exp, ["sinkraw"], ["esb"])
        def load_cache():
            if is_sample:
                dma("pool", kz[0][0:64, TS:TS + PAST], ckT_d[l][0:64, :], "c_ck", (), [("kst", "cache")])
                dma("pool", kz[1][64:128, TS:TS + PAST], ckT_d[l][64:128, :], "c_ck1", (), [("kst", "cache")])
                cvv = cv_d[l].rearrange("(t p) f -> p t f", p=128)
                dma("pool", vaug[:, 18:22, 0, 0:64], cvv[:, :, 0:64], "c_cv0", (), [("vaug", "cache")])
                dma("pool", vaug[:, 18:22, 1, 64:128], cvv[:, :, 64:128], "c_cv1", (), [("vaug", "cache")])

        PSC = lambda c: smallt[:, c:c + 1]
        CB = lambda c: smallt[:, 2 + c:3 + c]
        CNG = lambda c: smallt[:, 4 + c:5 + c]
        QG = smallt[:, 6:7]
        KG = smallt[:, 7:8]

        sA, tokA, _ = next_unit("WIN")
        winA = unit_views("WIN", sA)
        sB, tokB, _ = next_unit("WIN")
        winB = unit_views("WIN", sB)

        def wcol(ch):
            if ch < 6:
                return winA[:, :, ch * 128:(ch + 1) * 128], tokA
            return winB[:, :, (ch - 6) * 128:(ch - 5) * 128], tokB

        HN = 264
        items = []
        for (g0, gn) in groups:
            for off in range(0, gn, 256):
                items.append((g0 + off, g0))
        all_tmp_toks = [("tmp", i) for i in range(NTMP)] + [("tmph", i, h) for i in range(NTMP) for h in range(2)]

        def tmp_barrier():
            P.add("dve", lambda e: e.memset(scr[:, :], 0.0), (), all_tmp_toks)

        tmp_barrier()
        hctr = {"n": 0, "c": 0}
        pools = {"n": [0, 1], "c": [6, 7]}
        HSLOT = [(ti, h) for ti in (2, 3, 4, 5) for h in (0, 1)]

        def qn_slot(k):
            ti, h = HSLOT[k]
            return tmps[ti][:, h * HN:h * HN + 256], ("tmph", ti, h)

        def qb_slot(k):
            ti, h = HSLOT[5 + k // 2]
            o = 2 * h * HN + (k % 2) * 272
            return tmps[ti].bitcast(BF16)[:, o:o + 256], ("tmph", ti, h)

        def half(pool):
            lst = pools[pool]
            k = hctr[pool] % (2 * len(lst))
            hctr[pool] += 1
            ti, h = lst[k // 2], k % 2
            return tmps[ti][:, h * HN:h * HN + 256], tmps[ti].bitcast(BF16)[:, 2 * h * HN:2 * h * HN + 256], ("tmph", ti, h)

        def hb(i):
            return hgrp[:, :, (i % 2) * 256:(i % 2) * 256 + 256]

        def norm1(i):
            t0 = items[i][0]
            for c in range(NCH):
                _, sqv, sqt = half("n")
                act(sqv, X[:, c, t0:t0 + 256], AF.Square, [("X", c, items[i][1])], [sqt])
                mm(ps[:, 4, 0:256], ONES_MEAN, sqv, c == 0, c == NCH - 1, [sqt, "cmat"], [("ps", 4)])

        def norm2(i):
            t0 = items[i][0]
            lnv, _, lnt = half("c")
            act(lnv, ps[:, 4, 0:256], AF.Ln, [("ps", 4)], [lnt], bias=EPS)
            act(lnv, lnv, AF.Exp, [lnt], [lnt], scale=-0.5)
            for c in range(NCH):
                tv, _, ttok = half("n")
                stt(tv, X[:, c, t0:t0 + 256], AB[:, l, cj, 1, 0, c:c + 1], lnv, ALU.mult, ALU.mult,
                    [("X", c, items[i][1]), ("AB", l), lnt], [ttok])
                ts(hb(i)[:, c, :], tv, AB[:, l, cj, 1, 1, c:c + 1], ALU.add, [ttok, ("AB", l)], [("hgrp", i % 2, c)])

        ubank = [0]

        def proj(i, ch, bank=None):
            if bank is None:
                b = ubank[0] % 4
                ubank[0] += 1
            else:
                b = bank
            wv, wt = wcol(ch)
            for k in range(NCH):
                mm(ps[:, b, 0:256], wv[:, k, :], hb(i)[:, k, :], k == 0, k == NCH - 1, [wt, ("hgrp", i % 2, k)], [("ps", b)])
            return b

        def item_body(i, mid_hook):
            t0 = items[i][0]
            a = padcol(t0)
            if is_sample:
                rc_ = ropec[:, (i % 2) * 256:(i % 2) * 256 + 256]
                rs_ = ropes[:, (i % 2) * 256:(i % 2) * 256 + 256]
                vtk = [("vout", g_) for g_ in range(8)]
                dma("sp", rc_, cos_d[:, t0:t0 + 256], "c_rc%d" % (i % 2), (), [("ropec", i % 2)] + vtk)
                dma("sp", rs_, sin_d[:, t0:t0 + 256], "c_rs%d" % (i % 2), (), [("ropes", i % 2)] + vtk)
            for ch in (0, 1):
                b = proj(i, ch)
                act(upad[:, ch, a:a + 256], ps[:, b, 0:256], AF.Copy, [("ps", b)], [("upad", ch, items[i][1])])
            mid_hook()
            for cc in (0, 1):
                bgt = proj(i, 4 + cc)
                sgv, _, sgt = half("c")
                act(sgv, ps[:, bgt, 0:256], AF.Exp, [("ps", bgt)], [sgt], scale=-1.0)
                act(sgv, sgv, AF.Ln, [sgt], [sgt], bias=1.0)
                act(sgv, sgv, AF.Exp, [sgt], [sgt], scale=-1.0)
                ba = proj(i, 2 + cc)
                tt(gpad[:, cc, a:a + 256], ps[:, ba, 0:256], sgv, ALU.mult, [("ps", ba), sgt], [("gpad", cc, items[i][1])])
            st = {}
            QK = [6, 7, 8, 9, 10]
            PBANK = {6: 0, 7: 1, 8: 2, 9: 3, 10: 7}
            MSLOC = {6: (5, 0), 7: (5, 256), 8: (6, 0), 9: (4, 0), 10: (4, 256)}
            RBANK = {6: 0, 7: 1, 8: 2, 9: 3, 10: 5}

            def stA(ch):
                b = proj(i, ch)
                _, sqv, sqt = half("c")
                act(sqv, ps[:, b, 0:256], AF.Square, [("ps", b)], [sqt])
                st[ch] = dict(b=b, sqv=sqv, sqt=sqt)

            def stB1(ch):
                d = st[ch]
                mb, mc = MSLOC[ch]
                mm(ps[:, mb, mc:mc + 256], BLK64, d["sqv"], True, True, [d["sqt"], "cmat"], [("ps", mb)])

            def stB2(ch):
                d = st[ch]
                k = ch - 6
                mb, mc = MSLOC[ch]
                lnv, _, lnt = half("c")
                act(lnv, ps[:, mb, mc:mc + 256], AF.Ln, [("ps", mb)], [lnt], bias=EPS)
                act(lnv, lnv, AF.Exp, [lnt], [lnt], scale=-0.5)
                gsc = KG if ch == 10 else QG
                if ch == 10:
                    dstv, dtok = None, ("kst", items[i][1])
                else:
                    dstv, dtok = qst[:, ch - 6, t0:t0 + 256], ("qst", ch - 6, items[i][1])
                if not is_sample and ch != 10:
                    stt(dstv, ps[:, d["b"], 0:256], gsc, lnv, ALU.mult, ALU.mult, [("ps", d["b"]), "smallt", lnt], [dtok])
                    return
                qnv, qnt = qn_slot(k)
                stt(qnv, ps[:, d["b"], 0:256], gsc, lnv, ALU.mult, ALU.mult, [("ps", d["b"]), "smallt", lnt], [qnt])
                d.update(qnv=qnv, qnt=qnt, dstv=dstv, dtok=dtok)
                if not is_sample:
                    vcopy(kz[0][0:64, t0:t0 + 256], qnv[0:64, :], [qnt], [dtok])
                    vcopy(kz[1][64:128, t0:t0 + 256], qnv[64:128, :], [qnt], [dtok])
                    dma("sp", nk_d[l][:, t0:t0 + 256], qnv, "o_nk", [qnt], [])
                else:
                    qbv, qbt = qb_slot(k)
                    vcopy(qbv, qnv, [qnt], [qbt])
                    d.update(qbv=qbv, qbt=qbt)

            def stC1(ch):
                if not is_sample:
                    return
                d = st[ch]
                rb = {6: 5, 10: 6}.get(ch, d["b"])
                d["rb"] = rb
                mm(ps[:, rb, 0:256], PERM, d["qbv"], True, True, [d["qbt"], "cmat"], [("ps", rb)])

            def stC2(ch):
                if not is_sample:
                    return
                d = st[ch]
                rb = d["rb"]
                tt(d["qnv"], d["qnv"], rc_, ALU.mult, [d["qnt"], ("ropec", i % 2)], [d["qnt"]])
                t2v, _, t2t = half("c")
                tt(t2v, ps[:, rb, 0:256], rs_, ALU.mult, [("ps", rb), ("ropes", i % 2)], [t2t])
                if ch == 10:
                    tt(kz[0][0:64, t0:t0 + 256], d["qnv"][0:64, :], t2v[0:64, :], ALU.add, [d["qnt"], t2t], [d["dtok"]])
                    tt(kz[1][64:128, t0:t0 + 256], d["qnv"][64:128, :], t2v[64:128, :], ALU.add, [d["qnt"], t2t], [d["dtok"]])
                else:
                    tt(d["dstv"], d["qnv"], t2v, ALU.add, [d["qnt"], t2t], [d["dtok"]])

            B1, B2 = [6, 7, 8], [9, 10]
            for ch in B1:
                stA(ch)
            for ch in B1:
                stB1(ch)
            for ch in B1:
                stB2(ch)
            for ch in B2:
                stA(ch)
            for ch in B2:
                stB1(ch)
            for ch in B1:
                stC1(ch)
            for ch in B2:
                stB2(ch)
            for ch in B1:
                stC2(ch)
            for ch in B2:
                stC1(ch)
            V_LATE = True
            wv, wt = wcol(11)
            for tl in range(2):
                for k in range(NCH):
                    mm(ps[:, 7, tl * 128:(tl + 1) * 128], hb(i)[:, k, tl * 128:(tl + 1) * 128], wv[:, k, :], k == 0,
                       k == NCH - 1, [wt, ("hgrp", i % 2, k)], [("ps", 7)])
            for tl in range(2):
                gt = (t0 // 128) + tl
                act(vaug[:, gt, 0, 0:64], ps[:, 7, tl * 128:tl * 128 + 64], AF.Copy, [("ps", 7)], [("vaug", gt)])
                act(vaug[:, gt, 1, 64:128], ps[:, 7, tl * 128 + 64:tl * 128 + 128], AF.Copy, [("ps", 7)], [("vaug", gt)])
                if not is_sample:
                    vcopy(vout[:, gt, :], ps[:, 7, tl * 128:(tl + 1) * 128], [("ps", 7)], [("vout", gt)])
            for ch in B2:
                stC2(ch)

        norm1(0)
        norm2(0)
        init_big(is_sample, T)
        load_cache()
        for i in range(len(items)):
            if i + 1 < len(items):
                norm1(i + 1)
                item_body(i, lambda i=i: norm2(i + 1))
            else:
                item_body(i, lambda: None)
        tmp_barrier()
        release_unit()
        release_unit()
        if not is_sample:
            dma("sp", nv_d[l].rearrange("(t p) f -> p t f", p=128), vout, "o_nv",
                [("vout", gt) for gt in range(8)], [])

        sD, tokD, _ = next_unit("DIAG")
        diag = unit_views("DIAG", sD)
        sW, tokW, _ = next_unit("WOUT")
        wout, pwv, pbv = unit_views("WOUT", sW)
        for cc in range(2):
            for j in range(31):
                act(diag[:, cc, j, :], IDENT, AF.Copy, ["cmat", "dwt"], [tokD], scale=dwt[:, cc, j:j + 1])

        def wout_part(kc0, src, src_tokfn, t0, n):
            for d in range(NCH):
                bo = 4 + (d % 2)
                for kk in range(4):
                    mm(ps[:, bo, 0:n], wout[:, kc0 + kk, d * 128:(d + 1) * 128], src[:, kk, 0:n], kk == 0, kk == 3,
                       [tokW, src_tokfn(kk)], [("ps", bo)])
                stt(X[:, d, t0:t0 + n], ps[:, bo, 0:n], AB[:, l, cj, 1, 2, d:d + 1], X[:, d, t0:t0 + n], ALU.mult, ALU.add,
                    [("ps", bo), ("AB", l), ("X", d, t0)], [("X", d, t0)])

        allsegs = []
        for (t0, n) in groups:
            t = t0
            while t < t0 + n:
                for (s0, sl) in seqs:
                    if s0 <= t < s0 + sl:
                        e = min(t0 + n, s0 + sl)
                        allsegs.append(dict(st=t, sn=e - t, at_start=(t == s0), at_end=(e == s0 + sl), t0=t0, n=n,
                                            last=(e == t0 + n)))
                        t = e
                        break
        gr_all = {cc: [("gpad", cc, g0) for (g0, _) in groups] for cc in (0, 1)}
        ur_all = {ch: [("upad", ch, g0) for (g0, _) in groups] for ch in (0, 1)}

        def conv_mms(si):
            sg_ = allsegs[si]
            a, sn = padcol(sg_["st"]), sg_["sn"]
            for cc in (0, 1):
                b = 2 * (si % 2) + cc
                for j in range(31):
                    mm(ps[:, b, 0:sn], diag[:, cc, j, :], gpad[:, cc, a + j - 15:a + j - 15 + sn], j == 0, j == 30,
                       [tokD] + gr_all[cc], [("ps", b)])

        def pool_dve(si):
            sg_ = allsegs[si]
            a, sn, at_start, at_end = padcol(sg_["st"]), sg_["sn"], sg_["at_start"], sg_["at_end"]
            outs = []
            for ch in (0, 1):
                ur = ur_all[ch]
                A_, At = tmp()
                tt(A_[:, 0:sn + 14], upad[:, ch, a - 8:a + sn + 6], upad[:, ch, a - 7:a + sn + 7], ALU.add, ur, [At])
                B_, Bt = tmp()
                tt(B_[:, 0:sn + 12], A_[:, 0:sn + 12], A_[:, 2:sn + 14], ALU.add, [At], [Bt])
                if ch == 0:
                    lo_src, lo_off, lo_w = A_, 7, 2
                    hi_src, hi_off, hi_w = B_, 6, 4
                    lot, hit = At, Bt
                else:
                    C_, Ct = tmp()
                    tt(C_[:, 0:sn + 8], B_[:, 0:sn + 8], B_[:, 4:sn + 12], ALU.add, [Bt], [Ct])
                    D_, Dt = tmp()
                    tt(D_[64:128, 0:sn], C_[64:128, 0:sn], C_[64:128, 8:sn + 8], ALU.add, [Ct], [Dt])
                    lo_src, lo_off, lo_w = C_, 4, 8
                    hi_src, hi_off, hi_w = D_, 0, 16
                    lot, hit = Ct, Dt
                mean, mt = tmp()
                ts(mean[0:64, 0:sn], lo_src[0:64, lo_off:lo_off + sn], 1.0 / lo_w, ALU.mult, [lot], [mt])
                ts(mean[64:128, 0:sn], hi_src[64:128, hi_off:hi_off + sn], 1.0 / hi_w, ALU.mult, [hit], [mt])
                if at_start:
                    tt(mean[0:64, 0:8], lo_src[0:64, lo_off:lo_off + 8], edget[0:64, ch, 0, :], ALU.mult,
                       [lot, "edget"], [mt])
                    tt(mean[64:128, 0:8], hi_src[64:128, hi_off:hi_off + 8], edget[64:128, ch, 0, :], ALU.mult,
                       [hit, "edget"], [mt])
                if at_end:
                    tt(mean[0:64, sn - 8:sn], lo_src[0:64, lo_off + sn - 8:lo_off + sn], edget[0:64, ch, 1, :],
                       ALU.mult, [lot, "edget"], [mt])
                    tt(mean[64:128, sn - 8:sn], hi_src[64:128, hi_off + sn - 8:hi_off + sn], edget[64:128, ch, 1, :],
                       ALU.mult, [hit, "edget"], [mt])
                plv = A_.bitcast(BF16)[:, 0:sn]
                tt(plv, mean[:, 0:sn], upad[:, ch, a:a + sn], ALU.subtract, [mt] + ur, [At])
                outs.append((plv, At, tctr[0]))
            return outs

        def pool_mm(si, outs):
            sg_ = allsegs[si]
            sn, off = sg_["sn"], sg_["st"] - sg_["t0"]
            for ch in (0, 1):
                plv, plt, ser = outs[ch]
                assert tctr[0] - ser < NTMP - 1, "tmp ring wrapped (pool)"
                b = 6 + ch
                mm(ps[:, b, 0:sn], pbv[:, ch, :], plv, True, True, [tokW, plt], [("ps", b)])
                act(opc[:, ch, off:off + sn], ps[:, b, 0:sn], AF.Copy, [("ps", b), "smallt"], [("omix", ch)],
                    scale=PSC(ch))

        def conv_post(si):
            sg_ = allsegs[si]
            sn, off = sg_["sn"], sg_["st"] - sg_["t0"]
            ybs = []
            for cc in (0, 1):
                b = 2 * (si % 2) + cc
                yb, ybt = tmp()
                act(yb[:, 0:sn], ps[:, b, 0:sn], AF.Identity, [("ps", b), "smallt"], [ybt], bias=CB(cc))
                sq, sqt = tmp()
                sqv = sq.bitcast(BF16)[:, 0:sn]
                act(sqv, yb[:, 0:sn], AF.Square, [ybt], [sqt])
                mm(ps[:, 6, 0:sn], ONES256, sqv, cc == 0, cc == 1, [sqt, "cmat"], [("ps", 6)])
                ybs.append((yb, ybt))
            ln, lnt = tmp()
            act(ln[:, 0:sn], ps[:, 6, 0:sn], AF.Ln, [("ps", 6)], [lnt], bias=EPS)
            act(ln[:, 0:sn], ln[:, 0:sn], AF.Exp, [lnt], [lnt], scale=-0.5)
            zs = []
            for cc in (0, 1):
                yb, ybt = ybs[cc]
                stt(yb[:, 0:sn], yb[:, 0:sn], CNG(cc), ln[:, 0:sn], ALU.mult, ALU.mult, [ybt, "smallt", lnt], [ybt])
                zb, zbt = tmp()
                zbv = zb.bitcast(BF16)[:, 0:sn]
                act(zbv, yb[:, 0:sn], AF.Silu, [ybt], [zbt])
                zs.append((zbv, zbt))
            for co in (0, 1):
                b = 6 + co
                for ci in (0, 1):
                    mm(ps[:, b, 0:sn], pwv[:, ci, co * 128:(co + 1) * 128], zs[ci][0], ci == 0, ci == 1,
                       [tokW, zs[ci][1]], [("ps", b)])
                act(opc[:, 2 + co, off:off + sn], ps[:, b, 0:sn], AF.Copy, [("ps", b)], [("omix", 2 + co)])

        conv_mms(0)
        for si in range(len(allsegs)):
            outs = pool_dve(si)
            if si + 1 < len(allsegs):
                conv_mms(si + 1)
            pool_mm(si, outs)
            conv_post(si)
            if allsegs[si]["last"]:
                wout_part(0, opc, lambda kk: ("omix", kk), allsegs[si]["t0"], allsegs[si]["n"])

        release_unit()
        sbank = [0]
        kall = [("kst", g0) for (g0, _) in groups] + ([("kst", "cache")] if is_sample else [])
        allsteps = []
        ginfo = []
        for (t0, n) in groups:
            qall = [("qst", c, t0) for c in range(4)]
            first = len(allsteps)
            ntile0 = None
            for tl in range(n // 128):
                gt = t0 // 128 + tl
                q0 = t0 + tl * 128
                chunks = []
                if is_sample:
                    if gt > 0:
                        chunks.append((q0 - 128, gt - 1, 0))
                    chunks.append((q0, gt, None))
                    if gt < ntiles - 1:
                        chunks.append((q0 + 128, gt + 1, 1))
                    for cti in range(4):
                        chunks.append((TS + cti * 128, 18 + cti, None))
                else:
                    for (s0, sl) in seqs:
                        if s0 <= q0 < s0 + sl:
                            for kt in range(sl // 128):
                                chunks.append((s0 + kt * 128, (s0 // 128) + kt, None))
                for kvh in range(2):
                    for ci, ch in enumerate(chunks):
                        allsteps.append((tl, gt, q0, ci, len(chunks), ch, kvh, t0, qall))
                if ntile0 is None:
                    ntile0 = len(allsteps) - first
            ginfo.append((first, ntile0, t0, n, len(allsteps) - 1))

        def qk(step):
            tl, gt, q0, ci, nci, (kc0, vt, mk), kvh, t0, qall = step
            b = sbank[0] % 4
            sbank[0] += 1
            for hh in range(4):
                mm(ps[:, b, hh * 128:(hh + 1) * 128], kz[kvh][:, kc0:kc0 + 128],
                   qst[:, hh, q0:q0 + 128], True, True, kall + qall, [("ps", b)])
            pt, ptt = tmp()
            ptv = pt.bitcast(BF16)[:, 0:512]
            act(ptv, ps[:, b, :], AF.Exp, [("ps", b)], [ptt], scale=0.125)
            if mk is not None:
                pv4 = ptv.rearrange("p (h q) -> p h q", h=4)
                tt(pv4, pv4, masks[:, mk, :].unsqueeze(1).broadcast_to([128, 4, 128]), ALU.mult, [ptt, "masks"], [ptt])
            return (ptv, ptt, tctr[0])

        def pv(step, pts):
            tl, gt, q0, ci, nci, (kc0, vt, mk), kvh, t0, qall = step
            pb = 4 + 2 * (gt % 2)
            assert tctr[0] - pts[2] < NTMP, "tmp ring wrapped"
            vtok = ("vaug", vt) if vt < 18 else ("vaug", "cache")
            mm(ps[:, pb + kvh, :], vaug[:, vt, kvh, :], pts[0], ci == 0, ci == nci - 1,
               [vtok, pts[1]], [("ps", pb + kvh)])
            if ci == nci - 1:
                dlo, nlo = (64, 0) if kvh == 0 else (0, 64)
                rc2, rc2t = tmp()
                if kvh == 0:
                    ln, lnt = tmp()
                    for hh in range(4):
                        act(ln[dlo:dlo + 64, hh * 128:(hh + 1) * 128], ps[dlo:dlo + 64, pb + kvh, hh * 128:(hh + 1) * 128],
                            AF.Ln, [("ps", pb + kvh), "esb"], [lnt], bias=esb[dlo:dlo + 64, 4 * kvh + hh:4 * kvh + hh + 1])
                    act(ln[dlo:dlo + 64, 0:512], ln[dlo:dlo + 64, 0:512], AF.Exp, [lnt], [lnt], scale=-1.0)
                    vcopy(rc2[nlo:nlo + 64, 0:512], ln[dlo:dlo + 64, 0:512], [lnt], [rc2t])
                else:
                    for hh in range(4):
                        ts(rc2[nlo:nlo + 64, hh * 128:(hh + 1) * 128], ps[dlo:dlo + 64, pb + kvh, hh * 128:(hh + 1) * 128],
                           esb[dlo:dlo + 64, 4 * kvh + hh:4 * kvh + hh + 1], ALU.add, [("ps", pb + kvh), "esb"], [rc2t])
                    P.add("dve", lambda e, o_=rc2[nlo:nlo + 64, 0:512]: e.reciprocal(out=o_, in_=o_), [rc2t], [rc2t])
                tt(oattn[nlo:nlo + 64, :, tl * 128:(tl + 1) * 128],
                   ps[nlo:nlo + 64, pb + kvh, :].rearrange("p (h q) -> p h q", h=4),
                   rc2[nlo:nlo + 64, 0:512].rearrange("p (h q) -> p h q", h=4), ALU.mult,
                   [("ps", pb + kvh), rc2t], [("omix", kk) for kk in range(4)])


        def wout_attn(t0, n):
            for d in range(NCH):
                bo = 6 + (d % 2)
                for kk in range(4):
                    mm(ps[:, bo, 0:n], wout[:, 4 + kk, d * 128:(d + 1) * 128], oattn[:, kk, 0:n], kk == 0, kk == 3,
                       [tokW, ("omix", kk)], [("ps", bo)])
                stt(X[:, d, t0:t0 + n], ps[:, bo, 0:n], AB[:, l, cj, 1, 2, d:d + 1], X[:, d, t0:t0 + n], ALU.mult, ALU.add,
                    [("ps", bo), ("AB", l), ("X", d, t0)], [("X", d, t0)])

        DEPTH = 3
        pend = []
        due = []
        for si, step in enumerate(allsteps):
            while due and due[0][0] <= si:
                _, t0_, n_ = due.pop(0)
                wout_attn(t0_, n_)
            pend.append((si, step, qk(step)))
            if len(pend) > DEPTH:
                pi, pstep, ppts = pend.pop(0)
                pv(pstep, ppts)
                for gi, (first, ntile0, t0_, n_, last) in enumerate(ginfo):
                    if pi == last:
                        if gi + 1 < len(ginfo):
                            nfirst, nnt0 = ginfo[gi + 1][0], ginfo[gi + 1][1]
                            due.append((nfirst + min(DEPTH + 4, nnt0 // 2 - 1 + DEPTH), t0_, n_))
                        else:
                            due.append((10 ** 9, t0_, n_))
        while pend:
            pi, pstep, ppts = pend.pop(0)
            pv(pstep, ppts)
            for gi, (first, ntile0, t0_, n_, last) in enumerate(ginfo):
                if pi == last:
                    due.append((10 ** 9, t0_, n_))
        for (_, t0_, n_) in due:
            wout_attn(t0_, n_)
        release_unit()

    def mkgroups(T):
        g = []
        t = 0
        while t < T:
            n = min(512, T - t)
            g.append((t, n))
            t += n
        return g

    def big_switch():
        P.add("dve", lambda e: e.memset(scr[:, :], 0.0), (), ["BIG"])

    def init_big(is_sample, T):
        big_switch()
        seqs_ = [(0, TS)] if is_sample else [(s_ * 256, 256) for s_ in range(4)]
        ut = [("upad", ch, g0) for ch in (0, 1) for (g0, _) in mkgroups(T)]
        gt_ = [("gpad", ch, g0) for ch in (0, 1) for (g0, _) in mkgroups(T)]
        for si, (s0, sl) in enumerate(seqs_):
            a = s0 + PADW * (2 * si + 1)
            for (c0, c1) in ((a - PADW, a), (a + sl, a + sl + PADW)):
                vmemset(upad[:, :, c0:c1], 0.0, ut)
                vmemset(gpad[:, :, c0:c1], 0.0, gt_)
        vt = [("vaug", t) for t in range(18)] + [("vaug", "cache")]
        vmemset(vaug[:, :, 0, 64:128], 1.0, vt)
        vmemset(vaug[:, :, 1, 0:64], 1.0, vt)
        ktoks = [("kst", g0) for (g0, _) in mkgroups(T)] + [("kst", "cache")]
        vmemset(kz[0][64:128, :], 0.0, ktoks)
        vmemset(kz[1][0:64, :], 0.0, ktoks)

    for it in plan:
        if it[0] == "mod":
            mod_enqueue(it[1])
            if it[1] == 0:
                mod_pump(12)
                _load_ready()
                mod_cap[0] = 10 ** 9
            continue
        phase = it[1]
        is_sample = phase == 1
        T = TS if is_sample else TP
        xin = xs_d if is_sample else xp_d
        yout = ys_d if is_sample else yp_d
        groups = mkgroups(T)
        seqs = [(0, TS)] if is_sample else [(s * 256, 256) for s in range(4)]
        if it[0] == "load":
            if is_sample:
                mod_ensure(1, 2)
            for c in range(NCH):
                dma("sp", X[:, c, 0:T], xin[c], "x_in%d" % c, (), [("X", c, g0) for (g0, _) in groups])
        elif it[0] == "store":
            for c in range(NCH):
                dma("sp", yout[c], X[:, c, 0:T], "y_out%d" % c, [("X", c, g0) for (g0, _) in groups], [])
        elif it[0] == "ffn1":
            mod_ensure(it[2], 0)
            ffn(it[2], 0, phase, groups, hook=lambda: mod_pump(5))
        elif it[0] == "ffn2":
            mod_ensure(it[2], 2)
            ffn(it[2], 1, phase, groups, hook=lambda: mod_pump(5))
        elif it[0] == "mixer":
            mod_ensure(it[2], 1)
            mixer(it[2], phase, groups, seqs, is_sample)
            big_switch()

    final_keys = ["y_out%d" % c for c in range(NCH)] + ["o_nk", "o_nv"]
    P.emit(nc, es, final_keys)
    es.close()
    nc._prog_stats = P.stats
    return nc


_NC_CACHE = {}


def _head_perm():
    cols = []
    for j in range(4):
        cols += list(range(j * 64, j * 64 + 64))
        cols += list(range((4 + j) * 64, (4 + j) * 64 + 64))
    return np.array(cols)


def _consts():
    cm = np.zeros((128, 5, 128), np.float32)
    cm[:, 0, :] = 1.0 / 1024
    for b in range(2):
        cm[64 * b:64 * b + 64, 1, 64 * b:64 * b + 64] = 1.0 / 64
    cm[:, 2, :] = 1.0 / 256
    cm[:, 3, :] = np.eye(128, dtype=np.float32)
    for m in range(128):
        d = m % 32
        partner = m + 16 if d < 16 else m - 16
        cm[partner, 4, m] = 1.0
    k = np.arange(128)[:, None]
    q = np.arange(128)[None, :]
    mprev = (k >= q).astype(np.float32)
    mnext = (k <= q).astype(np.float32)
    masks = np.stack([mprev, mnext], axis=1)
    sinkl = np.zeros((1, 2, 128), np.float32)
    sinkl[0, 0, 64:128] = 1.0
    sinkl[0, 1, 0:64] = 1.0
    edge = np.zeros((128, 2, 2, 8), np.float32)
    wins = {(0, 0): 2, (0, 1): 4, (1, 0): 8, (1, 1): 16}
    for (ch, half), w in wins.items():
        for i in range(8):
            cs = (i + w // 2) - max(i - w // 2, 0)
            r = 8 - i
            ce = min(w // 2, r) + w // 2
            edge[64 * half:64 * half + 64, ch, 0, i] = 1.0 / cs
            edge[64 * half:64 * half + 64, ch, 1, i] = 1.0 / ce
    return cm, masks, sinkl, edge


def _rope_tables(pos0):
    pos = pos0 + np.arange(TS)
    row = (pos // 64).astype(np.float64)
    col = (pos % 64).astype(np.float64)
    half = 32
    inv = 10000.0 ** (-np.arange(0, half, 2, dtype=np.float64) / half)
    cos = np.zeros((128, TS), np.float32)
    sin = np.zeros((128, TS), np.float32)
    for p in range(128):
        d = p % 64
        posv = row if d < 32 else col
        dd = d % 32
        i = dd % 16
        ang = posv * inv[i]
        cos[p] = np.cos(ang)
        sin[p] = np.sin(ang) * (-1.0 if dd < 16 else 1.0)
    return cos, sin


def kernel(x_prompt, x_sample, cache_k, cache_v, c, c_ctx, mod_w, mod_b, norm_g,
           ffn1_wi, ffn1_wo, ffn2_wi, ffn2_wo, w_in, w_out, pool_w, pool_scale,
           conv_dw, conv_b, conv_norm_g, conv_pw, q_norm_g, k_norm_g, sink, _only_core=None, _debug_stop=None):
    f = lambda a: np.ascontiguousarray(np.asarray(a, dtype=np.float32))
    x_prompt, x_sample, cache_k, cache_v = f(x_prompt), f(x_sample), f(cache_k), f(cache_v)
    if "nc" not in _NC_CACHE:
        _NC_CACHE["nc"] = build_program()
    nc = _NC_CACHE["nc"]
    hp = _head_perm()
    w_in_p = f(w_in).copy()
    w_in_p[:, :, 768:1280] = f(w_in)[:, :, 768 + hp]
    w_out_p = f(w_out).copy()
    w_out_p[:, 512:1024, :] = f(w_out)[:, 512 + hp, :]
    modb_l = f(np.asarray(mod_b).reshape(2, 72, 128).transpose(0, 2, 1))
    ng_l = f(np.asarray(norm_g).reshape(2, 3, NCH, 128).transpose(0, 3, 1, 2))
    small = np.zeros((2, 128, 16), np.float32)
    small[:, :, 0:2] = np.asarray(pool_scale).reshape(2, 2, 128).transpose(0, 2, 1)
    small[:, :, 2:4] = np.asarray(conv_b).reshape(2, 2, 128).transpose(0, 2, 1)
    small[:, :, 4:6] = np.asarray(conv_norm_g).reshape(2, 2, 128).transpose(0, 2, 1)
    small[:, :, 6] = np.tile(np.asarray(q_norm_g), (1, 2))
    small[:, :, 7] = np.tile(np.asarray(k_norm_g), (1, 2))
    dw_l = f(np.asarray(conv_dw).reshape(2, 31, 2, 128).transpose(0, 3, 2, 1))
    sk = np.asarray(sink, dtype=np.float32)
    sink_b = f(np.broadcast_to(sk[:, None, :], (2, 128, 8)))
    cm, masks, sinkl, edge = _consts()
    shared = dict(mod_w=f(mod_w), mod_b=modb_l, norm_g=ng_l, ffn1_wi=f(ffn1_wi), ffn2_wi=f(ffn2_wi),
                  ffn1_wo=f(ffn1_wo), ffn2_wo=f(ffn2_wo), w_in=w_in_p, w_out=w_out_p, pool_w=f(pool_w),
                  small=small, conv_dw=dw_l, conv_pw=f(conv_pw), sink_b=sink_b, cmat=cm, masks=masks,
                  pool_edge=edge)
    in_maps = []
    starts = []
    for core in (range(8) if _only_core is None else [_only_core]):
        b, hf = core // 2, core % 2
        s0 = 0 if hf == 0 else 4096 - TS
        starts.append(s0)
        xp = x_prompt[4 * core:4 * core + 4].reshape(TP, D).T.reshape(NCH, 128, TP)
        xs = x_sample[b, s0:s0 + TS].T.reshape(NCH, 128, TS)
        cond = np.stack([np.asarray(c_ctx, np.float32), np.asarray(c, np.float32)[b]], axis=1)
        cond = cond.reshape(NCH, 128, 2).transpose(1, 0, 2)
        ckT = cache_k[b].reshape(2, PAST, 128).transpose(0, 2, 1)
        cv = cache_v[b].reshape(2, PAST, 128)
        cos, sin = _rope_tables(s0)
        m = dict(shared)
        m.update(xp=f(xp), xs=f(xs), cond=f(cond), cache_kT=f(ckT), cache_v=f(cv), rope_cos=cos, rope_sin=sin)
        in_maps.append(m)
    if _only_core is not None:
        nc = build_program(_debug_stop)
        res = run_bass_kernel_spmd(nc, in_maps, core_ids=[0])
        print("EXEC_NS", res.exec_time_ns, nc._prog_stats)
        return res.results[0]
    res = run_bass_kernel_spmd(nc, in_maps, core_ids=list(range(8)))
    y_prompt = np.zeros((32, 256, D), np.float32)
    y_sample = np.zeros((4, 4096, D), np.float32)
    nk = np.zeros((32, 2, 256, 2, 64), np.float32)
    nv = np.zeros((32, 2, 256, 2, 64), np.float32)
    for core in range(8):
        r = res.results[core]
        b, hf = core // 2, core % 2
        yp = np.asarray(r["yp"]).reshape(D, TP).T.reshape(4, 256, D)
        y_prompt[4 * core:4 * core + 4] = yp
        ys = np.asarray(r["ys"]).reshape(D, TS).T
        if hf == 0:
            y_sample[b, 0:2048] = ys[0:2048]
        else:
            y_sample[b, 2048:4096] = ys[TS - 2048:TS]
        k_ = np.asarray(r["nk"]).transpose(0, 2, 1).reshape(2, 4, 256, 2, 64)
        v_ = np.asarray(r["nv"]).reshape(2, 4, 256, 2, 64)
        nk[4 * core:4 * core + 4] = k_.transpose(1, 0, 2, 3, 4)
        nv[4 * core:4 * core + 4] = v_.transpose(1, 0, 2, 3, 4)
    return (y_prompt, y_sample, nk, nv)
```

```python
import numpy as np
from contextlib import ExitStack
import concourse.bass as bass
import concourse.mybir as mybir
from concourse.bass_utils import run_bass_kernel_spmd

F32 = mybir.dt.float32
BF16 = mybir.dt.bfloat16
AF = mybir.ActivationFunctionType
ALU = mybir.AluOpType

D = 1024
NCH = 8
DFF = 2816
NFC = 22
TP = 1024
TS = 2304
PAST = 512
PADW = 16
EPS = 1e-6
SLOT = 9216
FBLOCKS = [3, 3, 3, 3, 3, 3, 3, 1]
NTMP = 8


class Prog:
    def __init__(self):
        self.ops = []
        self.lastw = {}
        self.readers = {}
        self.ambient = True

    def add(self, eng, fn, reads=(), writes=(), dma=None):
        i = len(self.ops)
        def _exp(lst):
            o = []
            for t in lst:
                if isinstance(t, tuple) and len(t) == 2 and t[0] == "slotpair":
                    o += [("slot", t[1], "a"), ("slot", t[1], "b")]
                else:
                    o.append(t)
            return o
        reads = _exp(reads)
        writes = _exp(writes)
        if self.ambient and eng in ("pe", "act", "dve") and dma is None and "BIG" not in writes:
            reads.append("BIG")
        deps = set()
        for t in reads:
            w = self.lastw.get(t)
            if w is not None:
                deps.add(w)
        for t in writes:
            w = self.lastw.get(t)
            if w is not None:
                deps.add(w)
            deps.update(self.readers.get(t, ()))
        red = {}
        for d in deps:
            p = self.ops[d]
            k = ("dma", p["dma"]) if p["dma"] is not None else ("eng", p["eng"])
            if k not in red or d > red[k]:
                red[k] = d
        deps = set(red.values())
        self.ops.append(dict(eng=eng, fn=fn, deps=deps, dma=dma, val=None))
        for t in reads:
            self.readers.setdefault(t, []).append(i)
        for t in writes:
            self.lastw[t] = i
            self.readers[t] = []
        return i

    def emit(self, nc, es, final_keys):
        ops = self.ops
        needed = set()
        for op in ops:
            for d in op["deps"]:
                p = ops[d]
                if p["dma"] is None:
                    if p["eng"] == "pe" and op["eng"] == "pe" and op["dma"] is None:
                        continue
                    needed.add(d)
        cnt = {}
        dcnt = {}
        for i, op in enumerate(ops):
            if op["dma"] is not None:
                dcnt[op["dma"]] = dcnt.get(op["dma"], 0) + 16
                op["val"] = dcnt[op["dma"]]
            elif i in needed:
                cnt[op["eng"]] = cnt.get(op["eng"], 0) + 1
                op["val"] = cnt[op["eng"]]
        self.stats = dict(cnt=dict(cnt), dmax=max(dcnt.values()), nops=len(ops), ndma=len(dcnt))
        sems = {}
        for e in ["pe", "act", "dve", "pool", "sp"]:
            sems[("eng", e)] = es.enter_context(nc.semaphore("s_" + e))
        for k in dcnt:
            sems[("dma", k)] = es.enter_context(nc.semaphore("d_" + str(len(sems))))
        block = es.enter_context(nc.Block())
        per = {e: [] for e in ["pe", "act", "dve", "pool", "sp"]}
        for i, op in enumerate(ops):
            per[op["eng"]].append(i)

        def run(ename, eng):
            waited = {}
            for i in per[ename]:
                op = ops[i]
                need = {}
                for d in op["deps"]:
                    p = ops[d]
                    if p["dma"] is not None:
                        key = ("dma", p["dma"])
                    else:
                        if p["eng"] == "pe" and ename == "pe" and op["dma"] is None:
                            continue
                        key = ("eng", p["eng"])
                    v = p["val"]
                    if v > need.get(key, 0):
                        need[key] = v
                todo = []
                for key in sorted(need, key=str):
                    v = need[key]
                    if waited.get(key, 0) >= v:
                        continue
                    todo.append((key, v))
                    waited[key] = v
                for key, v in todo[:-1]:
                    eng.wait_ge(sems[key], v)
                ins = op["fn"](eng)
                if todo:
                    key, v = todo[-1]
                    ins.wait_op(sems[key], v, "sem-ge")
                if op["dma"] is not None:
                    ins.then_inc(sems[("dma", op["dma"])], 16)
                elif op["val"] is not None:
                    ins.then_inc(sems[("eng", ename)], 1)
            if ename == "sp":
                for k in final_keys:
                    if k in dcnt:
                        eng.wait_ge(sems[("dma", k)], dcnt[k])

        @block.tensor
        def _(e):
            run("pe", e)

        @block.scalar
        def _(e):
            run("act", e)

        @block.vector
        def _(e):
            run("dve", e)

        @block.gpsimd
        def _(e):
            run("pool", e)

        @block.sync
        def _(e):
            run("sp", e)


def build_program(debug_stop=None):
    nc = bass.Bass("TRN2", target_bir_lowering=False)
    P = Prog()
    es = ExitStack()

    def din(name, shape, dt=F32):
        return nc.dram_tensor(name, list(shape), dt, kind="ExternalInput").ap()

    def dout(name, shape):
        return nc.dram_tensor(name, list(shape), F32, kind="ExternalOutput").ap()

    xp_d = din("xp", [NCH, 128, TP])
    xs_d = din("xs", [NCH, 128, TS])
    cond_d = din("cond", [128, NCH, 2])
    modw_d = din("mod_w", [2, D, 9 * D])
    modb_d = din("mod_b", [2, 128, 72])
    ng_d = din("norm_g", [2, 128, 3, NCH])
    wi_d = [din("ffn1_wi", [2, D, 2 * DFF]), din("ffn2_wi", [2, D, 2 * DFF])]
    wo_d = [din("ffn1_wo", [2, DFF, D]), din("ffn2_wo", [2, DFF, D])]
    win_d = din("w_in", [2, D, 1536])
    wout_d = din("w_out", [2, D, D])
    poolw_d = din("pool_w", [2, 4, 64, 64])
    small_d = din("small", [2, 128, 16])
    dw_d = din("conv_dw", [2, 128, 2, 31])
    pw_d = din("conv_pw", [2, 256, 256])
    sink_d = din("sink_b", [2, 128, 8])
    ckT_d = din("cache_kT", [2, 128, PAST])
    cv_d = din("cache_v", [2, PAST, 128])
    cos_d = din("rope_cos", [128, TS])
    sin_d = din("rope_sin", [128, TS])
    cmat_d = din("cmat", [128, 5, 128])
    mask_d = din("masks", [128, 2, 128])
    edge_d = din("pool_edge", [128, 2, 2, 8])

    yp_d = dout("yp", [NCH, 128, TP])
    ys_d = dout("ys", [NCH, 128, TS])
    nk_d = dout("nk", [2, 128, TP])
    nv_d = dout("nv", [2, TP, 128])

    def sb(name, shape, dt):
        return es.enter_context(nc.sbuf_tensor(name, list(shape), dt))

    X = sb("X", [128, NCH, TS], F32)
    BIG = sb("BIG", [128, 29824], BF16)
    slots = [sb("slot0", [128, SLOT], BF16), sb("slot1", [128, SLOT], BF16)]
    tmps = [sb("tmp%d" % i, [128, 528], F32) for i in range(NTMP)]
    hgrp = sb("hgrp", [128, NCH, 512], BF16)
    cmat = sb("cmatb", [128, 5, 128], BF16)
    masks = sb("masksb", [128, 2, 128], BF16)
    esb = sb("esb", [128, 8], F32)
    sinkraw = sb("sinkraw", [128, 8], F32)
    condf = sb("condf", [128, NCH, 2], F32)
    condb = sb("condb", [128, NCH, 2], BF16)
    modv = [sb("modv%d" % l, [128, 72, 2], F32) for l in range(2)]
    modb = sb("modb", [128, 2, 72], F32)
    ngt = sb("ngt", [128, 2, 3, NCH], F32)
    AB = sb("ABt", [128, 2, 2, 3, 3, NCH], F32)
    smallt = sb("smallt", [128, 16], F32)
    dwt = sb("dwt", [128, 2, 31], F32)
    edget = sb("edget", [128, 2, 2, 8], F32)
    opc = sb("omix", [128, 4, 512], BF16)
    oattn = opc
    scr = sb("scr", [128, 2], F32)
    rv = sb("rv", [128, 1024], F32)
    ropec = rv[:, 0:512]
    ropes = rv[:, 512:1024]
    vout = rv[:, :].rearrange("p (t f) -> p t f", t=8)
    ps = es.enter_context(nc.psum_tensor("ps", [128, 8, 512], F32))

    Hv = BIG[:, 0:NCH * TS].rearrange("p (c t) -> p c t", c=NCH)
    actb = [BIG[:, NCH * TS + i * 1536:NCH * TS + (i + 1) * 1536].rearrange("p (c t) -> p c t", c=3) for i in range(2)]
    o = 0
    qst = BIG[:, o:o + 4 * TS].rearrange("p (c t) -> p c t", c=4); o += 4 * TS
    KW = TS + PAST
    kz = []
    for _ in range(2):
        kz.append(BIG[:, o:o + KW]); o += KW
    NVT = 22
    vaug = BIG[:, o:o + NVT * 256].rearrange("p (t g f) -> p t g f", t=NVT, g=2); o += NVT * 256
    TPAD = TS + 2 * PADW
    upad = BIG[:, o:o + 2 * TPAD].rearrange("p (c t) -> p c t", c=2); o += 2 * TPAD
    gpad = BIG[:, o:o + 2 * TPAD].rearrange("p (c t) -> p c t", c=2); o += 2 * TPAD
    assert o <= 29824, o

    tctr = [0]

    def tmp():
        i = tctr[0] % NTMP
        tctr[0] += 1
        return tmps[i], ("tmp", i)

    def tmpb(t):
        return t

    def mm(out, lhsT, rhs, start, stop, reads, writes):
        P.add("pe", lambda e: e.matmul(out, lhsT, rhs, start=start, stop=stop), reads, writes)

    def act(out, in_, func, reads, writes, scale=1.0, bias=0.0):
        P.add("act", lambda e: e.activation(out=out, in_=in_, func=func, scale=scale, bias=bias), reads, writes)

    def tt(out, in0, in1, op, reads, writes):
        P.add("dve", lambda e: e.tensor_tensor(out=out, in0=in0, in1=in1, op=op), reads, writes)

    def stt(out, in0, scalar, in1, op0, op1, reads, writes):
        P.add("dve", lambda e: e.scalar_tensor_tensor(out=out, in0=in0, scalar=scalar, in1=in1, op0=op0, op1=op1),
              reads, writes)

    def ts(out, in0, s1, op0, reads, writes, s2=None, op1=None):
        if op1 is None:
            P.add("dve", lambda e: e.tensor_scalar(out=out, in0=in0, scalar1=s1, scalar2=None, op0=op0), reads, writes)
        else:
            P.add("dve", lambda e: e.tensor_scalar(out=out, in0=in0, scalar1=s1, scalar2=s2, op0=op0, op1=op1),
                  reads, writes)

    def vcopy(out, in_, reads, writes):
        P.add("dve", lambda e: e.tensor_copy(out=out, in_=in_), reads, writes)

    def vmemset(ap, val, writes):
        P.add("dve", lambda e: e.memset(ap, val), (), writes)

    def dma(q, out, in_, key, reads, writes):
        P.add(q, lambda e: e.dma_start(out=out, in_=in_), reads, writes, dma=key)

    dma("pool", cmat[:], cmat_d[:, :, :], "c_cmat", (), ["cmat"])
    dma("pool", masks[:], mask_d[:, :, :], "c_mask", (), ["masks"])
    dma("sp", condf[:], cond_d[:, :, :], "c_cond", (), ["condf"])
    dma("sp", modb[:, 0, :], modb_d[0], "c_modb0", (), ["modb"])
    dma("sp", modb[:, 1, :], modb_d[1], "c_modb1", (), ["modb1"])
    dma("sp", ngt[:, 0], ng_d[0], "c_ng0", (), ["ngt"])
    dma("sp", ngt[:, 1], ng_d[1], "c_ng1", (), ["ngt1"])
    dma("sp", edget[:], edge_d[:, :, :, :], "c_edge", (), ["edget"])
    act(condb[:], condf[:], AF.Silu, ["condf"], ["condb"])
    ONES_MEAN = cmat[:, 0, :]
    BLK64 = cmat[:, 1, :]
    ONES256 = cmat[:, 2, :]
    IDENT = cmat[:, 3, :]
    PERM = cmat[:, 4, :]

    units = []

    def wi_src(l, which, f0, nf):
        v = wi_d[which][l].rearrange("(kc p) f -> p kc f", p=128)
        return v[:, :, f0 * 128:(f0 + nf) * 128], v[:, :, DFF + f0 * 128:DFF + (f0 + nf) * 128]

    for_units = []

    def plan_units():
        seq = []
        for l in range(1):
            pass
        return seq

    ucount = [0]
    ucursor = [0]
    ureleased = [0]

    def unit_views(kind, s, nf=3):
        sl = slots[s]
        if kind == "F":
            wi = sl[:, 0:8 * 2 * nf * 128].rearrange("p (k f) -> p k f", k=8)
            wo = sl[:, 6144:6144 + nf * 1024].rearrange("p (c f) -> p c f", c=nf)
            return wi, wo
        if kind == "M":
            return sl[:, 0:9216].rearrange("p (k f) -> p k f", k=8)
        if kind == "WIN":
            return sl[:, 0:6144].rearrange("p (k f) -> p k f", k=8)
        if kind == "WOUT":
            wout = sl[:, 0:8192].rearrange("p (k f) -> p k f", k=8)
            pw = sl[:, 8192:8704].rearrange("p (k f) -> p k f", k=2)
            pb = sl[:, 8704:8960].rearrange("p (k f) -> p k f", k=2)
            return wout, pw, pb
        if kind == "DIAG":
            return sl[:, 0:7936].rearrange("p (c j f) -> p c j f", c=2, j=31)
        raise ValueError(kind)

    def stoks(s):
        return [("slot", s, "a"), ("slot", s, "b")]

    def load_unit(u):
        idx, kind, l, arg = u
        s = idx % 2
        both = stoks(s)
        if kind == "F":
            which, f0, nf = arg
            wi, wo = unit_views("F", s, nf)
            g_src, u_src = wi_src(l, which, f0, nf)
            dma("pool", wi[:, :, 0:nf * 128], g_src, "u%d_a" % s, (), [("slot", s, "a")])
            dma("pool", wi[:, :, nf * 128:2 * nf * 128], u_src, "u%d_b" % s, (), [("slot", s, "a")])
            wsrc = wo_d[which][l][f0 * 128:(f0 + nf) * 128, :].rearrange("(c p) f -> p c f", p=128)
            dma("pool", wo, wsrc, "u%d_c" % s, (), [("slot", s, "b")])
        elif kind == "M":
            j = arg
            mv = unit_views("M", s)
            src = modw_d[l].rearrange("(kc p) f -> p kc f", p=128)[:, :, j * 1152:(j + 1) * 1152]
            dma("pool", mv, src, "u%d_a" % s, (), both)
        elif kind == "WIN":
            j = arg
            wv = unit_views("WIN", s)
            src = win_d[l].rearrange("(kc p) f -> p kc f", p=128)[:, :, j * 768:(j + 1) * 768]
            dma("pool", wv, src, "u%d_a" % s, (), both)
        elif kind == "WOUT":
            wout, pw, pb = unit_views("WOUT", s)
            dma("pool", wout, wout_d[l].rearrange("(kc p) f -> p kc f", p=128), "u%d_a" % s, (), both)
            dma("pool", pw, pw_d[l].rearrange("(kc p) f -> p kc f", p=128), "u%d_b" % s, (), both)
            P.add("pool", lambda e, pb=pb: e.memset(pb, 0.0), (), both)
            for gi in range(4):
                ch, half = gi // 2, gi % 2
                dma("pool", pb[64 * half:64 * half + 64, ch, 64 * half:64 * half + 64], poolw_d[l, gi],
                    "u%d_p%d" % (s, gi), (), both)
        elif kind == "DIAG":
            pass
        else:
            raise ValueError(kind)

    def _load_ready():
        while ucount[0] < len(units) and ucount[0] < ureleased[0] + 2:
            load_unit(units[ucount[0]])
            ucount[0] += 1

    def next_unit(expect_kind):
        u = units[ucursor[0]]
        assert u[1] == expect_kind, (u, expect_kind)
        _load_ready()
        assert ucount[0] > ucursor[0], ("unit not loadable yet", u, ureleased[0])
        ucursor[0] += 1
        return u[0] % 2, ("slotpair", u[0] % 2), u

    def release_unit():
        ureleased[0] += 1
        _load_ready()

    def add_units(kind_list):
        for (kind, l, arg) in kind_list:
            units.append((len(units), kind, l, arg))

    def layer_units(l):
        r = []
        f0 = 0
        for nf in FBLOCKS:
            r.append(("F", l, (0, f0, nf)))
            f0 += nf
        r += [("WIN", l, 0), ("WIN", l, 1), ("WOUT", l, None), ("DIAG", l, None)]
        f0 = 0
        for nf in FBLOCKS:
            r.append(("F", l, (1, f0, nf)))
            f0 += nf
        return r

    k0, k1 = (6, 6) if debug_stop is None else debug_stop
    plan = [("mod", 0)]
    need_mod1 = False
    if max(k0, k1) > 3:
        plan.append(("mod", 1))
    for phase, kk in ((0, k0), (1, k1)):
        if kk < 0:
            continue
        plan.append(("load", phase))
        for st in range(kk):
            l, sub = st // 3, st % 3
            if l == 1 and need_mod1:
                plan.append(("mod", 1))
                need_mod1 = False
            plan.append((("ffn1", "mixer", "ffn2")[sub], phase, l))
        plan.append(("store", phase))
    for it in plan:
        if it[0] == "mod":
            pass
        elif it[0] in ("ffn1", "ffn2"):
            f0 = 0
            for nf in FBLOCKS:
                add_units([("F", it[2], (0 if it[0] == "ffn1" else 1, f0, nf))])
                f0 += nf
        elif it[0] == "mixer":
            add_units([("WIN", it[2], 0), ("WIN", it[2], 1), ("DIAG", it[2], None), ("WOUT", it[2], None)])

    mod_q = []
    mod_loaded = [0]
    mod_done = [0]
    mod_list = []

    def mini_view(slot):
        return X[:, slot, 1024:2048].bitcast(BF16).rearrange("p (k f) -> p k f", k=8)

    def mini_toks(slot):
        return [("X", slot, 1024), ("X", slot, 1536)]

    def mod_enqueue(l):
        for m in range(36):
            mod_list.append((l, m))

    mod_cap = [12]

    def _mod_load_ahead():
        while mod_loaded[0] < len(mod_list) and mod_loaded[0] < min(mod_done[0] + 8, mod_cap[0]):
            l, m = mod_list[mod_loaded[0]]
            slot = mod_loaded[0] % 8
            src = modw_d[l].rearrange("(kc p) f -> p kc f", p=128)[:, :, m * 256:(m + 1) * 256]
            dma("pool", mini_view(slot), src, "mm%d" % slot, (), mini_toks(slot))
            mod_loaded[0] += 1

    def compute_AB(l, i):
        ng = "ngt" if l == 0 else "ngt1"
        for cj in range(2):
            sh = modv[l][:, (3 * i) * 8:(3 * i) * 8 + 8, cj]
            sc = modv[l][:, (3 * i + 1) * 8:(3 * i + 1) * 8 + 8, cj]
            gg = modv[l][:, (3 * i + 2) * 8:(3 * i + 2) * 8 + 8, cj]
            A = AB[:, l, cj, i, 0, :]
            B = AB[:, l, cj, i, 1, :]
            G = AB[:, l, cj, i, 2, :]
            stt(A, sc, 1.0, ngt[:, l, i, :], ALU.add, ALU.mult, [("modv", l), ng], [("AB", l)])
            vcopy(B, sh, [("modv", l)], [("AB", l)])
            ts(G, gg, 0.5 if i != 1 else 1.0, ALU.mult, [("modv", l)], [("AB", l)])

    def mod_pump(n):
        for _ in range(n):
            if mod_done[0] >= len(mod_list):
                return
            _mod_load_ahead()
            idx = mod_done[0]
            l, m = mod_list[idx]
            slot = idx % 8
            mv = mini_view(slot)
            mb = "modb" if l == 0 else "modb1"
            bank = 6 + (idx % 2)
            col0 = 0
            for cc in range(2):
                for k in range(NCH):
                    mm(ps[:, bank, col0 + 2 * cc:col0 + 2 * cc + 2], mv[:, k, cc * 128:(cc + 1) * 128], condb[:, k, :],
                       k == 0, k == NCH - 1, mini_toks(slot) + ["condb"], [("ps", bank)])
            for cc in range(2):
                cg = 2 * m + cc
                ts(modv[l][:, cg, :], ps[:, bank, col0 + 2 * cc:col0 + 2 * cc + 2], modb[:, l, cg:cg + 1], ALU.add,
                   [("ps", bank), mb], [("modv", l)])
            mod_done[0] += 1
            _mod_load_ahead()
            if m % 12 == 11:
                compute_AB(l, m // 12)

    def mod_ensure(l, i):
        while mod_done[0] < len(mod_list) and mod_list[mod_done[0]] <= (l, 12 * i + 11):
            mod_pump(1)

    def norm_mod(l, cj, i, t0, n, dst, dst_tok_fn):
        msb = 6
        for c in range(NCH):
            sq, sqt = tmp()
            sqv = sq.bitcast(BF16)[:, 0:n]
            act(sqv, X[:, c, t0:t0 + n], AF.Square, [("X", c, t0)], [sqt])
            mm(ps[:, msb, 0:n], ONES_MEAN, sqv, c == 0, c == NCH - 1, [sqt, "cmat"], [("ps", msb)])
        ln, lnt = tmp()
        act(ln[:, 0:n], ps[:, msb, 0:n], AF.Ln, [("ps", msb)], [lnt], bias=EPS)
        rs, rst = tmp()
        act(rs[:, 0:n], ln[:, 0:n], AF.Exp, [lnt], [rst], scale=-0.5)
        for c in range(NCH):
            t, ttok = tmp()
            stt(t[:, 0:n], X[:, c, t0:t0 + n], AB[:, l, cj, i, 0, c:c + 1], rs[:, 0:n], ALU.mult, ALU.mult,
                [("X", c, t0), ("AB", l), rst], [ttok])
            act(dst[:, c, 0:n] if dst is hgrp else dst[:, c, t0:t0 + n], t[:, 0:n], AF.Identity,
                [ttok, ("AB", l)], [dst_tok_fn(c)], bias=AB[:, l, cj, i, 1, c:c + 1])

    def ffn(l, which, cj, groups, hook=None):
        i = 0 if which == 0 else 2
        LOOK = 1
        for (t0, n) in groups[:LOOK]:
            norm_mod(l, cj, i, t0, n, Hv, lambda c, t0=t0: ("H", c, t0))
        items = []
        f0 = 0
        blk_info = []
        for bi, nf in enumerate(FBLOCKS):
            for gi, (t0, n) in enumerate(groups):
                items.append((bi, nf, f0, gi, t0, n))
            f0 += nf
        state = {"cur_blk": -1, "views": None, "tok": None}
        gu_ctr = [0]
        blk_views = {}

        def GU(it, ab):
            bi, nf, f0, gi, t0, n = it
            if bi not in blk_views:
                s, tok, u = next_unit("F")
                blk_views[bi] = (unit_views("F", s, nf), tok)
            (wi, wo), tok = blk_views[bi]
            for fc in range(nf):
                pr = gu_ctr[0] % 2
                gu_ctr[0] += 1
                bg, bu = 2 * pr, 2 * pr + 1
                for k in range(NCH):
                    mm(ps[:, bg, 0:n], wi[:, k, fc * 128:(fc + 1) * 128], Hv[:, k, t0:t0 + n], k == 0, k == NCH - 1,
                       [("slot", tok[1], "a"), ("H", k, t0)], [("ps", bg)])
                for k in range(NCH):
                    mm(ps[:, bu, 0:n], wi[:, k, (nf + fc) * 128:(nf + fc + 1) * 128], Hv[:, k, t0:t0 + n], k == 0,
                       k == NCH - 1, [("slot", tok[1], "a"), ("H", k, t0)], [("ps", bu)])
                sil, silt = tmp()
                act(sil[:, 0:n], ps[:, bg, 0:n], AF.Silu, [("ps", bg)], [silt])
                tt(actb[ab][:, fc, 0:n], ps[:, bu, 0:n], sil[:, 0:n], ALU.mult, [("ps", bu), silt], [("actb", ab, fc)])

        def WO(it, ab, last_of_block):
            bi, nf, f0, gi, t0, n = it
            (wi, wo), tok = blk_views[bi]
            for d in range(NCH):
                bo = 4 + (d % 2)
                for fc in range(nf):
                    mm(ps[:, bo, 0:n], wo[:, fc, d * 128:(d + 1) * 128], actb[ab][:, fc, 0:n], fc == 0, fc == nf - 1,
                       [("slot", tok[1], "b"), ("actb", ab, fc)], [("ps", bo)])
                stt(X[:, d, t0:t0 + n], ps[:, bo, 0:n], AB[:, l, cj, i, 2, d:d + 1], X[:, d, t0:t0 + n], ALU.mult, ALU.add,
                    [("ps", bo), ("AB", l), ("X", d, t0)], [("X", d, t0)])
            if last_of_block:
                release_unit()
                if hook is not None:
                    hook()

        ng = len(groups)
        for ii, it in enumerate(items):
            if it[0] == 0 and it[3] + LOOK < ng:
                (t0_, n_) = groups[it[3] + LOOK]
                norm_mod(l, cj, i, t0_, n_, Hv, lambda c, t0_=t0_: ("H", c, t0_))
            GU(it, ii % 2)
            if ii > 0:
                pit = items[ii - 1]
                WO(pit, (ii - 1) % 2, pit[3] == ng - 1)
        pit = items[-1]
        WO(pit, (len(items) - 1) % 2, True)

    def mixer(l, cj, groups, seqs, is_sample):
        T = sum(n for _, n in groups)
        ntiles = T // 128
        stok = "small%d" % 0

        def padcol(t):
            for si, (s0, sl) in enumerate(seqs):
                if s0 <= t < s0 + sl:
                    return t + PADW * (2 * si + 1)
            raise ValueError(t)

        dma("sp", smallt[:], small_d[l], "c_small", (), ["smallt"])
        dma("sp", dwt[:], dw_d[l], "c_dw", (), ["dwt"])
        dma("sp", sinkraw[:], sink_d[l], "c_sink", (), ["sinkraw"])
        act(esb[:], sinkraw[:], AF.Exp, ["sinkraw"], ["esb"])
        def load_cache():
            if is_sample:
                dma("pool", kz[0][0:64, TS:TS + PAST], ckT_d[l][0:64, :], "c_ck", (), [("kst", "cache")])
                dma("pool", kz[1][64:128, TS:TS + PAST], ckT_d[l][64:128, :], "c_ck1", (), [("kst", "cache")])
                cvv = cv_d[l].rearrange("(t p) f -> p t f", p=128)
                dma("pool", vaug[:, 18:22, 0, 0:64], cvv[:, :, 0:64], "c_cv0", (), [("vaug", "cache")])
                dma("pool", vaug[:, 18:22, 1, 64:128], cvv[:, :, 64:128], "c_cv1", (), [("vaug", "cache")])

        PSC = lambda c: smallt[:, c:c + 1]
        CB = lambda c: smallt[:, 2 + c:3 + c]
        CNG = lambda c: smallt[:, 4 + c:5 + c]
        QG = smallt[:, 6:7]
        KG = smallt[:, 7:8]

        sA, tokA, _ = next_unit("WIN")
        winA = unit_views("WIN", sA)
        sB, tokB, _ = next_unit("WIN")
        winB = unit_views("WIN", sB)

        def wcol(ch):
            if ch < 6:
                return winA[:, :, ch * 128:(ch + 1) * 128], tokA
            return winB[:, :, (ch - 6) * 128:(ch - 5) * 128], tokB

        HN = 264
        items = []
        for (g0, gn) in groups:
            for off in range(0, gn, 256):
                items.append((g0 + off, g0))
        all_tmp_toks = [("tmp", i) for i in range(NTMP)] + [("tmph", i, h) for i in range(NTMP) for h in range(2)]

        def tmp_barrier():
            P.add("dve", lambda e: e.memset(scr[:, :], 0.0), (), all_tmp_toks)

        tmp_barrier()
        hctr = {"n": 0, "c": 0}
        pools = {"n": [0, 1], "c": [6, 7]}
        HSLOT = [(ti, h) for ti in (2, 3, 4, 5) for h in (0, 1)]

        def qn_slot(k):
            ti, h = HSLOT[k]
            return tmps[ti][:, h * HN:h * HN + 256], ("tmph", ti, h)

        def qb_slot(k):
            ti, h = HSLOT[5 + k // 2]
            o = 2 * h * HN + (k % 2) * 272
            return tmps[ti].bitcast(BF16)[:, o:o + 256], ("tmph", ti, h)

        def half(pool):
            lst = pools[pool]
            k = hctr[pool] % (2 * len(lst))
            hctr[pool] += 1
            ti, h = lst[k // 2], k % 2
            return tmps[ti][:, h * HN:h * HN + 256], tmps[ti].bitcast(BF16)[:, 2 * h * HN:2 * h * HN + 256], ("tmph", ti, h)

        def hb(i):
            return hgrp[:, :, (i % 2) * 256:(i % 2) * 256 + 256]

        def norm1(i):
            t0 = items[i][0]
            for c in range(NCH):
                _, sqv, sqt = half("n")
                act(sqv, X[:, c, t0:t0 + 256], AF.Square, [("X", c, items[i][1])], [sqt])
                mm(ps[:, 4, 0:256], ONES_MEAN, sqv, c == 0, c == NCH - 1, [sqt, "cmat"], [("ps", 4)])

        def norm2(i):
            t0 = items[i][0]
            lnv, _, lnt = half("c")
            act(lnv, ps[:, 4, 0:256], AF.Ln, [("ps", 4)], [lnt], bias=EPS)
            act(lnv, lnv, AF.Exp, [lnt], [lnt], scale=-0.5)
            for c in range(NCH):
                tv, _, ttok = half("n")
                stt(tv, X[:, c, t0:t0 + 256], AB[:, l, cj, 1, 0, c:c + 1], lnv, ALU.mult, ALU.mult,
                    [("X", c, items[i][1]), ("AB", l), lnt], [ttok])
                act(hb(i)[:, c, :], tv, AF.Identity, [ttok, ("AB", l)], [("hgrp", i % 2, c)], bias=AB[:, l, cj, 1, 1, c:c + 1])

        ubank = [0]

        def proj(i, ch, bank=None):
            if bank is None:
                b = ubank[0] % 4
                ubank[0] += 1
            else:
                b = bank
            wv, wt = wcol(ch)
            for k in range(NCH):
                mm(ps[:, b, 0:256], wv[:, k, :], hb(i)[:, k, :], k == 0, k == NCH - 1, [wt, ("hgrp", i % 2, k)], [("ps", b)])
            return b

        def item_body(i, mid_hook):
            t0 = items[i][0]
            a = padcol(t0)
            if is_sample:
                rc_ = ropec[:, (i % 2) * 256:(i % 2) * 256 + 256]
                rs_ = ropes[:, (i % 2) * 256:(i % 2) * 256 + 256]
                vtk = [("vout", g_) for g_ in range(8)]
                dma("sp", rc_, cos_d[:, t0:t0 + 256], "c_rc%d" % (i % 2), (), [("ropec", i % 2)] + vtk)
                dma("sp", rs_, sin_d[:, t0:t0 + 256], "c_rs%d" % (i % 2), (), [("ropes", i % 2)] + vtk)
            for ch in (0, 1):
                b = proj(i, ch)
                act(upad[:, ch, a:a + 256], ps[:, b, 0:256], AF.Copy, [("ps", b)], [("upad", ch, items[i][1])])
            for cc in (0, 1):
                bgt = proj(i, 4 + cc)
                sgv, _, sgt = half("c")
                act(sgv, ps[:, bgt, 0:256], AF.Exp, [("ps", bgt)], [sgt], scale=-1.0)
                act(sgv, sgv, AF.Ln, [sgt], [sgt], bias=1.0)
                act(sgv, sgv, AF.Exp, [sgt], [sgt], scale=-1.0)
                ba = proj(i, 2 + cc)
                tt(gpad[:, cc, a:a + 256], ps[:, ba, 0:256], sgv, ALU.mult, [("ps", ba), sgt], [("gpad", cc, items[i][1])])
            mid_hook()
            st = {}
            QK = [6, 7, 8, 9, 10]
            PBANK = {6: 0, 7: 1, 8: 2, 9: 3, 10: 7}
            MSLOC = {6: (5, 0), 7: (5, 256), 8: (6, 0), 9: (4, 0), 10: (4, 256)}
            RBANK = {6: 0, 7: 1, 8: 2, 9: 3, 10: 5}

            def stA(ch):
                b = proj(i, ch)
                _, sqv, sqt = half("c")
                act(sqv, ps[:, b, 0:256], AF.Square, [("ps", b)], [sqt])
                st[ch] = dict(b=b, sqv=sqv, sqt=sqt)

            def stB1(ch):
                d = st[ch]
                mb, mc = MSLOC[ch]
                mm(ps[:, mb, mc:mc + 256], BLK64, d["sqv"], True, True, [d["sqt"], "cmat"], [("ps", mb)])

            def stB2(ch):
                d = st[ch]
                k = ch - 6
                mb, mc = MSLOC[ch]
                lnv, _, lnt = half("c")
                act(lnv, ps[:, mb, mc:mc + 256], AF.Ln, [("ps", mb)], [lnt], bias=EPS)
                act(lnv, lnv, AF.Exp, [lnt], [lnt], scale=-0.5)
                gsc = KG if ch == 10 else QG
                if ch == 10:
                    dstv, dtok = None, ("kst", items[i][1])
                else:
                    dstv, dtok = qst[:, ch - 6, t0:t0 + 256], ("qst", ch - 6, items[i][1])
                if not is_sample and ch != 10:
                    stt(dstv, ps[:, d["b"], 0:256], gsc, lnv, ALU.mult, ALU.mult, [("ps", d["b"]), "smallt", lnt], [dtok])
                    return
                qnv, qnt = qn_slot(k)
                stt(qnv, ps[:, d["b"], 0:256], gsc, lnv, ALU.mult, ALU.mult, [("ps", d["b"]), "smallt", lnt], [qnt])
                d.update(qnv=qnv, qnt=qnt, dstv=dstv, dtok=dtok)
                if not is_sample:
                    vcopy(kz[0][0:64, t0:t0 + 256], qnv[0:64, :], [qnt], [dtok])
                    vcopy(kz[1][64:128, t0:t0 + 256], qnv[64:128, :], [qnt], [dtok])
                    dma("sp", nk_d[l][:, t0:t0 + 256], qnv, "o_nk", [qnt], [])
                else:
                    qbv, qbt = qb_slot(k)
                    vcopy(qbv, qnv, [qnt], [qbt])
                    d.update(qbv=qbv, qbt=qbt)

            def stC1(ch):
                if not is_sample:
                    return
                d = st[ch]
                rb = {6: 5, 10: 6}.get(ch, d["b"])
                d["rb"] = rb
                mm(ps[:, rb, 0:256], PERM, d["qbv"], True, True, [d["qbt"], "cmat"], [("ps", rb)])

            def stC2(ch):
                if not is_sample:
                    return
                d = st[ch]
                rb = d["rb"]
                tt(d["qnv"], d["qnv"], rc_, ALU.mult, [d["qnt"], ("ropec", i % 2)], [d["qnt"]])
                t2v, _, t2t = half("c")
                tt(t2v, ps[:, rb, 0:256], rs_, ALU.mult, [("ps", rb), ("ropes", i % 2)], [t2t])
                if ch == 10:
                    tt(kz[0][0:64, t0:t0 + 256], d["qnv"][0:64, :], t2v[0:64, :], ALU.add, [d["qnt"], t2t], [d["dtok"]])
                    tt(kz[1][64:128, t0:t0 + 256], d["qnv"][64:128, :], t2v[64:128, :], ALU.add, [d["qnt"], t2t], [d["dtok"]])
                else:
                    tt(d["dstv"], d["qnv"], t2v, ALU.add, [d["qnt"], t2t], [d["dtok"]])

            B1, B2 = [6, 7, 8], [9, 10]
            for ch in B1:
                stA(ch)
            for ch in B1:
                stB1(ch)
            for ch in B1:
                stB2(ch)
            for ch in B2:
                stA(ch)
            for ch in B2:
                stB1(ch)
            for ch in B1:
                stC1(ch)
            for ch in B2:
                stB2(ch)
            for ch in B1:
                stC2(ch)
            for ch in B2:
                stC1(ch)
            V_LATE = True
            wv, wt = wcol(11)
            for tl in range(2):
                for k in range(NCH):
                    mm(ps[:, 7, tl * 128:(tl + 1) * 128], hb(i)[:, k, tl * 128:(tl + 1) * 128], wv[:, k, :], k == 0,
                       k == NCH - 1, [wt, ("hgrp", i % 2, k)], [("ps", 7)])
            for tl in range(2):
                gt = (t0 // 128) + tl
                act(vaug[:, gt, 0, 0:64], ps[:, 7, tl * 128:tl * 128 + 64], AF.Copy, [("ps", 7)], [("vaug", gt)])
                act(vaug[:, gt, 1, 64:128], ps[:, 7, tl * 128 + 64:tl * 128 + 128], AF.Copy, [("ps", 7)], [("vaug", gt)])
                if not is_sample:
                    vcopy(vout[:, gt, :], ps[:, 7, tl * 128:(tl + 1) * 128], [("ps", 7)], [("vout", gt)])
            for ch in B2:
                stC2(ch)

        norm1(0)
        norm2(0)
        init_big(is_sample, T)
        load_cache()
        for i in range(len(items)):
            if i + 1 < len(items):
                norm1(i + 1)
                item_body(i, lambda i=i: norm2(i + 1))
            else:
                item_body(i, lambda: None)
        tmp_barrier()
        release_unit()
        release_unit()
        if not is_sample:
            dma("sp", nv_d[l].rearrange("(t p) f -> p t f", p=128), vout, "o_nv",
                [("vout", gt) for gt in range(8)], [])

        sD, tokD, _ = next_unit("DIAG")
        diag = unit_views("DIAG", sD)
        sW, tokW, _ = next_unit("WOUT")
        wout, pwv, pbv = unit_views("WOUT", sW)
        for cc in range(2):
            for j in range(31):
                act(diag[:, cc, j, :], IDENT, AF.Copy, ["cmat", "dwt"], [tokD], scale=dwt[:, cc, j:j + 1])

        def wout_part(kc0, src, src_tokfn, t0, n):
            for d in range(NCH):
                bo = 4 + (d % 2)
                for kk in range(4):
                    mm(ps[:, bo, 0:n], wout[:, kc0 + kk, d * 128:(d + 1) * 128], src[:, kk, 0:n], kk == 0, kk == 3,
                       [tokW, src_tokfn(kk)], [("ps", bo)])
                stt(X[:, d, t0:t0 + n], ps[:, bo, 0:n], AB[:, l, cj, 1, 2, d:d + 1], X[:, d, t0:t0 + n], ALU.mult, ALU.add,
                    [("ps", bo), ("AB", l), ("X", d, t0)], [("X", d, t0)])

        allsegs = []
        for (t0, n) in groups:
            t = t0
            while t < t0 + n:
                for (s0, sl) in seqs:
                    if s0 <= t < s0 + sl:
                        e = min(t0 + n, s0 + sl)
                        allsegs.append(dict(st=t, sn=e - t, at_start=(t == s0), at_end=(e == s0 + sl), t0=t0, n=n,
                                            last=(e == t0 + n)))
                        t = e
                        break
        gr_all = {cc: [("gpad", cc, g0) for (g0, _) in groups] for cc in (0, 1)}
        ur_all = {ch: [("upad", ch, g0) for (g0, _) in groups] for ch in (0, 1)}

        def conv_mms(si):
            sg_ = allsegs[si]
            a, sn = padcol(sg_["st"]), sg_["sn"]
            for cc in (0, 1):
                b = 2 * (si % 2) + cc
                for j in range(31):
                    mm(ps[:, b, 0:sn], diag[:, cc, j, :], gpad[:, cc, a + j - 15:a + j - 15 + sn], j == 0, j == 30,
                       [tokD] + gr_all[cc], [("ps", b)])

        def pool_dve(si):
            sg_ = allsegs[si]
            a, sn, at_start, at_end = padcol(sg_["st"]), sg_["sn"], sg_["at_start"], sg_["at_end"]
            outs = []
            for ch in (0, 1):
                ur = ur_all[ch]
                A_, At = tmp()
                tt(A_[:, 0:sn + 14], upad[:, ch, a - 8:a + sn + 6], upad[:, ch, a - 7:a + sn + 7], ALU.add, ur, [At])
                B_, Bt = tmp()
                tt(B_[:, 0:sn + 12], A_[:, 0:sn + 12], A_[:, 2:sn + 14], ALU.add, [At], [Bt])
                if ch == 0:
                    lo_src, lo_off, lo_w = A_, 7, 2
                    hi_src, hi_off, hi_w = B_, 6, 4
                    lot, hit = At, Bt
                else:
                    C_, Ct = tmp()
                    tt(C_[:, 0:sn + 8], B_[:, 0:sn + 8], B_[:, 4:sn + 12], ALU.add, [Bt], [Ct])
                    D_, Dt = tmp()
                    tt(D_[64:128, 0:sn], C_[64:128, 0:sn], C_[64:128, 8:sn + 8], ALU.add, [Ct], [Dt])
                    lo_src, lo_off, lo_w = C_, 4, 8
                    hi_src, hi_off, hi_w = D_, 0, 16
                    lot, hit = Ct, Dt
                mean, mt = tmp()
                ts(mean[0:64, 0:sn], lo_src[0:64, lo_off:lo_off + sn], 1.0 / lo_w, ALU.mult, [lot], [mt])
                ts(mean[64:128, 0:sn], hi_src[64:128, hi_off:hi_off + sn], 1.0 / hi_w, ALU.mult, [hit], [mt])
                if at_start:
                    tt(mean[0:64, 0:8], lo_src[0:64, lo_off:lo_off + 8], edget[0:64, ch, 0, :], ALU.mult,
                       [lot, "edget"], [mt])
                    tt(mean[64:128, 0:8], hi_src[64:128, hi_off:hi_off + 8], edget[64:128, ch, 0, :], ALU.mult,
                       [hit, "edget"], [mt])
                if at_end:
                    tt(mean[0:64, sn - 8:sn], lo_src[0:64, lo_off + sn - 8:lo_off + sn], edget[0:64, ch, 1, :],
                       ALU.mult, [lot, "edget"], [mt])
                    tt(mean[64:128, sn - 8:sn], hi_src[64:128, hi_off + sn - 8:hi_off + sn], edget[64:128, ch, 1, :],
                       ALU.mult, [hit, "edget"], [mt])
                plv = A_.bitcast(BF16)[:, 0:sn]
                tt(plv, mean[:, 0:sn], upad[:, ch, a:a + sn], ALU.subtract, [mt] + ur, [At])
                outs.append((plv, At, tctr[0]))
            return outs

        def pool_mm(si, outs):
            sg_ = allsegs[si]
            sn, off = sg_["sn"], sg_["st"] - sg_["t0"]
            for ch in (0, 1):
                plv, plt, ser = outs[ch]
                assert tctr[0] - ser < NTMP - 1, "tmp ring wrapped (pool)"
                b = 6 + ch
                mm(ps[:, b, 0:sn], pbv[:, ch, :], plv, True, True, [tokW, plt], [("ps", b)])
                act(opc[:, ch, off:off + sn], ps[:, b, 0:sn], AF.Copy, [("ps", b), "smallt"], [("omix", ch)],
                    scale=PSC(ch))

        def conv_post(si):
            sg_ = allsegs[si]
            sn, off = sg_["sn"], sg_["st"] - sg_["t0"]
            ybs = []
            for cc in (0, 1):
                b = 2 * (si % 2) + cc
                yb, ybt = tmp()
                act(yb[:, 0:sn], ps[:, b, 0:sn], AF.Identity, [("ps", b), "smallt"], [ybt], bias=CB(cc))
                sq, sqt = tmp()
                sqv = sq.bitcast(BF16)[:, 0:sn]
                act(sqv, yb[:, 0:sn], AF.Square, [ybt], [sqt])
                mm(ps[:, 6, 0:sn], ONES256, sqv, cc == 0, cc == 1, [sqt, "cmat"], [("ps", 6)])
                ybs.append((yb, ybt))
            ln, lnt = tmp()
            act(ln[:, 0:sn], ps[:, 6, 0:sn], AF.Ln, [("ps", 6)], [lnt], bias=EPS)
            act(ln[:, 0:sn], ln[:, 0:sn], AF.Exp, [lnt], [lnt], scale=-0.5)
            zs = []
            for cc in (0, 1):
                yb, ybt = ybs[cc]
                stt(yb[:, 0:sn], yb[:, 0:sn], CNG(cc), ln[:, 0:sn], ALU.mult, ALU.mult, [ybt, "smallt", lnt], [ybt])
                zb, zbt = tmp()
                zbv = zb.bitcast(BF16)[:, 0:sn]
                act(zbv, yb[:, 0:sn], AF.Silu, [ybt], [zbt])
                zs.append((zbv, zbt))
            for co in (0, 1):
                b = 6 + co
                for ci in (0, 1):
                    mm(ps[:, b, 0:sn], pwv[:, ci, co * 128:(co + 1) * 128], zs[ci][0], ci == 0, ci == 1,
                       [tokW, zs[ci][1]], [("ps", b)])
                act(opc[:, 2 + co, off:off + sn], ps[:, b, 0:sn], AF.Copy, [("ps", b)], [("omix", 2 + co)])

        conv_mms(0)
        for si in range(len(allsegs)):
            outs = pool_dve(si)
            if si + 1 < len(allsegs):
                conv_mms(si + 1)
            pool_mm(si, outs)
            conv_post(si)
            if allsegs[si]["last"]:
                wout_part(0, opc, lambda kk: ("omix", kk), allsegs[si]["t0"], allsegs[si]["n"])

        release_unit()
        sbank = [0]
        kall = [("kst", g0) for (g0, _) in groups] + ([("kst", "cache")] if is_sample else [])
        allsteps = []
        ginfo = []
        for (t0, n) in groups:
            qall = [("qst", c, t0) for c in range(4)]
            first = len(allsteps)
            ntile0 = None
            for tl in range(n // 128):
                gt = t0 // 128 + tl
                q0 = t0 + tl * 128
                chunks = []
                if is_sample:
                    if gt > 0:
                        chunks.append((q0 - 128, gt - 1, 0))
                    chunks.append((q0, gt, None))
                    if gt < ntiles - 1:
                        chunks.append((q0 + 128, gt + 1, 1))
                    for cti in range(4):
                        chunks.append((TS + cti * 128, 18 + cti, None))
                else:
                    for (s0, sl) in seqs:
                        if s0 <= q0 < s0 + sl:
                            for kt in range(sl // 128):
                                chunks.append((s0 + kt * 128, (s0 // 128) + kt, None))
                for kvh in range(2):
                    for ci, ch in enumerate(chunks):
                        allsteps.append((tl, gt, q0, ci, len(chunks), ch, kvh, t0, qall))
                if ntile0 is None:
                    ntile0 = len(allsteps) - first
            ginfo.append((first, ntile0, t0, n, len(allsteps) - 1))

        def qk(step):
            tl, gt, q0, ci, nci, (kc0, vt, mk), kvh, t0, qall = step
            b = sbank[0] % 4
            sbank[0] += 1
            for hh in range(4):
                mm(ps[:, b, hh * 128:(hh + 1) * 128], kz[kvh][:, kc0:kc0 + 128],
                   qst[:, hh, q0:q0 + 128], True, True, kall + qall, [("ps", b)])
            pt, ptt = tmp()
            ptv = pt.bitcast(BF16)[:, 0:512]
            act(ptv, ps[:, b, :], AF.Exp, [("ps", b)], [ptt], scale=0.125)
            if mk is not None:
                pv4 = ptv.rearrange("p (h q) -> p h q", h=4)
                tt(pv4, pv4, masks[:, mk, :].unsqueeze(1).broadcast_to([128, 4, 128]), ALU.mult, [ptt, "masks"], [ptt])
            return (ptv, ptt, tctr[0])

        def pv(step, pts):
            tl, gt, q0, ci, nci, (kc0, vt, mk), kvh, t0, qall = step
            pb = 4 + 2 * (gt % 2)
            assert tctr[0] - pts[2] < NTMP, "tmp ring wrapped"
            vtok = ("vaug", vt) if vt < 18 else ("vaug", "cache")
            mm(ps[:, pb + kvh, :], vaug[:, vt, kvh, :], pts[0], ci == 0, ci == nci - 1,
               [vtok, pts[1]], [("ps", pb + kvh)])
            if ci == nci - 1:
                dlo, nlo = (64, 0) if kvh == 0 else (0, 64)
                rc2, rc2t = tmp()
                if kvh == 0:
                    ln, lnt = tmp()
                    for hh in range(4):
                        act(ln[dlo:dlo + 64, hh * 128:(hh + 1) * 128], ps[dlo:dlo + 64, pb + kvh, hh * 128:(hh + 1) * 128],
                            AF.Ln, [("ps", pb + kvh), "esb"], [lnt], bias=esb[dlo:dlo + 64, 4 * kvh + hh:4 * kvh + hh + 1])
                    act(ln[dlo:dlo + 64, 0:512], ln[dlo:dlo + 64, 0:512], AF.Exp, [lnt], [lnt], scale=-1.0)
                    vcopy(rc2[nlo:nlo + 64, 0:512], ln[dlo:dlo + 64, 0:512], [lnt], [rc2t])
                else:
                    for hh in range(4):
                        ts(rc2[nlo:nlo + 64, hh * 128:(hh + 1) * 128], ps[dlo:dlo + 64, pb + kvh, hh * 128:(hh + 1) * 128],
                           esb[dlo:dlo + 64, 4 * kvh + hh:4 * kvh + hh + 1], ALU.add, [("ps", pb + kvh), "esb"], [rc2t])
                    P.add("dve", lambda e, o_=rc2[nlo:nlo + 64, 0:512]: e.reciprocal(out=o_, in_=o_), [rc2t], [rc2t])
                tt(oattn[nlo:nlo + 64, :, tl * 128:(tl + 1) * 128],
                   ps[nlo:nlo + 64, pb + kvh, :].rearrange("p (h q) -> p h q", h=4),
                   rc2[nlo:nlo + 64, 0:512].rearrange("p (h q) -> p h q", h=4), ALU.mult,
                   [("ps", pb + kvh), rc2t], [("omix", kk) for kk in range(4)])


        def wout_attn(t0, n):
            for d in range(NCH):
                bo = 6 + (d % 2)
                for kk in range(4):
                    mm(ps[:, bo, 0:n], wout[:, 4 + kk, d * 128:(d + 1) * 128], oattn[:, kk, 0:n], kk == 0, kk == 3,
                       [tokW, ("omix", kk)], [("ps", bo)])
                stt(X[:, d, t0:t0 + n], ps[:, bo, 0:n], AB[:, l, cj, 1, 2, d:d + 1], X[:, d, t0:t0 + n], ALU.mult, ALU.add,
                    [("ps", bo), ("AB", l), ("X", d, t0)], [("X", d, t0)])

        DEPTH = 3
        pend = []
        due = []
        for si, step in enumerate(allsteps):
            while due and due[0][0] <= si:
                _, t0_, n_ = due.pop(0)
                wout_attn(t0_, n_)
            pend.append((si, step, qk(step)))
            if len(pend) > DEPTH:
                pi, pstep, ppts = pend.pop(0)
                pv(pstep, ppts)
                for gi, (first, ntile0, t0_, n_, last) in enumerate(ginfo):
                    if pi == last:
                        if gi + 1 < len(ginfo):
                            nfirst, nnt0 = ginfo[gi + 1][0], ginfo[gi + 1][1]
                            due.append((nfirst + min(DEPTH + 4, nnt0 // 2 - 1 + DEPTH), t0_, n_))
                        else:
                            due.append((10 ** 9, t0_, n_))
        while pend:
            pi, pstep, ppts = pend.pop(0)
            pv(pstep, ppts)
            for gi, (first, ntile0, t0_, n_, last) in enumerate(ginfo):
                if pi == last:
                    due.append((10 ** 9, t0_, n_))
        for (_, t0_, n_) in due:
            wout_attn(t0_, n_)
        release_unit()

    def mkgroups(T):
        g = []
        t = 0
        while t < T:
            n = min(512, T - t)
            g.append((t, n))
            t += n
        return g

    def big_switch():
        P.add("dve", lambda e: e.memset(scr[:, :], 0.0), (), ["BIG"])

    def init_big(is_sample, T):
        big_switch()
        seqs_ = [(0, TS)] if is_sample else [(s_ * 256, 256) for s_ in range(4)]
        ut = [("upad", ch, g0) for ch in (0, 1) for (g0, _) in mkgroups(T)]
        gt_ = [("gpad", ch, g0) for ch in (0, 1) for (g0, _) in mkgroups(T)]
        for si, (s0, sl) in enumerate(seqs_):
            a = s0 + PADW * (2 * si + 1)
            for (c0, c1) in ((a - PADW, a), (a + sl, a + sl + PADW)):
                vmemset(upad[:, :, c0:c1], 0.0, ut)
                vmemset(gpad[:, :, c0:c1], 0.0, gt_)
        vt = [("vaug", t) for t in range(18)] + [("vaug", "cache")]
        vmemset(vaug[:, :, 0, 64:128], 1.0, vt)
        vmemset(vaug[:, :, 1, 0:64], 1.0, vt)
        ktoks = [("kst", g0) for (g0, _) in mkgroups(T)] + [("kst", "cache")]
        vmemset(kz[0][64:128, :], 0.0, ktoks)
        vmemset(kz[1][0:64, :], 0.0, ktoks)

    for it in plan:
        if it[0] == "mod":
            mod_enqueue(it[1])
            if it[1] == 0:
                mod_pump(12)
                _load_ready()
                mod_cap[0] = 10 ** 9
            continue
        phase = it[1]
        is_sample = phase == 1
        T = TS if is_sample else TP
        xin = xs_d if is_sample else xp_d
        yout = ys_d if is_sample else yp_d
        groups = mkgroups(T)
        seqs = [(0, TS)] if is_sample else [(s * 256, 256) for s in range(4)]
        if it[0] == "load":
            if is_sample:
                mod_ensure(1, 2)
            for c in range(NCH):
                dma("sp", X[:, c, 0:T], xin[c], "x_in%d" % c, (), [("X", c, g0) for (g0, _) in groups])
        elif it[0] == "store":
            for c in range(NCH):
                dma("sp", yout[c], X[:, c, 0:T], "y_out%d" % c, [("X", c, g0) for (g0, _) in groups], [])
        elif it[0] == "ffn1":
            mod_ensure(it[2], 0)
            ffn(it[2], 0, phase, groups, hook=lambda: mod_pump(5))
        elif it[0] == "ffn2":
            mod_ensure(it[2], 2)
            ffn(it[2], 1, phase, groups, hook=lambda: mod_pump(5))
        elif it[0] == "mixer":
            mod_ensure(it[2], 1)
            mixer(it[2], phase, groups, seqs, is_sample)
            big_switch()

    final_keys = ["y_out%d" % c for c in range(NCH)] + ["o_nk", "o_nv"]
    P.emit(nc, es, final_keys)
    es.close()
    nc._prog_stats = P.stats
    return nc


_NC_CACHE = {}


def _head_perm():
    cols = []
    for j in range(4):
        cols += list(range(j * 64, j * 64 + 64))
        cols += list(range((4 + j) * 64, (4 + j) * 64 + 64))
    return np.array(cols)


def _consts():
    cm = np.zeros((128, 5, 128), np.float32)
    cm[:, 0, :] = 1.0 / 1024
    for b in range(2):
        cm[64 * b:64 * b + 64, 1, 64 * b:64 * b + 64] = 1.0 / 64
    cm[:, 2, :] = 1.0 / 256
    cm[:, 3, :] = np.eye(128, dtype=np.float32)
    for m in range(128):
        d = m % 32
        partner = m + 16 if d < 16 else m - 16
        cm[partner, 4, m] = 1.0
    k = np.arange(128)[:, None]
    q = np.arange(128)[None, :]
    mprev = (k >= q).astype(np.float32)
    mnext = (k <= q).astype(np.float32)
    masks = np.stack([mprev, mnext], axis=1)
    sinkl = np.zeros((1, 2, 128), np.float32)
    sinkl[0, 0, 64:128] = 1.0
    sinkl[0, 1, 0:64] = 1.0
    edge = np.zeros((128, 2, 2, 8), np.float32)
    wins = {(0, 0): 2, (0, 1): 4, (1, 0): 8, (1, 1): 16}
    for (ch, half), w in wins.items():
        for i in range(8):
            cs = (i + w // 2) - max(i - w // 2, 0)
            r = 8 - i
            ce = min(w // 2, r) + w // 2
            edge[64 * half:64 * half + 64, ch, 0, i] = 1.0 / cs
            edge[64 * half:64 * half + 64, ch, 1, i] = 1.0 / ce
    return cm, masks, sinkl, edge


def _rope_tables(pos0):
    pos = pos0 + np.arange(TS)
    row = (pos // 64).astype(np.float64)
    col = (pos % 64).astype(np.float64)
    half = 32
    inv = 10000.0 ** (-np.arange(0, half, 2, dtype=np.float64) / half)
    cos = np.zeros((128, TS), np.float32)
    sin = np.zeros((128, TS), np.float32)
    for p in range(128):
        d = p % 64
        posv = row if d < 32 else col
        dd = d % 32
        i = dd % 16
        ang = posv * inv[i]
        cos[p] = np.cos(ang)
        sin[p] = np.sin(ang) * (-1.0 if dd < 16 else 1.0)
    return cos, sin


def kernel(x_prompt, x_sample, cache_k, cache_v, c, c_ctx, mod_w, mod_b, norm_g,
           ffn1_wi, ffn1_wo, ffn2_wi, ffn2_wo, w_in, w_out, pool_w, pool_scale,
           conv_dw, conv_b, conv_norm_g, conv_pw, q_norm_g, k_norm_g, sink, _only_core=None, _debug_stop=None):
    f = lambda a: np.ascontiguousarray(np.asarray(a, dtype=np.float32))
    x_prompt, x_sample, cache_k, cache_v = f(x_prompt), f(x_sample), f(cache_k), f(cache_v)
    if "nc" not in _NC_CACHE:
        _NC_CACHE["nc"] = build_program()
    nc = _NC_CACHE["nc"]
    hp = _head_perm()
    w_in_p = f(w_in).copy()
    w_in_p[:, :, 768:1280] = f(w_in)[:, :, 768 + hp]
    w_out_p = f(w_out).copy()
    w_out_p[:, 512:1024, :] = f(w_out)[:, 512 + hp, :]
    modb_l = f(np.asarray(mod_b).reshape(2, 72, 128).transpose(0, 2, 1))
    ng_l = f(np.asarray(norm_g).reshape(2, 3, NCH, 128).transpose(0, 3, 1, 2))
    small = np.zeros((2, 128, 16), np.float32)
    small[:, :, 0:2] = np.asarray(pool_scale).reshape(2, 2, 128).transpose(0, 2, 1)
    small[:, :, 2:4] = np.asarray(conv_b).reshape(2, 2, 128).transpose(0, 2, 1)
    small[:, :, 4:6] = np.asarray(conv_norm_g).reshape(2, 2, 128).transpose(0, 2, 1)
    small[:, :, 6] = np.tile(np.asarray(q_norm_g), (1, 2))
    small[:, :, 7] = np.tile(np.asarray(k_norm_g), (1, 2))
    dw_l = f(np.asarray(conv_dw).reshape(2, 31, 2, 128).transpose(0, 3, 2, 1))
    sk = np.asarray(sink, dtype=np.float32)
    sink_b = f(np.broadcast_to(sk[:, None, :], (2, 128, 8)))
    cm, masks, sinkl, edge = _consts()
    shared = dict(mod_w=f(mod_w), mod_b=modb_l, norm_g=ng_l, ffn1_wi=f(ffn1_wi), ffn2_wi=f(ffn2_wi),
                  ffn1_wo=f(ffn1_wo), ffn2_wo=f(ffn2_wo), w_in=w_in_p, w_out=w_out_p, pool_w=f(pool_w),
                  small=small, conv_dw=dw_l, conv_pw=f(conv_pw), sink_b=sink_b, cmat=cm, masks=masks,
                  pool_edge=edge)
    in_maps = []
    starts = []
    for core in (range(8) if _only_core is None else [_only_core]):
        b, hf = core // 2, core % 2
        s0 = 0 if hf == 0 else 4096 - TS
        starts.append(s0)
        xp = x_prompt[4 * core:4 * core + 4].reshape(TP, D).T.reshape(NCH, 128, TP)
        xs = x_sample[b, s0:s0 + TS].T.reshape(NCH, 128, TS)
        cond = np.stack([np.asarray(c_ctx, np.float32), np.asarray(c, np.float32)[b]], axis=1)
        cond = cond.reshape(NCH, 128, 2).transpose(1, 0, 2)
        ckT = cache_k[b].reshape(2, PAST, 128).transpose(0, 2, 1)
        cv = cache_v[b].reshape(2, PAST, 128)
        cos, sin = _rope_tables(s0)
        m = dict(shared)
        m.update(xp=f(xp), xs=f(xs), cond=f(cond), cache_kT=f(ckT), cache_v=f(cv), rope_cos=cos, rope_sin=sin)
        in_maps.append(m)
    if _only_core is not None:
        nc = build_program(_debug_stop)
        res = run_bass_kernel_spmd(nc, in_maps, core_ids=[0])
        print("EXEC_NS", res.exec_time_ns, nc._prog_stats)
        return res.results[0]
    res = run_bass_kernel_spmd(nc, in_maps, core_ids=list(range(8)))
    y_prompt = np.zeros((32, 256, D), np.float32)
    y_sample = np.zeros((4, 4096, D), np.float32)
    nk = np.zeros((32, 2, 256, 2, 64), np.float32)
    nv = np.zeros((32, 2, 256, 2, 64), np.float32)
    for core in range(8):
        r = res.results[core]
        b, hf = core // 2, core % 2
        yp = np.asarray(r["yp"]).reshape(D, TP).T.reshape(4, 256, D)
        y_prompt[4 * core:4 * core + 4] = yp
        ys = np.asarray(r["ys"]).reshape(D, TS).T
        if hf == 0:
            y_sample[b, 0:2048] = ys[0:2048]
        else:
            y_sample[b, 2048:4096] = ys[TS - 2048:TS]
        k_ = np.asarray(r["nk"]).transpose(0, 2, 1).reshape(2, 4, 256, 2, 64)
        v_ = np.asarray(r["nv"]).reshape(2, 4, 256, 2, 64)
        nk[4 * core:4 * core + 4] = k_.transpose(1, 0, 2, 3, 4)
        nv[4 * core:4 * core + 4] = v_.transpose(1, 0, 2, 3, 4)
    return (y_prompt, y_sample, nk, nv)
```

```python
import numpy as np
from contextlib import ExitStack
import concourse.bass as bass
import concourse.mybir as mybir
from concourse.bass_utils import run_bass_kernel_spmd

F32 = mybir.dt.float32
BF16 = mybir.dt.bfloat16
AF = mybir.ActivationFunctionType
ALU = mybir.AluOpType

D = 1024
NCH = 8
DFF = 2816
NFC = 22
TP = 1024
TS = 2304
PAST = 512
PADW = 16
EPS = 1e-6
SLOT = 9216
FBLOCKS = [3, 3, 3, 3, 3, 3, 3, 1]
NTMP = 8


class Prog:
    def __init__(self):
        self.ops = []
        self.lastw = {}
        self.readers = {}
        self.ambient = True

    def add(self, eng, fn, reads=(), writes=(), dma=None):
        i = len(self.ops)
        def _exp(lst):
            o = []
            for t in lst:
                if isinstance(t, tuple) and len(t) == 2 and t[0] == "slotpair":
                    o += [("slot", t[1], "a"), ("slot", t[1], "b")]
                else:
                    o.append(t)
            return o
        reads = _exp(reads)
        writes = _exp(writes)
        if self.ambient and eng in ("pe", "act", "dve") and dma is None and "BIG" not in writes:
            reads.append("BIG")
        deps = set()
        for t in reads:
            w = self.lastw.get(t)
            if w is not None:
                deps.add(w)
        for t in writes:
            w = self.lastw.get(t)
            if w is not None:
                deps.add(w)
            deps.update(self.readers.get(t, ()))
        red = {}
        for d in deps:
            p = self.ops[d]
            k = ("dma", p["dma"]) if p["dma"] is not None else ("eng", p["eng"])
            if k not in red or d > red[k]:
                red[k] = d
        deps = set(red.values())
        self.ops.append(dict(eng=eng, fn=fn, deps=deps, dma=dma, val=None))
        for t in reads:
            self.readers.setdefault(t, []).append(i)
        for t in writes:
            self.lastw[t] = i
            self.readers[t] = []
        return i

    def emit(self, nc, es, final_keys):
        ops = self.ops
        needed = set()
        for op in ops:
            for d in op["deps"]:
                p = ops[d]
                if p["dma"] is None:
                    if p["eng"] == "pe" and op["eng"] == "pe" and op["dma"] is None:
                        continue
                    needed.add(d)
        cnt = {}
        dcnt = {}
        for i, op in enumerate(ops):
            if op["dma"] is not None:
                dcnt[op["dma"]] = dcnt.get(op["dma"], 0) + 16
                op["val"] = dcnt[op["dma"]]
            elif i in needed:
                cnt[op["eng"]] = cnt.get(op["eng"], 0) + 1
                op["val"] = cnt[op["eng"]]
        self.stats = dict(cnt=dict(cnt), dmax=max(dcnt.values()), nops=len(ops), ndma=len(dcnt))
        sems = {}
        for e in ["pe", "act", "dve", "pool", "sp"]:
            sems[("eng", e)] = es.enter_context(nc.semaphore("s_" + e))
        for k in dcnt:
            sems[("dma", k)] = es.enter_context(nc.semaphore("d_" + str(len(sems))))
        block = es.enter_context(nc.Block())
        per = {e: [] for e in ["pe", "act", "dve", "pool", "sp"]}
        for i, op in enumerate(ops):
            per[op["eng"]].append(i)

        def run(ename, eng):
            waited = {}
            for i in per[ename]:
                op = ops[i]
                need = {}
                for d in op["deps"]:
                    p = ops[d]
                    if p["dma"] is not None:
                        key = ("dma", p["dma"])
                    else:
                        if p["eng"] == "pe" and ename == "pe" and op["dma"] is None:
                            continue
                        key = ("eng", p["eng"])
                    v = p["val"]
                    if v > need.get(key, 0):
                        need[key] = v
                todo = []
                for key in sorted(need, key=str):
                    v = need[key]
                    if waited.get(key, 0) >= v:
                        continue
                    todo.append((key, v))
                    waited[key] = v
                for key, v in todo[:-1]:
                    eng.wait_ge(sems[key], v)
                ins = op["fn"](eng)
                if todo:
                    key, v = todo[-1]
                    ins.wait_op(sems[key], v, "sem-ge")
                if op["dma"] is not None:
                    ins.then_inc(sems[("dma", op["dma"])], 16)
                elif op["val"] is not None:
                    ins.then_inc(sems[("eng", ename)], 1)
            if ename == "sp":
                for k in final_keys:
                    if k in dcnt:
                        eng.wait_ge(sems[("dma", k)], dcnt[k])

        @block.tensor
        def _(e):
            run("pe", e)

        @block.scalar
        def _(e):
            run("act", e)

        @block.vector
        def _(e):
            run("dve", e)

        @block.gpsimd
        def _(e):
            run("pool", e)

        @block.sync
        def _(e):
            run("sp", e)


def build_program(debug_stop=None):
    nc = bass.Bass("TRN2", target_bir_lowering=False)
    P = Prog()
    es = ExitStack()

    def din(name, shape, dt=F32):
        return nc.dram_tensor(name, list(shape), dt, kind="ExternalInput").ap()

    def dout(name, shape):
        return nc.dram_tensor(name, list(shape), F32, kind="ExternalOutput").ap()

    xp_d = din("xp", [NCH, 128, TP])
    xs_d = din("xs", [NCH, 128, TS])
    cond_d = din("cond", [128, NCH, 2])
    modw_d = din("mod_w", [2, D, 9 * D])
    modb_d = din("mod_b", [2, 128, 72])
    ng_d = din("norm_g", [2, 128, 3, NCH])
    wi_d = [din("ffn1_wi", [2, D, 2 * DFF]), din("ffn2_wi", [2, D, 2 * DFF])]
    wo_d = [din("ffn1_wo", [2, DFF, D]), din("ffn2_wo", [2, DFF, D])]
    win_d = din("w_in", [2, D, 1536])
    wout_d = din("w_out", [2, D, D])
    poolw_d = din("pool_w", [2, 4, 64, 64])
    small_d = din("small", [2, 128, 16])
    dw_d = din("conv_dw", [2, 128, 2, 31])
    pw_d = din("conv_pw", [2, 256, 256])
    sink_d = din("sink_b", [2, 128, 8])
    ckT_d = din("cache_kT", [2, 128, PAST])
    cv_d = din("cache_v", [2, PAST, 128])
    cos_d = din("rope_cos", [128, TS])
    sin_d = din("rope_sin", [128, TS])
    cmat_d = din("cmat", [128, 5, 128])
    mask_d = din("masks", [128, 2, 128])
    edge_d = din("pool_edge", [128, 2, 2, 8])

    yp_d = dout("yp", [NCH, 128, TP])
    ys_d = dout("ys", [NCH, 128, TS])
    nk_d = dout("nk", [2, 128, TP])
    nv_d = dout("nv", [2, TP, 128])

    def sb(name, shape, dt):
        return es.enter_context(nc.sbuf_tensor(name, list(shape), dt))

    X = sb("X", [128, NCH, TS], F32)
    BIG = sb("BIG", [128, 29824], BF16)
    slots = [sb("slot0", [128, SLOT], BF16), sb("slot1", [128, SLOT], BF16)]
    tmps = [sb("tmp%d" % i, [128, 528], F32) for i in range(NTMP)]
    hgrp = sb("hgrp", [128, NCH, 512], BF16)
    cmat = sb("cmatb", [128, 5, 128], BF16)
    masks = sb("masksb", [128, 2, 128], BF16)
    esb = sb("esb", [128, 8], F32)
    sinkraw = sb("sinkraw", [128, 8], F32)
    condf = sb("condf", [128, NCH, 2], F32)
    condb = sb("condb", [128, NCH, 2], BF16)
    modv = [sb("modv%d" % l, [128, 72, 2], F32) for l in range(2)]
    modb = sb("modb", [128, 2, 72], F32)
    ngt = sb("ngt", [128, 2, 3, NCH], F32)
    AB = sb("ABt", [128, 2, 2, 3, 3, NCH], F32)
    smallt = sb("smallt", [128, 16], F32)
    dwt = sb("dwt", [128, 2, 31], F32)
    edget = sb("edget", [128, 2, 2, 8], F32)
    opc = sb("omix", [128, 4, 512], BF16)
    oattn = opc
    scr = sb("scr", [128, 2], F32)
    rv = sb("rv", [128, 1024], F32)
    ropec = rv[:, 0:512]
    ropes = rv[:, 512:1024]
    vout = rv[:, :].rearrange("p (t f) -> p t f", t=8)
    ps = es.enter_context(nc.psum_tensor("ps", [128, 8, 512], F32))

    Hv = BIG[:, 0:NCH * TS].rearrange("p (c t) -> p c t", c=NCH)
    actb = [BIG[:, NCH * TS + i * 1536:NCH * TS + (i + 1) * 1536].rearrange("p (c t) -> p c t", c=3) for i in range(2)]
    o = 0
    qst = BIG[:, o:o + 4 * TS].rearrange("p (c t) -> p c t", c=4); o += 4 * TS
    KW = TS + PAST
    kz = []
    for _ in range(2):
        kz.append(BIG[:, o:o + KW]); o += KW
    NVT = 22
    vaug = BIG[:, o:o + NVT * 256].rearrange("p (t g f) -> p t g f", t=NVT, g=2); o += NVT * 256
    TPAD = TS + 2 * PADW
    upad = BIG[:, o:o + 2 * TPAD].rearrange("p (c t) -> p c t", c=2); o += 2 * TPAD
    gpad = BIG[:, o:o + 2 * TPAD].rearrange("p (c t) -> p c t", c=2); o += 2 * TPAD
    assert o <= 29824, o

    tctr = [0]

    def tmp():
        i = tctr[0] % NTMP
        tctr[0] += 1
        return tmps[i], ("tmp", i)

    def tmpb(t):
        return t

    def mm(out, lhsT, rhs, start, stop, reads, writes):
        P.add("pe", lambda e: e.matmul(out, lhsT, rhs, start=start, stop=stop), reads, writes)

    def act(out, in_, func, reads, writes, scale=1.0, bias=0.0):
        P.add("act", lambda e: e.activation(out=out, in_=in_, func=func, scale=scale, bias=bias), reads, writes)

    def tt(out, in0, in1, op, reads, writes):
        P.add("dve", lambda e: e.tensor_tensor(out=out, in0=in0, in1=in1, op=op), reads, writes)

    def stt(out, in0, scalar, in1, op0, op1, reads, writes):
        P.add("dve", lambda e: e.scalar_tensor_tensor(out=out, in0=in0, scalar=scalar, in1=in1, op0=op0, op1=op1),
              reads, writes)

    def ts(out, in0, s1, op0, reads, writes, s2=None, op1=None):
        if op1 is None:
            P.add("dve", lambda e: e.tensor_scalar(out=out, in0=in0, scalar1=s1, scalar2=None, op0=op0), reads, writes)
        else:
            P.add("dve", lambda e: e.tensor_scalar(out=out, in0=in0, scalar1=s1, scalar2=s2, op0=op0, op1=op1),
                  reads, writes)

    def vcopy(out, in_, reads, writes):
        P.add("dve", lambda e: e.tensor_copy(out=out, in_=in_), reads, writes)

    def vmemset(ap, val, writes):
        P.add("dve", lambda e: e.memset(ap, val), (), writes)

    def dma(q, out, in_, key, reads, writes):
        P.add(q, lambda e: e.dma_start(out=out, in_=in_), reads, writes, dma=key)

    dma("pool", cmat[:], cmat_d[:, :, :], "c_cmat", (), ["cmat"])
    dma("pool", masks[:], mask_d[:, :, :], "c_mask", (), ["masks"])
    dma("sp", condf[:], cond_d[:, :, :], "c_cond", (), ["condf"])
    dma("sp", modb[:, 0, :], modb_d[0], "c_modb0", (), ["modb"])
    dma("sp", modb[:, 1, :], modb_d[1], "c_modb1", (), ["modb1"])
    dma("sp", ngt[:, 0], ng_d[0], "c_ng0", (), ["ngt"])
    dma("sp", ngt[:, 1], ng_d[1], "c_ng1", (), ["ngt1"])
    dma("sp", edget[:], edge_d[:, :, :, :], "c_edge", (), ["edget"])
    act(condb[:], condf[:], AF.Silu, ["condf"], ["condb"])
    ONES_MEAN = cmat[:, 0, :]
    BLK64 = cmat[:, 1, :]
    ONES256 = cmat[:, 2, :]
    IDENT = cmat[:, 3, :]
    PERM = cmat[:, 4, :]

    units = []

    def wi_src(l, which, f0, nf):
        v = wi_d[which][l].rearrange("(kc p) f -> p kc f", p=128)
        return v[:, :, f0 * 128:(f0 + nf) * 128], v[:, :, DFF + f0 * 128:DFF + (f0 + nf) * 128]

    for_units = []

    def plan_units():
        seq = []
        for l in range(1):
            pass
        return seq

    ucount = [0]
    ucursor = [0]
    ureleased = [0]

    def unit_views(kind, s, nf=3):
        sl = slots[s]
        if kind == "F":
            wi = sl[:, 0:8 * 2 * nf * 128].rearrange("p (k f) -> p k f", k=8)
            wo = sl[:, 6144:6144 + nf * 1024].rearrange("p (c f) -> p c f", c=nf)
            return wi, wo
        if kind == "M":
            return sl[:, 0:9216].rearrange("p (k f) -> p k f", k=8)
        if kind == "WIN":
            return sl[:, 0:6144].rearrange("p (k f) -> p k f", k=8)
        if kind == "WOUT":
            wout = sl[:, 0:8192].rearrange("p (k f) -> p k f", k=8)
            pw = sl[:, 8192:8704].rearrange("p (k f) -> p k f", k=2)
            pb = sl[:, 8704:8960].rearrange("p (k f) -> p k f", k=2)
            return wout, pw, pb
        if kind == "DIAG":
            return sl[:, 0:7936].rearrange("p (c j f) -> p c j f", c=2, j=31)
        raise ValueError(kind)

    def stoks(s):
        return [("slot", s, "a"), ("slot", s, "b")]

    def load_unit(u):
        idx, kind, l, arg = u
        s = idx % 2
        both = stoks(s)
        if kind == "F":
            which, f0, nf = arg
            wi, wo = unit_views("F", s, nf)
            g_src, u_src = wi_src(l, which, f0, nf)
            dma("pool", wi[:, :, 0:nf * 128], g_src, "u%d_a" % s, (), [("slot", s, "a")])
            dma("pool", wi[:, :, nf * 128:2 * nf * 128], u_src, "u%d_b" % s, (), [("slot", s, "a")])
            wsrc = wo_d[which][l][f0 * 128:(f0 + nf) * 128, :].rearrange("(c p) f -> p c f", p=128)
            dma("pool", wo, wsrc, "u%d_c" % s, (), [("slot", s, "b")])
        elif kind == "M":
            j = arg
            mv = unit_views("M", s)
            src = modw_d[l].rearrange("(kc p) f -> p kc f", p=128)[:, :, j * 1152:(j + 1) * 1152]
            dma("pool", mv, src, "u%d_a" % s, (), both)
        elif kind == "WIN":
            j = arg
            wv = unit_views("WIN", s)
            src = win_d[l].rearrange("(kc p) f -> p kc f", p=128)[:, :, j * 768:(j + 1) * 768]
            dma("pool", wv, src, "u%d_a" % s, (), both)
        elif kind == "WOUT":
            wout, pw, pb = unit_views("WOUT", s)
            dma("pool", wout, wout_d[l].rearrange("(kc p) f -> p kc f", p=128), "u%d_a" % s, (), both)
            dma("pool", pw, pw_d[l].rearrange("(kc p) f -> p kc f", p=128), "u%d_b" % s, (), both)
            P.add("pool", lambda e, pb=pb: e.memset(pb, 0.0), (), both)
            for gi in range(4):
                ch, half = gi // 2, gi % 2
                dma("pool", pb[64 * half:64 * half + 64, ch, 64 * half:64 * half + 64], poolw_d[l, gi],
                    "u%d_p%d" % (s, gi), (), both)
        elif kind == "DIAG":
            pass
        else:
            raise ValueError(kind)

    def _load_ready():
        while ucount[0] < len(units) and ucount[0] < ureleased[0] + 2:
            load_unit(units[ucount[0]])
            ucount[0] += 1

    def next_unit(expect_kind):
        u = units[ucursor[0]]
        assert u[1] == expect_kind, (u, expect_kind)
        _load_ready()
        assert ucount[0] > ucursor[0], ("unit not loadable yet", u, ureleased[0])
        ucursor[0] += 1
        return u[0] % 2, ("slotpair", u[0] % 2), u

    def release_unit():
        ureleased[0] += 1
        _load_ready()

    def add_units(kind_list):
        for (kind, l, arg) in kind_list:
            units.append((len(units), kind, l, arg))

    def layer_units(l):
        r = []
        f0 = 0
        for nf in FBLOCKS:
            r.append(("F", l, (0, f0, nf)))
            f0 += nf
        r += [("WIN", l, 0), ("WIN", l, 1), ("WOUT", l, None), ("DIAG", l, None)]
        f0 = 0
        for nf in FBLOCKS:
            r.append(("F", l, (1, f0, nf)))
            f0 += nf
        return r

    k0, k1 = (6, 6) if debug_stop is None else debug_stop
    plan = [("mod", 0)]
    need_mod1 = False
    if max(k0, k1) > 3:
        plan.append(("mod", 1))
    for phase, kk in ((0, k0), (1, k1)):
        if kk < 0:
            continue
        plan.append(("load", phase))
        for st in range(kk):
            l, sub = st // 3, st % 3
            if l == 1 and need_mod1:
                plan.append(("mod", 1))
                need_mod1 = False
            plan.append((("ffn1", "mixer", "ffn2")[sub], phase, l))
        plan.append(("store", phase))
    for it in plan:
        if it[0] == "mod":
            pass
        elif it[0] in ("ffn1", "ffn2"):
            f0 = 0
            for nf in FBLOCKS:
                add_units([("F", it[2], (0 if it[0] == "ffn1" else 1, f0, nf))])
                f0 += nf
        elif it[0] == "mixer":
            add_units([("WIN", it[2], 0), ("WIN", it[2], 1), ("DIAG", it[2], None), ("WOUT", it[2], None)])

    mod_q = []
    mod_loaded = [0]
    mod_done = [0]
    mod_list = []

    def mini_view(slot):
        return X[:, slot, 1024:2048].bitcast(BF16).rearrange("p (k f) -> p k f", k=8)

    def mini_toks(slot):
        return [("X", slot, 1024), ("X", slot, 1536)]

    def mod_enqueue(l):
        for m in range(36):
            mod_list.append((l, m))

    mod_cap = [12]

    def _mod_load_ahead():
        while mod_loaded[0] < len(mod_list) and mod_loaded[0] < min(mod_done[0] + 8, mod_cap[0]):
            l, m = mod_list[mod_loaded[0]]
            slot = mod_loaded[0] % 8
            src = modw_d[l].rearrange("(kc p) f -> p kc f", p=128)[:, :, m * 256:(m + 1) * 256]
            dma("pool", mini_view(slot), src, "mm%d" % slot, (), mini_toks(slot))
            mod_loaded[0] += 1

    def compute_AB(l, i):
        ng = "ngt" if l == 0 else "ngt1"
        for cj in range(2):
            sh = modv[l][:, (3 * i) * 8:(3 * i) * 8 + 8, cj]
            sc = modv[l][:, (3 * i + 1) * 8:(3 * i + 1) * 8 + 8, cj]
            gg = modv[l][:, (3 * i + 2) * 8:(3 * i + 2) * 8 + 8, cj]
            A = AB[:, l, cj, i, 0, :]
            B = AB[:, l, cj, i, 1, :]
            G = AB[:, l, cj, i, 2, :]
            stt(A, sc, 1.0, ngt[:, l, i, :], ALU.add, ALU.mult, [("modv", l), ng], [("AB", l)])
            vcopy(B, sh, [("modv", l)], [("AB", l)])
            ts(G, gg, 0.5 if i != 1 else 1.0, ALU.mult, [("modv", l)], [("AB", l)])

    def mod_pump(n):
        for _ in range(n):
            if mod_done[0] >= len(mod_list):
                return
            _mod_load_ahead()
            idx = mod_done[0]
            l, m = mod_list[idx]
            slot = idx % 8
            mv = mini_view(slot)
            mb = "modb" if l == 0 else "modb1"
            bank = 6 + (idx % 2)
            col0 = 0
            for cc in range(2):
                for k in range(NCH):
                    mm(ps[:, bank, col0 + 2 * cc:col0 + 2 * cc + 2], mv[:, k, cc * 128:(cc + 1) * 128], condb[:, k, :],
                       k == 0, k == NCH - 1, mini_toks(slot) + ["condb"], [("ps", bank)])
            for cc in range(2):
                cg = 2 * m + cc
                ts(modv[l][:, cg, :], ps[:, bank, col0 + 2 * cc:col0 + 2 * cc + 2], modb[:, l, cg:cg + 1], ALU.add,
                   [("ps", bank), mb], [("modv", l)])
            mod_done[0] += 1
            _mod_load_ahead()
            if m % 12 == 11:
                compute_AB(l, m // 12)

    def mod_ensure(l, i):
        while mod_done[0] < len(mod_list) and mod_list[mod_done[0]] <= (l, 12 * i + 11):
            mod_pump(1)

    def norm_mod(l, cj, i, t0, n, dst, dst_tok_fn):
        msb = 6
        for c in range(NCH):
            sq, sqt = tmp()
            sqv = sq.bitcast(BF16)[:, 0:n]
            act(sqv, X[:, c, t0:t0 + n], AF.Square, [("X", c, t0)], [sqt])
            mm(ps[:, msb, 0:n], ONES_MEAN, sqv, c == 0, c == NCH - 1, [sqt, "cmat"], [("ps", msb)])
        ln, lnt = tmp()
        act(ln[:, 0:n], ps[:, msb, 0:n], AF.Ln, [("ps", msb)], [lnt], bias=EPS)
        rs, rst = tmp()
        act(rs[:, 0:n], ln[:, 0:n], AF.Exp, [lnt], [rst], scale=-0.5)
        for c in range(NCH):
            t, ttok = tmp()
            stt(t[:, 0:n], X[:, c, t0:t0 + n], AB[:, l, cj, i, 0, c:c + 1], rs[:, 0:n], ALU.mult, ALU.mult,
                [("X", c, t0), ("AB", l), rst], [ttok])
            dtk = dst_tok_fn(c)
            act(dst[:, c, 0:n] if dst is hgrp else dst[:, c, t0:t0 + n], t[:, 0:n], AF.Identity,
                [ttok, ("AB", l)], dtk if isinstance(dtk, list) else [dtk], bias=AB[:, l, cj, i, 1, c:c + 1])

    def h0_toks(c):
        return [("hgrp", 0, c), ("hgrp", 1, c)]

    def ffn_prenorm(l, cj, groups):
        t0, n = groups[0]
        norm_mod(l, cj, 2, t0, n, hgrp, h0_toks)

    def ffn(l, which, cj, groups, hook=None, pre0=False):
        i = 0 if which == 0 else 2
        LOOK = 1
        for gi0, (t0, n) in enumerate(groups[:LOOK]):
            if pre0 and gi0 == 0:
                continue
            norm_mod(l, cj, i, t0, n, Hv, lambda c, t0=t0: ("H", c, t0))
        items = []
        f0 = 0
        blk_info = []
        for bi, nf in enumerate(FBLOCKS):
            for gi, (t0, n) in enumerate(groups):
                items.append((bi, nf, f0, gi, t0, n))
            f0 += nf
        state = {"cur_blk": -1, "views": None, "tok": None}
        gu_ctr = [0]
        blk_views = {}

        def GU(it, ab):
            bi, nf, f0, gi, t0, n = it
            if bi not in blk_views:
                s, tok, u = next_unit("F")
                blk_views[bi] = (unit_views("F", s, nf), tok)
            (wi, wo), tok = blk_views[bi]
            for fc in range(nf):
                pr = gu_ctr[0] % 2
                gu_ctr[0] += 1
                bg, bu = 2 * pr, 2 * pr + 1
                for k in range(NCH):
                    hsrc = hgrp[:, k, 0:n] if (pre0 and gi == 0) else Hv[:, k, t0:t0 + n]
                    htk = h0_toks(k) if (pre0 and gi == 0) else [("H", k, t0)]
                    mm(ps[:, bg, 0:n], wi[:, k, fc * 128:(fc + 1) * 128], hsrc, k == 0, k == NCH - 1,
                       [("slot", tok[1], "a")] + htk, [("ps", bg)])
                for k in range(NCH):
                    hsrc = hgrp[:, k, 0:n] if (pre0 and gi == 0) else Hv[:, k, t0:t0 + n]
                    htk = h0_toks(k) if (pre0 and gi == 0) else [("H", k, t0)]
                    mm(ps[:, bu, 0:n], wi[:, k, (nf + fc) * 128:(nf + fc + 1) * 128], hsrc, k == 0,
                       k == NCH - 1, [("slot", tok[1], "a")] + htk, [("ps", bu)])
                sil, silt = tmp()
                act(sil[:, 0:n], ps[:, bg, 0:n], AF.Silu, [("ps", bg)], [silt])
                tt(actb[ab][:, fc, 0:n], ps[:, bu, 0:n], sil[:, 0:n], ALU.mult, [("ps", bu), silt], [("actb", ab, fc)])

        def WO(it, ab, last_of_block):
            bi, nf, f0, gi, t0, n = it
            (wi, wo), tok = blk_views[bi]
            for d in range(NCH):
                bo = 4 + (d % 2)
                for fc in range(nf):
                    mm(ps[:, bo, 0:n], wo[:, fc, d * 128:(d + 1) * 128], actb[ab][:, fc, 0:n], fc == 0, fc == nf - 1,
                       [("slot", tok[1], "b"), ("actb", ab, fc)], [("ps", bo)])
                stt(X[:, d, t0:t0 + n], ps[:, bo, 0:n], AB[:, l, cj, i, 2, d:d + 1], X[:, d, t0:t0 + n], ALU.mult, ALU.add,
                    [("ps", bo), ("AB", l), ("X", d, t0)], [("X", d, t0)])
            if last_of_block:
                release_unit()
                if hook is not None:
                    hook()

        ng = len(groups)
        for ii, it in enumerate(items):
            if it[0] == 0 and it[3] + LOOK < ng:
                (t0_, n_) = groups[it[3] + LOOK]
                norm_mod(l, cj, i, t0_, n_, Hv, lambda c, t0_=t0_: ("H", c, t0_))
            GU(it, ii % 2)
            if ii > 0:
                pit = items[ii - 1]
                WO(pit, (ii - 1) % 2, pit[3] == ng - 1)
        pit = items[-1]
        WO(pit, (len(items) - 1) % 2, True)

    def mixer(l, cj, groups, seqs, is_sample):
        T = sum(n for _, n in groups)
        ntiles = T // 128
        stok = "small%d" % 0

        def padcol(t):
            for si, (s0, sl) in enumerate(seqs):
                if s0 <= t < s0 + sl:
                    return t + PADW * (2 * si + 1)
            raise ValueError(t)

        dma("sp", smallt[:], small_d[l], "c_small", (), ["smallt"])
        dma("sp", dwt[:], dw_d[l], "c_dw", (), ["dwt"])
        dma("sp", sinkraw[:], sink_d[l], "c_sink", (), ["sinkraw"])
        act(esb[:], sinkraw[:], AF.Exp, ["sinkraw"], ["esb"])
        def load_cache():
            if is_sample:
                dma("pool", kz[0][0:64, TS:TS + PAST], ckT_d[l][0:64, :], "c_ck", (), [("kst", "cache")])
                dma("pool", kz[1][64:128, TS:TS + PAST], ckT_d[l][64:128, :], "c_ck1", (), [("kst", "cache")])
                cvv = cv_d[l].rearrange("(t p) f -> p t f", p=128)
                dma("pool", vaug[:, 18:22, 0, 0:64], cvv[:, :, 0:64], "c_cv0", (), [("vaug", "cache")])
                dma("pool", vaug[:, 18:22, 1, 64:128], cvv[:, :, 64:128], "c_cv1", (), [("vaug", "cache")])

        PSC = lambda c: smallt[:, c:c + 1]
        CB = lambda c: smallt[:, 2 + c:3 + c]
        CNG = lambda c: smallt[:, 4 + c:5 + c]
        QG = smallt[:, 6:7]
        KG = smallt[:, 7:8]

        sA, tokA, _ = next_unit("WIN")
        winA = unit_views("WIN", sA)
        sB, tokB, _ = next_unit("WIN")
        winB = unit_views("WIN", sB)

        def wcol(ch):
            if ch < 6:
                return winA[:, :, ch * 128:(ch + 1) * 128], tokA
            return winB[:, :, (ch - 6) * 128:(ch - 5) * 128], tokB

        HN = 264
        items = []
        for (g0, gn) in groups:
            for off in range(0, gn, 256):
                items.append((g0 + off, g0))
        all_tmp_toks = [("tmp", i) for i in range(NTMP)] + [("tmph", i, h) for i in range(NTMP) for h in range(2)]

        def tmp_barrier():
            P.add("dve", lambda e: e.memset(scr[:, :], 0.0), (), all_tmp_toks)

        tmp_barrier()
        hctr = {"n": 0, "c": 0}
        pools = {"n": [0, 1], "c": [6, 7]}
        HSLOT = [(ti, h) for ti in (2, 3, 4, 5) for h in (0, 1)]

        def qn_slot(k):
            ti, h = HSLOT[k]
            return tmps[ti][:, h * HN:h * HN + 256], ("tmph", ti, h)

        def qb_slot(k):
            ti, h = HSLOT[5 + k // 2]
            o = 2 * h * HN + (k % 2) * 272
            return tmps[ti].bitcast(BF16)[:, o:o + 256], ("tmph", ti, h)

        def half(pool):
            lst = pools[pool]
            k = hctr[pool] % (2 * len(lst))
            hctr[pool] += 1
            ti, h = lst[k // 2], k % 2
            return tmps[ti][:, h * HN:h * HN + 256], tmps[ti].bitcast(BF16)[:, 2 * h * HN:2 * h * HN + 256], ("tmph", ti, h)

        def hb(i):
            return hgrp[:, :, (i % 2) * 256:(i % 2) * 256 + 256]

        def norm1(i):
            t0 = items[i][0]
            for c in range(NCH):
                _, sqv, sqt = half("n")
                act(sqv, X[:, c, t0:t0 + 256], AF.Square, [("X", c, items[i][1])], [sqt])
                mm(ps[:, 4, 0:256], ONES_MEAN, sqv, c == 0, c == NCH - 1, [sqt, "cmat"], [("ps", 4)])

        def norm2(i):
            t0 = items[i][0]
            lnv, _, lnt = half("c")
            act(lnv, ps[:, 4, 0:256], AF.Ln, [("ps", 4)], [lnt], bias=EPS)
            act(lnv, lnv, AF.Exp, [lnt], [lnt], scale=-0.5)
            for c in range(NCH):
                tv, _, ttok = half("n")
                stt(tv, X[:, c, t0:t0 + 256], AB[:, l, cj, 1, 0, c:c + 1], lnv, ALU.mult, ALU.mult,
                    [("X", c, items[i][1]), ("AB", l), lnt], [ttok])
                act(hb(i)[:, c, :], tv, AF.Identity, [ttok, ("AB", l)], [("hgrp", i % 2, c)], bias=AB[:, l, cj, 1, 1, c:c + 1])

        ubank = [0]

        def proj(i, ch, bank=None):
            if bank is None:
                b = ubank[0] % 4
                ubank[0] += 1
            else:
                b = bank
            wv, wt = wcol(ch)
            for k in range(NCH):
                mm(ps[:, b, 0:256], wv[:, k, :], hb(i)[:, k, :], k == 0, k == NCH - 1, [wt, ("hgrp", i % 2, k)], [("ps", b)])
            return b

        def item_body(i, mid_hook):
            t0 = items[i][0]
            a = padcol(t0)
            if is_sample:
                rc_ = ropec[:, (i % 2) * 256:(i % 2) * 256 + 256]
                rs_ = ropes[:, (i % 2) * 256:(i % 2) * 256 + 256]
                vtk = [("vout", g_) for g_ in range(8)]
                dma("sp", rc_, cos_d[:, t0:t0 + 256], "c_rc%d" % (i % 2), (), [("ropec", i % 2)] + vtk)
                dma("sp", rs_, sin_d[:, t0:t0 + 256], "c_rs%d" % (i % 2), (), [("ropes", i % 2)] + vtk)
            for ch in (0, 1):
                b = proj(i, ch)
                act(upad[:, ch, a:a + 256], ps[:, b, 0:256], AF.Copy, [("ps", b)], [("upad", ch, items[i][1])])
            for cc in (0, 1):
                bgt = proj(i, 4 + cc)
                sgv, _, sgt = half("c")
                act(sgv, ps[:, bgt, 0:256], AF.Exp, [("ps", bgt)], [sgt], scale=-1.0)
                act(sgv, sgv, AF.Ln, [sgt], [sgt], bias=1.0)
                act(sgv, sgv, AF.Exp, [sgt], [sgt], scale=-1.0)
                ba = proj(i, 2 + cc)
                tt(gpad[:, cc, a:a + 256], ps[:, ba, 0:256], sgv, ALU.mult, [("ps", ba), sgt], [("gpad", cc, items[i][1])])
            mid_hook()
            st = {}
            QK = [6, 7, 8, 9, 10]
            PBANK = {6: 0, 7: 1, 8: 2, 9: 3, 10: 7}
            MSLOC = {6: (5, 0), 7: (5, 256), 8: (6, 0), 9: (4, 0), 10: (4, 256)}
            RBANK = {6: 0, 7: 1, 8: 2, 9: 3, 10: 5}

            def stA(ch):
                b = proj(i, ch)
                _, sqv, sqt = half("c")
                act(sqv, ps[:, b, 0:256], AF.Square, [("ps", b)], [sqt])
                st[ch] = dict(b=b, sqv=sqv, sqt=sqt)

            def stB1(ch):
                d = st[ch]
                mb, mc = MSLOC[ch]
                mm(ps[:, mb, mc:mc + 256], BLK64, d["sqv"], True, True, [d["sqt"], "cmat"], [("ps", mb)])

            def stB2(ch):
                d = st[ch]
                k = ch - 6
                mb, mc = MSLOC[ch]
                lnv, _, lnt = half("c")
                act(lnv, ps[:, mb, mc:mc + 256], AF.Ln, [("ps", mb)], [lnt], bias=EPS)
                act(lnv, lnv, AF.Exp, [lnt], [lnt], scale=-0.5)
                gsc = KG if ch == 10 else QG
                if ch == 10:
                    dstv, dtok = None, ("kst", items[i][1])
                else:
                    dstv, dtok = qst[:, ch - 6, t0:t0 + 256], ("qst", ch - 6, items[i][1])
                if not is_sample and ch != 10:
                    stt(dstv, ps[:, d["b"], 0:256], gsc, lnv, ALU.mult, ALU.mult, [("ps", d["b"]), "smallt", lnt], [dtok])
                    return
                qnv, qnt = qn_slot(k)
                stt(qnv, ps[:, d["b"], 0:256], gsc, lnv, ALU.mult, ALU.mult, [("ps", d["b"]), "smallt", lnt], [qnt])
                d.update(qnv=qnv, qnt=qnt, dstv=dstv, dtok=dtok)
                if not is_sample:
                    vcopy(kz[0][0:64, t0:t0 + 256], qnv[0:64, :], [qnt], [dtok])
                    vcopy(kz[1][64:128, t0:t0 + 256], qnv[64:128, :], [qnt], [dtok])
                    dma("sp", nk_d[l][:, t0:t0 + 256], qnv, "o_nk", [qnt], [])
                else:
                    qbv, qbt = qb_slot(k)
                    vcopy(qbv, qnv, [qnt], [qbt])
                    d.update(qbv=qbv, qbt=qbt)

            def stC1(ch):
                if not is_sample:
                    return
                d = st[ch]
                rb = {6: 5, 10: 6}.get(ch, d["b"])
                d["rb"] = rb
                mm(ps[:, rb, 0:256], PERM, d["qbv"], True, True, [d["qbt"], "cmat"], [("ps", rb)])

            def stC2(ch):
                if not is_sample:
                    return
                d = st[ch]
                rb = d["rb"]
                tt(d["qnv"], d["qnv"], rc_, ALU.mult, [d["qnt"], ("ropec", i % 2)], [d["qnt"]])
                t2v, _, t2t = half("c")
                tt(t2v, ps[:, rb, 0:256], rs_, ALU.mult, [("ps", rb), ("ropes", i % 2)], [t2t])
                if ch == 10:
                    tt(kz[0][0:64, t0:t0 + 256], d["qnv"][0:64, :], t2v[0:64, :], ALU.add, [d["qnt"], t2t], [d["dtok"]])
                    tt(kz[1][64:128, t0:t0 + 256], d["qnv"][64:128, :], t2v[64:128, :], ALU.add, [d["qnt"], t2t], [d["dtok"]])
                else:
                    tt(d["dstv"], d["qnv"], t2v, ALU.add, [d["qnt"], t2t], [d["dtok"]])

            B1, B2 = [6, 7, 8], [9, 10]
            for ch in B1:
                stA(ch)
            for ch in B1:
                stB1(ch)
            for ch in B1:
                stB2(ch)
            for ch in B2:
                stA(ch)
            for ch in B2:
                stB1(ch)
            for ch in B1:
                stC1(ch)
            for ch in B2:
                stB2(ch)
            for ch in B1:
                stC2(ch)
            for ch in B2:
                stC1(ch)
            V_LATE = True
            wv, wt = wcol(11)
            for tl in range(2):
                for k in range(NCH):
                    mm(ps[:, 7, tl * 128:(tl + 1) * 128], hb(i)[:, k, tl * 128:(tl + 1) * 128], wv[:, k, :], k == 0,
                       k == NCH - 1, [wt, ("hgrp", i % 2, k)], [("ps", 7)])
            for tl in range(2):
                gt = (t0 // 128) + tl
                act(vaug[:, gt, 0, 0:64], ps[:, 7, tl * 128:tl * 128 + 64], AF.Copy, [("ps", 7)], [("vaug", gt)])
                act(vaug[:, gt, 1, 64:128], ps[:, 7, tl * 128 + 64:tl * 128 + 128], AF.Copy, [("ps", 7)], [("vaug", gt)])
                if not is_sample:
                    vcopy(vout[:, gt, :], ps[:, 7, tl * 128:(tl + 1) * 128], [("ps", 7)], [("vout", gt)])
            for ch in B2:
                stC2(ch)

        norm1(0)
        norm2(0)
        init_big(is_sample, T)
        load_cache()
        for i in range(len(items)):
            if i + 1 < len(items):
                norm1(i + 1)
                item_body(i, lambda i=i: norm2(i + 1))
            else:
                item_body(i, lambda: None)
        tmp_barrier()
        release_unit()
        release_unit()
        if not is_sample:
            dma("sp", nv_d[l].rearrange("(t p) f -> p t f", p=128), vout, "o_nv",
                [("vout", gt) for gt in range(8)], [])

        sD, tokD, _ = next_unit("DIAG")
        diag = unit_views("DIAG", sD)
        sW, tokW, _ = next_unit("WOUT")
        wout, pwv, pbv = unit_views("WOUT", sW)
        for cc in range(2):
            for j in range(31):
                act(diag[:, cc, j, :], IDENT, AF.Copy, ["cmat", "dwt"], [tokD], scale=dwt[:, cc, j:j + 1])

        def wout_part(kc0, src, src_tokfn, t0, n):
            for d in range(NCH):
                bo = 4 + (d % 2)
                for kk in range(4):
                    mm(ps[:, bo, 0:n], wout[:, kc0 + kk, d * 128:(d + 1) * 128], src[:, kk, 0:n], kk == 0, kk == 3,
                       [tokW, src_tokfn(kk)], [("ps", bo)])
                stt(X[:, d, t0:t0 + n], ps[:, bo, 0:n], AB[:, l, cj, 1, 2, d:d + 1], X[:, d, t0:t0 + n], ALU.mult, ALU.add,
                    [("ps", bo), ("AB", l), ("X", d, t0)], [("X", d, t0)])

        allsegs = []
        for (t0, n) in groups:
            t = t0
            while t < t0 + n:
                for (s0, sl) in seqs:
                    if s0 <= t < s0 + sl:
                        e = min(t0 + n, s0 + sl)
                        allsegs.append(dict(st=t, sn=e - t, at_start=(t == s0), at_end=(e == s0 + sl), t0=t0, n=n,
                                            last=(e == t0 + n)))
                        t = e
                        break
        gr_all = {cc: [("gpad", cc, g0) for (g0, _) in groups] for cc in (0, 1)}
        ur_all = {ch: [("upad", ch, g0) for (g0, _) in groups] for ch in (0, 1)}

        def conv_mms(si):
            sg_ = allsegs[si]
            a, sn = padcol(sg_["st"]), sg_["sn"]
            for cc in (0, 1):
                b = 2 * (si % 2) + cc
                for j in range(31):
                    mm(ps[:, b, 0:sn], diag[:, cc, j, :], gpad[:, cc, a + j - 15:a + j - 15 + sn], j == 0, j == 30,
                       [tokD] + gr_all[cc], [("ps", b)])

        def pool_dve(si):
            sg_ = allsegs[si]
            a, sn, at_start, at_end = padcol(sg_["st"]), sg_["sn"], sg_["at_start"], sg_["at_end"]
            outs = []
            for ch in (0, 1):
                ur = ur_all[ch]
                A_, At = tmp()
                tt(A_[:, 0:sn + 14], upad[:, ch, a - 8:a + sn + 6], upad[:, ch, a - 7:a + sn + 7], ALU.add, ur, [At])
                B_, Bt = tmp()
                tt(B_[:, 0:sn + 12], A_[:, 0:sn + 12], A_[:, 2:sn + 14], ALU.add, [At], [Bt])
                if ch == 0:
                    lo_src, lo_off, lo_w = A_, 7, 2
                    hi_src, hi_off, hi_w = B_, 6, 4
                    lot, hit = At, Bt
                else:
                    C_, Ct = tmp()
                    tt(C_[:, 0:sn + 8], B_[:, 0:sn + 8], B_[:, 4:sn + 12], ALU.add, [Bt], [Ct])
                    D_, Dt = tmp()
                    tt(D_[64:128, 0:sn], C_[64:128, 0:sn], C_[64:128, 8:sn + 8], ALU.add, [Ct], [Dt])
                    lo_src, lo_off, lo_w = C_, 4, 8
                    hi_src, hi_off, hi_w = D_, 0, 16
                    lot, hit = Ct, Dt
                mean, mt = tmp()
                ts(mean[0:64, 0:sn], lo_src[0:64, lo_off:lo_off + sn], 1.0 / lo_w, ALU.mult, [lot], [mt])
                ts(mean[64:128, 0:sn], hi_src[64:128, hi_off:hi_off + sn], 1.0 / hi_w, ALU.mult, [hit], [mt])
                if at_start:
                    tt(mean[0:64, 0:8], lo_src[0:64, lo_off:lo_off + 8], edget[0:64, ch, 0, :], ALU.mult,
                       [lot, "edget"], [mt])
                    tt(mean[64:128, 0:8], hi_src[64:128, hi_off:hi_off + 8], edget[64:128, ch, 0, :], ALU.mult,
                       [hit, "edget"], [mt])
                if at_end:
                    tt(mean[0:64, sn - 8:sn], lo_src[0:64, lo_off + sn - 8:lo_off + sn], edget[0:64, ch, 1, :],
                       ALU.mult, [lot, "edget"], [mt])
                    tt(mean[64:128, sn - 8:sn], hi_src[64:128, hi_off + sn - 8:hi_off + sn], edget[64:128, ch, 1, :],
                       ALU.mult, [hit, "edget"], [mt])
                plv = A_.bitcast(BF16)[:, 0:sn]
                tt(plv, mean[:, 0:sn], upad[:, ch, a:a + sn], ALU.subtract, [mt] + ur, [At])
                outs.append((plv, At, tctr[0]))
            return outs

        def pool_mm(si, outs):
            sg_ = allsegs[si]
            sn, off = sg_["sn"], sg_["st"] - sg_["t0"]
            for ch in (0, 1):
                plv, plt, ser = outs[ch]
                assert tctr[0] - ser < NTMP - 1, "tmp ring wrapped (pool)"
                b = 6 + ch
                mm(ps[:, b, 0:sn], pbv[:, ch, :], plv, True, True, [tokW, plt], [("ps", b)])
                act(opc[:, ch, off:off + sn], ps[:, b, 0:sn], AF.Copy, [("ps", b), "smallt"], [("omix", ch)],
                    scale=PSC(ch))

        def conv_post(si):
            sg_ = allsegs[si]
            sn, off = sg_["sn"], sg_["st"] - sg_["t0"]
            ybs = []
            for cc in (0, 1):
                b = 2 * (si % 2) + cc
                yb, ybt = tmp()
                act(yb[:, 0:sn], ps[:, b, 0:sn], AF.Identity, [("ps", b), "smallt"], [ybt], bias=CB(cc))
                sq, sqt = tmp()
                sqv = sq.bitcast(BF16)[:, 0:sn]
                act(sqv, yb[:, 0:sn], AF.Square, [ybt], [sqt])
                mm(ps[:, 6, 0:sn], ONES256, sqv, cc == 0, cc == 1, [sqt, "cmat"], [("ps", 6)])
                ybs.append((yb, ybt))
            ln, lnt = tmp()
            act(ln[:, 0:sn], ps[:, 6, 0:sn], AF.Ln, [("ps", 6)], [lnt], bias=EPS)
            act(ln[:, 0:sn], ln[:, 0:sn], AF.Exp, [lnt], [lnt], scale=-0.5)
            zs = []
            for cc in (0, 1):
                yb, ybt = ybs[cc]
                stt(yb[:, 0:sn], yb[:, 0:sn], CNG(cc), ln[:, 0:sn], ALU.mult, ALU.mult, [ybt, "smallt", lnt], [ybt])
                zb, zbt = tmp()
                zbv = zb.bitcast(BF16)[:, 0:sn]
                act(zbv, yb[:, 0:sn], AF.Silu, [ybt], [zbt])
                zs.append((zbv, zbt))
            for co in (0, 1):
                b = 6 + co
                for ci in (0, 1):
                    mm(ps[:, b, 0:sn], pwv[:, ci, co * 128:(co + 1) * 128], zs[ci][0], ci == 0, ci == 1,
                       [tokW, zs[ci][1]], [("ps", b)])
                act(opc[:, 2 + co, off:off + sn], ps[:, b, 0:sn], AF.Copy, [("ps", b)], [("omix", 2 + co)])

        conv_mms(0)
        for si in range(len(allsegs)):
            outs = pool_dve(si)
            if si + 1 < len(allsegs):
                conv_mms(si + 1)
            pool_mm(si, outs)
            conv_post(si)
            if allsegs[si]["last"]:
                wout_part(0, opc, lambda kk: ("omix", kk), allsegs[si]["t0"], allsegs[si]["n"])

        release_unit()
        sbank = [0]
        kall = [("kst", g0) for (g0, _) in groups] + ([("kst", "cache")] if is_sample else [])
        allsteps = []
        ginfo = []
        for (t0, n) in groups:
            qall = [("qst", c, t0) for c in range(4)]
            first = len(allsteps)
            ntile0 = None
            for tl in range(n // 128):
                gt = t0 // 128 + tl
                q0 = t0 + tl * 128
                chunks = []
                if is_sample:
                    if gt > 0:
                        chunks.append((q0 - 128, gt - 1, 0))
                    chunks.append((q0, gt, None))
                    if gt < ntiles - 1:
                        chunks.append((q0 + 128, gt + 1, 1))
                    for cti in range(4):
                        chunks.append((TS + cti * 128, 18 + cti, None))
                else:
                    for (s0, sl) in seqs:
                        if s0 <= q0 < s0 + sl:
                            for kt in range(sl // 128):
                                chunks.append((s0 + kt * 128, (s0 // 128) + kt, None))
                for kvh in range(2):
                    for ci, ch in enumerate(chunks):
                        allsteps.append((tl, gt, q0, ci, len(chunks), ch, kvh, t0, qall))
                if ntile0 is None:
                    ntile0 = len(allsteps) - first
            ginfo.append((first, ntile0, t0, n, len(allsteps) - 1))

        def qk(step):
            tl, gt, q0, ci, nci, (kc0, vt, mk), kvh, t0, qall = step
            b = sbank[0] % 4
            sbank[0] += 1
            for hh in range(4):
                mm(ps[:, b, hh * 128:(hh + 1) * 128], kz[kvh][:, kc0:kc0 + 128],
                   qst[:, hh, q0:q0 + 128], True, True, kall + qall, [("ps", b)])
            pt, ptt = tmp()
            ptv = pt.bitcast(BF16)[:, 0:512]
            act(ptv, ps[:, b, :], AF.Exp, [("ps", b)], [ptt], scale=0.125)
            if mk is not None:
                pv4 = ptv.rearrange("p (h q) -> p h q", h=4)
                tt(pv4, pv4, masks[:, mk, :].unsqueeze(1).broadcast_to([128, 4, 128]), ALU.mult, [ptt, "masks"], [ptt])
            return (ptv, ptt, tctr[0])

        def pv(step, pts):
            tl, gt, q0, ci, nci, (kc0, vt, mk), kvh, t0, qall = step
            pb = 4 + 2 * (gt % 2)
            assert tctr[0] - pts[2] < NTMP, "tmp ring wrapped"
            vtok = ("vaug", vt) if vt < 18 else ("vaug", "cache")
            mm(ps[:, pb + kvh, :], vaug[:, vt, kvh, :], pts[0], ci == 0, ci == nci - 1,
               [vtok, pts[1]], [("ps", pb + kvh)])
            if ci == nci - 1:
                dlo, nlo = (64, 0) if kvh == 0 else (0, 64)
                rc2, rc2t = tmp()
                if kvh == 0:
                    ln, lnt = tmp()
                    for hh in range(4):
                        act(ln[dlo:dlo + 64, hh * 128:(hh + 1) * 128], ps[dlo:dlo + 64, pb + kvh, hh * 128:(hh + 1) * 128],
                            AF.Ln, [("ps", pb + kvh), "esb"], [lnt], bias=esb[dlo:dlo + 64, 4 * kvh + hh:4 * kvh + hh + 1])
                    act(ln[dlo:dlo + 64, 0:512], ln[dlo:dlo + 64, 0:512], AF.Exp, [lnt], [lnt], scale=-1.0)
                    vcopy(rc2[nlo:nlo + 64, 0:512], ln[dlo:dlo + 64, 0:512], [lnt], [rc2t])
                else:
                    for hh in range(4):
                        ts(rc2[nlo:nlo + 64, hh * 128:(hh + 1) * 128], ps[dlo:dlo + 64, pb + kvh, hh * 128:(hh + 1) * 128],
                           esb[dlo:dlo + 64, 4 * kvh + hh:4 * kvh + hh + 1], ALU.add, [("ps", pb + kvh), "esb"], [rc2t])
                    P.add("dve", lambda e, o_=rc2[nlo:nlo + 64, 0:512]: e.reciprocal(out=o_, in_=o_), [rc2t], [rc2t])
                tt(oattn[nlo:nlo + 64, :, tl * 128:(tl + 1) * 128],
                   ps[nlo:nlo + 64, pb + kvh, :].rearrange("p (h q) -> p h q", h=4),
                   rc2[nlo:nlo + 64, 0:512].rearrange("p (h q) -> p h q", h=4), ALU.mult,
                   [("ps", pb + kvh), rc2t], [("omix", kk) for kk in range(4)])


        def wout_attn(t0, n):
            for d in range(NCH):
                bo = 6 + (d % 2)
                for kk in range(4):
                    mm(ps[:, bo, 0:n], wout[:, 4 + kk, d * 128:(d + 1) * 128], oattn[:, kk, 0:n], kk == 0, kk == 3,
                       [tokW, ("omix", kk)], [("ps", bo)])
                stt(X[:, d, t0:t0 + n], ps[:, bo, 0:n], AB[:, l, cj, 1, 2, d:d + 1], X[:, d, t0:t0 + n], ALU.mult, ALU.add,
                    [("ps", bo), ("AB", l), ("X", d, t0)], [("X", d, t0)])

        DEPTH = 3
        pend = []
        due = []
        for si, step in enumerate(allsteps):
            while due and due[0][0] <= si:
                _, t0_, n_ = due.pop(0)
                wout_attn(t0_, n_)
            pend.append((si, step, qk(step)))
            if len(pend) > DEPTH:
                pi, pstep, ppts = pend.pop(0)
                pv(pstep, ppts)
                for gi, (first, ntile0, t0_, n_, last) in enumerate(ginfo):
                    if pi == last:
                        if gi + 1 < len(ginfo):
                            nfirst, nnt0 = ginfo[gi + 1][0], ginfo[gi + 1][1]
                            due.append((nfirst + min(DEPTH + 4, nnt0 // 2 - 1 + DEPTH), t0_, n_))
                        else:
                            due.append((10 ** 9, t0_, n_))
        while pend:
            pi, pstep, ppts = pend.pop(0)
            pv(pstep, ppts)
            for gi, (first, ntile0, t0_, n_, last) in enumerate(ginfo):
                if pi == last:
                    due.append((10 ** 9, t0_, n_))
        for (_, t0_, n_) in due:
            wout_attn(t0_, n_)
        release_unit()

    def mkgroups(T):
        g = []
        t = 0
        while t < T:
            n = min(512, T - t)
            g.append((t, n))
            t += n
        return g

    def big_switch():
        P.add("dve", lambda e: e.memset(scr[:, :], 0.0), (), ["BIG"])

    def init_big(is_sample, T):
        big_switch()
        seqs_ = [(0, TS)] if is_sample else [(s_ * 256, 256) for s_ in range(4)]
        ut = [("upad", ch, g0) for ch in (0, 1) for (g0, _) in mkgroups(T)]
        gt_ = [("gpad", ch, g0) for ch in (0, 1) for (g0, _) in mkgroups(T)]
        for si, (s0, sl) in enumerate(seqs_):
            a = s0 + PADW * (2 * si + 1)
            for (c0, c1) in ((a - PADW, a), (a + sl, a + sl + PADW)):
                vmemset(upad[:, :, c0:c1], 0.0, ut)
                vmemset(gpad[:, :, c0:c1], 0.0, gt_)
        vt = [("vaug", t) for t in range(18)] + [("vaug", "cache")]
        vmemset(vaug[:, :, 0, 64:128], 1.0, vt)
        vmemset(vaug[:, :, 1, 0:64], 1.0, vt)
        ktoks = [("kst", g0) for (g0, _) in mkgroups(T)] + [("kst", "cache")]
        vmemset(kz[0][64:128, :], 0.0, ktoks)
        vmemset(kz[1][0:64, :], 0.0, ktoks)

    prenormed = {}
    for it in plan:
        if it[0] == "mod":
            mod_enqueue(it[1])
            if it[1] == 0:
                mod_pump(12)
                _load_ready()
                mod_cap[0] = 10 ** 9
            continue
        phase = it[1]
        is_sample = phase == 1
        T = TS if is_sample else TP
        xin = xs_d if is_sample else xp_d
        yout = ys_d if is_sample else yp_d
        groups = mkgroups(T)
        seqs = [(0, TS)] if is_sample else [(s * 256, 256) for s in range(4)]
        if it[0] == "load":
            if is_sample:
                mod_ensure(1, 2)
            for c in range(NCH):
                dma("sp", X[:, c, 0:T], xin[c], "x_in%d" % c, (), [("X", c, g0) for (g0, _) in groups])
        elif it[0] == "store":
            for c in range(NCH):
                dma("sp", yout[c], X[:, c, 0:T], "y_out%d" % c, [("X", c, g0) for (g0, _) in groups], [])
        elif it[0] == "ffn1":
            mod_ensure(it[2], 0)
            ffn(it[2], 0, phase, groups, hook=lambda: mod_pump(5))
        elif it[0] == "ffn2":
            mod_ensure(it[2], 2)
            ffn(it[2], 1, phase, groups, hook=lambda: mod_pump(5), pre0=prenormed.get((phase, it[2]), False))
        elif it[0] == "mixer":
            mod_ensure(it[2], 1)
            mixer(it[2], phase, groups, seqs, is_sample)
            if ("ffn2", phase, it[2]) in plan:
                mod_ensure(it[2], 2)
                ffn_prenorm(it[2], phase, groups)
                prenormed[(phase, it[2])] = True
            big_switch()

    final_keys = ["y_out%d" % c for c in range(NCH)] + ["o_nk", "o_nv"]
    P.emit(nc, es, final_keys)
    es.close()
    nc._prog_stats = P.stats
    return nc


_NC_CACHE = {}


def _head_perm():
    cols = []
    for j in range(4):
        cols += list(range(j * 64, j * 64 + 64))
        cols += list(range((4 + j) * 64, (4 + j) * 64 + 64))
    return np.array(cols)


def _consts():
    cm = np.zeros((128, 5, 128), np.float32)
    cm[:, 0, :] = 1.0 / 1024
    for b in range(2):
        cm[64 * b:64 * b + 64, 1, 64 * b:64 * b + 64] = 1.0 / 64
    cm[:, 2, :] = 1.0 / 256
    cm[:, 3, :] = np.eye(128, dtype=np.float32)
    for m in range(128):
        d = m % 32
        partner = m + 16 if d < 16 else m - 16
        cm[partner, 4, m] = 1.0
    k = np.arange(128)[:, None]
    q = np.arange(128)[None, :]
    mprev = (k >= q).astype(np.float32)
    mnext = (k <= q).astype(np.float32)
    masks = np.stack([mprev, mnext], axis=1)
    sinkl = np.zeros((1, 2, 128), np.float32)
    sinkl[0, 0, 64:128] = 1.0
    sinkl[0, 1, 0:64] = 1.0
    edge = np.zeros((128, 2, 2, 8), np.float32)
    wins = {(0, 0): 2, (0, 1): 4, (1, 0): 8, (1, 1): 16}
    for (ch, half), w in wins.items():
        for i in range(8):
            cs = (i + w // 2) - max(i - w // 2, 0)
            r = 8 - i
            ce = min(w // 2, r) + w // 2
            edge[64 * half:64 * half + 64, ch, 0, i] = 1.0 / cs
            edge[64 * half:64 * half + 64, ch, 1, i] = 1.0 / ce
    return cm, masks, sinkl, edge


def _rope_tables(pos0):
    pos = pos0 + np.arange(TS)
    row = (pos // 64).astype(np.float64)
    col = (pos % 64).astype(np.float64)
    half = 32
    inv = 10000.0 ** (-np.arange(0, half, 2, dtype=np.float64) / half)
    cos = np.zeros((128, TS), np.float32)
    sin = np.zeros((128, TS), np.float32)
    for p in range(128):
        d = p % 64
        posv = row if d < 32 else col
        dd = d % 32
        i = dd % 16
        ang = posv * inv[i]
        cos[p] = np.cos(ang)
        sin[p] = np.sin(ang) * (-1.0 if dd < 16 else 1.0)
    return cos, sin


def kernel(x_prompt, x_sample, cache_k, cache_v, c, c_ctx, mod_w, mod_b, norm_g,
           ffn1_wi, ffn1_wo, ffn2_wi, ffn2_wo, w_in, w_out, pool_w, pool_scale,
           conv_dw, conv_b, conv_norm_g, conv_pw, q_norm_g, k_norm_g, sink, _only_core=None, _debug_stop=None):
    f = lambda a: np.ascontiguousarray(np.asarray(a, dtype=np.float32))
    x_prompt, x_sample, cache_k, cache_v = f(x_prompt), f(x_sample), f(cache_k), f(cache_v)
    if "nc" not in _NC_CACHE:
        _NC_CACHE["nc"] = build_program()
    nc = _NC_CACHE["nc"]
    hp = _head_perm()
    w_in_p = f(w_in).copy()
    w_in_p[:, :, 768:1280] = f(w_in)[:, :, 768 + hp]
    w_out_p = f(w_out).copy()
    w_out_p[:, 512:1024, :] = f(w_out)[:, 512 + hp, :]
    modb_l = f(np.asarray(mod_b).reshape(2, 72, 128).transpose(0, 2, 1))
    ng_l = f(np.asarray(norm_g).reshape(2, 3, NCH, 128).transpose(0, 3, 1, 2))
    small = np.zeros((2, 128, 16), np.float32)
    small[:, :, 0:2] = np.asarray(pool_scale).reshape(2, 2, 128).transpose(0, 2, 1)
    small[:, :, 2:4] = np.asarray(conv_b).reshape(2, 2, 128).transpose(0, 2, 1)
    small[:, :, 4:6] = np.asarray(conv_norm_g).reshape(2, 2, 128).transpose(0, 2, 1)
    small[:, :, 6] = np.tile(np.asarray(q_norm_g), (1, 2))
    small[:, :, 7] = np.tile(np.asarray(k_norm_g), (1, 2))
    dw_l = f(np.asarray(conv_dw).reshape(2, 31, 2, 128).transpose(0, 3, 2, 1))
    sk = np.asarray(sink, dtype=np.float32)
    sink_b = f(np.broadcast_to(sk[:, None, :], (2, 128, 8)))
    cm, masks, sinkl, edge = _consts()
    shared = dict(mod_w=f(mod_w), mod_b=modb_l, norm_g=ng_l, ffn1_wi=f(ffn1_wi), ffn2_wi=f(ffn2_wi),
                  ffn1_wo=f(ffn1_wo), ffn2_wo=f(ffn2_wo), w_in=w_in_p, w_out=w_out_p, pool_w=f(pool_w),
                  small=small, conv_dw=dw_l, conv_pw=f(conv_pw), sink_b=sink_b, cmat=cm, masks=masks,
                  pool_edge=edge)
    in_maps = []
    starts = []
    for core in (range(8) if _only_core is None else [_only_core]):
        b, hf = core // 2, core % 2
        s0 = 0 if hf == 0 else 4096 - TS
        starts.append(s0)
        xp = x_prompt[4 * core:4 * core + 4].reshape(TP, D).T.reshape(NCH, 128, TP)
        xs = x_sample[b, s0:s0 + TS].T.reshape(NCH, 128, TS)
        cond = np.stack([np.asarray(c_ctx, np.float32), np.asarray(c, np.float32)[b]], axis=1)
        cond = cond.reshape(NCH, 128, 2).transpose(1, 0, 2)
        ckT = cache_k[b].reshape(2, PAST, 128).transpose(0, 2, 1)
        cv = cache_v[b].reshape(2, PAST, 128)
        cos, sin = _rope_tables(s0)
        m = dict(shared)
        m.update(xp=f(xp), xs=f(xs), cond=f(cond), cache_kT=f(ckT), cache_v=f(cv), rope_cos=cos, rope_sin=sin)
        in_maps.append(m)
    if _only_core is not None:
        nc = build_program(_debug_stop)
        res = run_bass_kernel_spmd(nc, in_maps, core_ids=[0])
        print("EXEC_NS", res.exec_time_ns, nc._prog_stats)
        return res.results[0]
    res = run_bass_kernel_spmd(nc, in_maps, core_ids=list(range(8)))
    y_prompt = np.zeros((32, 256, D), np.float32)
    y_sample = np.zeros((4, 4096, D), np.float32)
    nk = np.zeros((32, 2, 256, 2, 64), np.float32)
    nv = np.zeros((32, 2, 256, 2, 64), np.float32)
    for core in range(8):
        r = res.results[core]
        b, hf = core // 2, core % 2
        yp = np.asarray(r["yp"]).reshape(D, TP).T.reshape(4, 256, D)
        y_prompt[4 * core:4 * core + 4] = yp
        ys = np.asarray(r["ys"]).reshape(D, TS).T
        if hf == 0:
            y_sample[b, 0:2048] = ys[0:2048]
        else:
            y_sample[b, 2048:4096] = ys[TS - 2048:TS]
        k_ = np.asarray(r["nk"]).transpose(0, 2, 1).reshape(2, 4, 256, 2, 64)
        v_ = np.asarray(r["nv"]).reshape(2, 4, 256, 2, 64)
        nk[4 * core:4 * core + 4] = k_.transpose(1, 0, 2, 3, 4)
        nv[4 * core:4 * core + 4] = v_.transpose(1, 0, 2, 3, 4)
    return (y_prompt, y_sample, nk, nv)
```
